# Optimizing a Trainium2 kernel written in Bass

```python
import math
import jax, jax.numpy as jnp
from jax import lax
import numpy as np

D_MODEL = 1024
BATCH = 4
SEQ = 4096
DEPTH = 2
DEC_BATCH = 128
DEC_SEQ = 8
PAST_LEN = 8192
PAGE_SIZE = 128

N_HEADS = 16
N_KV_HEADS = 2
HEAD_DIM = 64
GROUP = N_HEADS // N_KV_HEADS
ATTN_WIDTH = N_HEADS * HEAD_DIM
KV_WIDTH = N_KV_HEADS * HEAD_DIM
CONV_DIM = D_MODEL
CONV_GROUPS = 16
CONV_WIDTH = 3
WINDOW = 128
BLOCK = 128
N_BUCKETS = 32
MAX_DISTANCE = 128
D_FF = 4 * D_MODEL
N_BRANCH = 2
N_MOD = 6
RMS_EPS = 1e-6
NEG_INF = -1e30
PROJ_COLS = 3 * CONV_DIM + ATTN_WIDTH + 2 * KV_WIDTH + N_BRANCH * D_MODEL
SPLITS = (CONV_DIM, 2 * CONV_DIM, 3 * CONV_DIM, 3 * CONV_DIM + ATTN_WIDTH,
          3 * CONV_DIM + ATTN_WIDTH + KV_WIDTH, 3 * CONV_DIM + ATTN_WIDTH + 2 * KV_WIDTH)

kernel_name = "hybrid_conv_swa_sink_decoder_step"


def rms_norm(x, g):
    xf = x.astype(jnp.float32)
    y = xf * lax.rsqrt(jnp.mean(xf * xf, axis=-1, keepdims=True) + RMS_EPS)
    return (y * g.astype(jnp.float32)).astype(x.dtype)


def rel_bucket(dist):
    n = jnp.maximum(dist, 0)
    max_exact = N_BUCKETS // 2
    nf = jnp.maximum(n, 1).astype(jnp.float32)
    large = max_exact + (jnp.log(nf / max_exact) / math.log(MAX_DISTANCE / max_exact)
                         * (N_BUCKETS - max_exact)).astype(jnp.int32)
    large = jnp.minimum(large, N_BUCKETS - 1)
    return jnp.where(n < max_exact, n, large)


def window_attention(q, k, v, dist, key_ok, rel_table, sinks):
    n, lq = q.shape[:2]
    lk = k.shape[1]
    qg = q.reshape(n, lq, N_KV_HEADS, GROUP, HEAD_DIM)
    s = jnp.einsum('nqkgd,nskd->nkgqs', qg, k, preferred_element_type=jnp.float32) * (HEAD_DIM ** -0.5)
    bias = rel_table[rel_bucket(dist)].astype(jnp.float32)
    bias = jnp.transpose(bias, (2, 0, 1)).reshape(N_KV_HEADS, GROUP, lq, lk)
    valid = (dist >= 0) & (dist < WINDOW)
    mask = valid[None] & key_ok[:, None, :]
    s = jnp.where(mask[:, None, None], s + bias, NEG_INF)
    sink = jnp.broadcast_to(sinks.astype(jnp.float32).reshape(1, N_KV_HEADS, GROUP, 1, 1),
                            s.shape[:-1] + (1,))
    p = jax.nn.softmax(jnp.concatenate([s, sink], axis=-1), axis=-1)[..., :-1]
    o = jnp.einsum('nkgqs,nskd->nqkgd', p.astype(v.dtype), v)
    return o.reshape(n, lq, ATTN_WIDTH)


def prompt_attention(q, k, v, rel_table, sinks):
    b, s = q.shape[:2]
    nb = s // BLOCK
    qb = q.reshape(b * nb, BLOCK, N_HEADS, HEAD_DIM)

    def band(t):
        tb = t.reshape(b, nb, BLOCK, N_KV_HEADS, HEAD_DIM)
        prev = jnp.pad(tb, ((0, 0), (1, 0), (0, 0), (0, 0), (0, 0)))[:, :-1]
        return jnp.concatenate([prev, tb], axis=2).reshape(b * nb, 2 * BLOCK, N_KV_HEADS, HEAD_DIM)

    dist = (jnp.arange(BLOCK)[:, None] + BLOCK) - jnp.arange(2 * BLOCK)[None, :]
    key_ok = (jnp.arange(nb)[:, None] > 0) | (jnp.arange(2 * BLOCK)[None, :] >= BLOCK)
    key_ok = jnp.tile(key_ok, (b, 1))
    o = window_attention(qb, band(k), band(v), dist, key_ok, rel_table, sinks)
    w = min(WINDOW, s)
    return o.reshape(b, s, ATTN_WIDTH), k[:, -w:], v[:, -w:]


def sample_attention(q, k, v, k_buf, v_buf, rel_table, sinks):
    buf = k_buf.shape[1]
    t = q.shape[1]
    kk = jnp.concatenate([k_buf, k.astype(k_buf.dtype)], axis=1)
    vv = jnp.concatenate([v_buf, v.astype(v_buf.dtype)], axis=1)
    q_pos = buf + jnp.arange(t)
    k_pos = jnp.arange(buf + t)
    dist = q_pos[:, None] - k_pos[None, :]
    key_ok = jnp.ones((1, buf + t), dtype=bool)
    o = window_attention(q, kk, vv, dist, key_ok, rel_table, sinks)
    return o, kk[:, -buf:], vv[:, -buf:]


def short_conv(u, prefix, conv_w):
    ext = jnp.concatenate([prefix.astype(u.dtype), u], axis=1)
    L = u.shape[1]
    y = conv_w[0] * ext[:, 0:L]
    for j in range(1, CONV_WIDTH):
        y = y + conv_w[j] * ext[:, j:j + L]
    return y, ext[:, -(CONV_WIDTH - 1):]


def trunk_layer(x, c, conv_prefix, attn_fn, w_ada, b_ada, g_pre1, w_in, conv_w, w_br_conv,
                w_br_attn, w_o, sinks, g_post1, g_pre2, w_ff1, w_ff2, g_post2, rel_table):
    n, L = x.shape[:2]
    mod = jnp.einsum('bd,de->be', jax.nn.silu(c), w_ada) + b_ada
    sh1, sc1, ga1, sh2, sc2, ga2 = [m[:, None, :] for m in jnp.split(mod, N_MOD, axis=-1)]
    h = rms_norm(x, g_pre1) * (1 + sc1) + sh1
    proj = jnp.einsum('bld,de->ble', h, w_in)
    b_g, c_g, xc, q, k, v, gates = jnp.split(proj, SPLITS, axis=-1)
    conv_out, conv_tail = short_conv(c_g * xc, conv_prefix, conv_w)
    y_conv = jnp.einsum('blc,cd->bld', b_g * conv_out, w_br_conv)
    attn_out, k_tail, v_tail = attn_fn(q.reshape(n, L, N_HEADS, HEAD_DIM),
                                       k.reshape(n, L, N_KV_HEADS, HEAD_DIM),
                                       v.reshape(n, L, N_KV_HEADS, HEAD_DIM), rel_table, sinks)
    y_attn = jnp.einsum('bla,ad->bld', attn_out, w_br_attn)
    g_conv, g_attn = jnp.split(jax.nn.sigmoid(gates), N_BRANCH, axis=-1)
    mixed = jnp.einsum('bld,de->ble', g_conv * y_conv + g_attn * y_attn, w_o)
    x = x + ga1 * rms_norm(mixed, g_post1)
    h2 = rms_norm(x, g_pre2) * (1 + sc2) + sh2
    ff = jnp.einsum('blf,fd->bld', jnp.square(jax.nn.relu(jnp.einsum('bld,df->blf', h2, w_ff1))), w_ff2)
    x = x + ga2 * rms_norm(ff, g_post2)
    return x, conv_tail, k_tail, v_tail


def setup_inputs(seed: int = 0) -> dict:
    key = jax.random.key(seed)
    ks = jax.random.split(key, 26)

    def nrm(k, shape, scale):
        return jax.random.normal(k, shape, jnp.float32) * scale

    buf = min(WINDOW, PAST_LEN)
    return {
        "x_prompt": nrm(ks[0], (BATCH, SEQ, D_MODEL), 1.0),
        "x_sample": nrm(ks[1], (DEC_BATCH, DEC_SEQ, D_MODEL), 1.0),
        "c_prompt": nrm(ks[2], (BATCH, D_MODEL), 1.0),
        "c_sample": nrm(ks[3], (DEC_BATCH, D_MODEL), 1.0),
        "state_conv": nrm(ks[4], (DEPTH, DEC_BATCH, CONV_WIDTH - 1, CONV_DIM), 1.0),
        "cache_k": nrm(ks[5], (DEPTH, DEC_BATCH, buf, N_KV_HEADS, HEAD_DIM), 1.0),
        "cache_v": nrm(ks[6], (DEPTH, DEC_BATCH, buf, N_KV_HEADS, HEAD_DIM), 1.0),
        "w_ada": nrm(ks[7], (DEPTH, D_MODEL, N_MOD * D_MODEL), 0.3 * D_MODEL ** -0.5),
        "b_ada": nrm(ks[8], (DEPTH, N_MOD * D_MODEL), 0.02),
        "g_pre1": 1.0 + nrm(ks[9], (DEPTH, D_MODEL), 0.05),
        "w_in": nrm(ks[10], (DEPTH, D_MODEL, PROJ_COLS), D_MODEL ** -0.5),
        "conv_w": nrm(ks[11], (DEPTH, CONV_WIDTH, CONV_DIM), CONV_WIDTH ** -0.5),
        "w_br_conv": nrm(ks[12], (DEPTH, CONV_DIM, D_MODEL), CONV_DIM ** -0.5),
        "w_br_attn": nrm(ks[13], (DEPTH, ATTN_WIDTH, D_MODEL), ATTN_WIDTH ** -0.5),
        "w_o": nrm(ks[14], (DEPTH, D_MODEL, D_MODEL), D_MODEL ** -0.5),
        "sinks": nrm(ks[15], (DEPTH, N_HEADS), 0.5),
        "g_post1": 1.0 + nrm(ks[16], (DEPTH, D_MODEL), 0.05),
        "g_pre2": 1.0 + nrm(ks[17], (DEPTH, D_MODEL), 0.05),
        "w_ff1": nrm(ks[18], (DEPTH, D_MODEL, D_FF), D_MODEL ** -0.5),
        "w_ff2": nrm(ks[19], (DEPTH, D_FF, D_MODEL), D_FF ** -0.5),
        "g_post2": 1.0 + nrm(ks[20], (DEPTH, D_MODEL), 0.05),
        "rel_table": nrm(ks[21], (N_BUCKETS, N_HEADS), 0.5),
    }


def reference(x_prompt, x_sample, c_prompt, c_sample, state_conv, cache_k, cache_v,
              w_ada, b_ada, g_pre1, w_in, conv_w, w_br_conv, w_br_attn, w_o, sinks,
              g_post1, g_pre2, w_ff1, w_ff2, g_post2, rel_table):
    xp, xs = x_prompt, x_sample
    conv_p, k_p, v_p, conv_s, k_s, v_s = [], [], [], [], [], []
    zero_prefix = jnp.zeros((xp.shape[0], CONV_WIDTH - 1, CONV_DIM), xp.dtype)
    for l in range(DEPTH):
        weights = (w_ada[l], b_ada[l], g_pre1[l], w_in[l], conv_w[l], w_br_conv[l], w_br_attn[l],
                   w_o[l], sinks[l], g_post1[l], g_pre2[l], w_ff1[l], w_ff2[l], g_post2[l], rel_table)
        xp, ct, kt, vt = trunk_layer(xp, c_prompt, zero_prefix, prompt_attention, *weights)
        conv_p.append(ct); k_p.append(kt); v_p.append(vt)

        def samp_attn(q, k, v, tbl, snk, kb=cache_k[l], vb=cache_v[l]):
            return sample_attention(q, k, v, kb, vb, tbl, snk)

        xs, ct, kt, vt = trunk_layer(xs, c_sample, state_conv[l], samp_attn, *weights)
        conv_s.append(ct); k_s.append(kt); v_s.append(vt)
    conv_prompt = jnp.stack(conv_p)
    k_prompt = jnp.stack(k_p)
    v_prompt = jnp.stack(v_p)
    conv_sample = jnp.stack(conv_s)
    k_sample = jnp.stack(k_s)
    v_sample = jnp.stack(v_s)
    return (xp, xs, conv_prompt, k_prompt, v_prompt, conv_sample, k_sample, v_sample)
```

```python
import math
from contextlib import ExitStack

import numpy as np
import concourse.bass as bass
import concourse.mybir as mybir
from concourse.bass_utils import run_bass_kernel_spmd

F32 = mybir.dt.float32
BF16 = mybir.dt.bfloat16
AF = mybir.ActivationFunctionType
ALU = mybir.AluOpType

NCORES = 8
D = 1024
NCH = 8
DEPTH = 2
NPB = 18
NSEQ = 16
PROJ = 6400
DFF = 4096
EPS = 1e-6
NWSLOT = 9
WCOLS = 256
ENGS = ("pe", "act", "dve", "pool", "sp")


class Buf:
    __slots__ = ("name", "last_w", "readers", "excl")

    def __init__(self, name, excl=False):
        self.name = name
        self.last_w = None
        self.readers = []
        self.excl = excl


class Op:
    __slots__ = ("eng", "fn", "deps", "is_dma", "dsem", "needed", "inc_val")

    def __init__(self, eng, fn, deps, is_dma, dsem):
        self.eng = eng
        self.fn = fn
        self.deps = deps
        self.is_dma = is_dma
        self.dsem = dsem
        self.needed = False
        self.inc_val = None


class Sched:
    def __init__(self):
        self.ops = []
        self.n_dsem = 0

    def new_dsem(self):
        self.n_dsem += 1
        return self.n_dsem - 1

    def op(self, eng, fn, reads=(), writes=(), dma=False, dsem=None):
        ex = [b for b in reads if b.excl]
        if ex:
            writes = list(writes) + [b for b in ex if b not in writes]
            reads = [b for b in reads if not b.excl]
        deps = set()
        for b in reads:
            if b.last_w is not None:
                deps.add(b.last_w)
        for b in writes:
            if b.last_w is not None:
                deps.add(b.last_w)
            deps.update(b.readers)
        oid = len(self.ops)
        if eng == "pe":
            deps = {d for d in deps if self.ops[d].eng != "pe" or self.ops[d].is_dma}
        self.ops.append(Op(eng, fn, deps, dma, dsem))
        for b in writes:
            b.last_w = oid
            b.readers = []
        for b in reads:
            b.readers.append(oid)
        return oid

    def emit(self, nc, final_wait_eng="sp"):
        ops = self.ops
        for o in ops:
            latest = {}
            keep = set()
            for d in o.deps:
                p = ops[d]
                if p.is_dma:
                    keep.add(d)
                elif d > latest.get(p.eng, -1):
                    latest[p.eng] = d
            keep.update(latest.values())
            o.deps = keep
            for d in keep:
                ops[d].needed = True
        eng_cnt = {e: 0 for e in ENGS}
        dsem_cnt = {}
        for o in ops:
            if o.is_dma:
                dsem_cnt[o.dsem] = dsem_cnt.get(o.dsem, 0) + 16
                o.inc_val = dsem_cnt[o.dsem]
            elif o.needed:
                eng_cnt[o.eng] += 1
                o.inc_val = eng_cnt[o.eng]
        streams = {e: [] for e in ENGS}
        waited_e = {e: {f: 0 for f in ENGS} for e in ENGS}
        waited_d = {e: {} for e in ENGS}
        for o in ops:
            waits = []
            need_e = {}
            need_d = {}
            for d in o.deps:
                p = ops[d]
                if p.is_dma:
                    need_d[p.dsem] = max(need_d.get(p.dsem, 0), p.inc_val)
                else:
                    need_e[p.eng] = max(need_e.get(p.eng, 0), p.inc_val)
            for f, v in need_e.items():
                if v > waited_e[o.eng][f]:
                    waited_e[o.eng][f] = v
                    waits.append(("e", f, v))
            for s, v in need_d.items():
                if v > waited_d[o.eng].get(s, 0):
                    waited_d[o.eng][s] = v
                    waits.append(("d", s, v))
            streams[o.eng].append((waits, o))
        final_waits = []
        for s, v in dsem_cnt.items():
            if v > waited_d[final_wait_eng].get(s, 0):
                final_waits.append(("d", s, v))
        for f in ENGS:
            if eng_cnt[f] > waited_e[final_wait_eng][f]:
                final_waits.append(("e", f, eng_cnt[f]))

        with ExitStack() as es:
            esem = {e: es.enter_context(nc.semaphore("s_" + e)) for e in ENGS}
            dsems = [es.enter_context(nc.semaphore("d%d" % i)) for i in range(self.n_dsem)]
            block = es.enter_context(nc.Block())

            def run_stream(engname, eng):
                for waits, o in streams[engname]:
                    for kind, key, v in waits:
                        eng.wait_ge(esem[key] if kind == "e" else dsems[key], v)
                    ins = o.fn(eng)
                    if o.is_dma:
                        ins.then_inc(dsems[o.dsem], 16)
                    elif o.needed:
                        ins.then_inc(esem[o.eng], 1)
                if engname == final_wait_eng:
                    for kind, key, v in final_waits:
                        eng.wait_ge(esem[key] if kind == "e" else dsems[key], v)

            @block.tensor
            def _(e):
                run_stream("pe", e)

            @block.scalar
            def _(e):
                run_stream("act", e)

            @block.vector
            def _(e):
                run_stream("dve", e)

            @block.gpsimd
            def _(e):
                run_stream("pool", e)

            @block.sync
            def _(e):
                run_stream("sp", e)
        return eng_cnt


def _rel_bucket_np(dist):
    n = np.maximum(dist, 0)
    max_exact = 16
    nf = np.maximum(n, 1).astype(np.float32)
    large = max_exact + (np.log(nf / np.float32(max_exact)) / np.float32(math.log(128 / max_exact))
                         * np.float32(32 - max_exact)).astype(np.int32)
    large = np.minimum(large, 31)
    return np.where(n < max_exact, n, large)


def _frev():
    f = np.zeros((33, 383), np.float32)
    for y in range(383):
        dist = 255 - y
        if 0 <= dist < 128:
            f[int(_rel_bucket_np(np.array(dist))), y] = 1.0
        else:
            f[32, y] = 1.0
    return f


def _win_perm():
    cg0, xc0, bg0, q0, k0, v0, gt0 = 1024, 2048, 0, 3072, 4096, 4224, 4352
    cols = []
    for m in range(8):
        cols += list(range(cg0 + m * 128, cg0 + (m + 1) * 128))
        cols += list(range(xc0 + m * 128, xc0 + (m + 1) * 128))
    cols += list(range(bg0, bg0 + 1024))
    for c in range(8):
        cols += list(range(q0 + c * 64, q0 + (c + 1) * 64))
        cols += list(range(q0 + (8 + c) * 64, q0 + (9 + c) * 64))
    cols += list(range(gt0, gt0 + 2048))
    cols += list(range(k0, k0 + 128))
    cols += list(range(v0, v0 + 128))
    assert len(cols) == PROJ
    return np.array(cols)


def _attn_row_perm():
    rows = []
    for c in range(8):
        rows += list(range(c * 64, (c + 1) * 64))
        rows += list(range((8 + c) * 64, (9 + c) * 64))
    return np.array(rows)


def _proj_tiles():
    kinds = []
    for m in range(8):
        kinds += [("cg", m), ("xc", m)]
    kinds += [("bg", m) for m in range(8)]
    kinds += [("q", m) for m in range(8)]
    kinds += [("gate", m) for m in range(16)]
    return kinds


class _Stop(Exception):
    pass


def build_program(stop=None):
    nc = bass.Bass("TRN2", target_bir_lowering=False)

    ckstate = {"n": 0}

    def ckpt(n):
        ckstate["n"] += 1
        if stop is not None and ckstate["n"] == stop:
            print("STOP at checkpoint #%d (label %d)" % (stop, n))
            raise _Stop()

    def din(name, shape):
        return nc.dram_tensor(name, list(shape), F32, kind="ExternalInput").ap()

    def dout(name, shape):
        return nc.dram_tensor(name, list(shape), F32, kind="ExternalOutput").ap()

    xp = din("xp", [NPB * 128, D])
    xs = din("xs", [128, D])
    cin = din("cin", [17, D])
    vecs = din("vecs", [208, 128])
    sinkrep = din("sinkrep", [128, 16])
    hmask_d = din("hmask", [128, 1])
    relt = din("relt", [32, 16])
    frev_d = din("frev", [33, 383])
    bdmask_d = din("bdmask", [128, 128])
    ident_d = din("identf", [128, 128])
    w_ada = din("w_ada", [DEPTH, 24, 128, 8 * WCOLS])
    w_in = din("w_in", [DEPTH, 25, 128, 8 * WCOLS])
    w_brc = din("w_brc", [DEPTH, 4, 128, 8 * WCOLS])
    w_bra = din("w_bra", [DEPTH, 4, 128, 8 * WCOLS])
    w_o = din("w_o", [DEPTH, 4, 128, 8 * WCOLS])
    w_ff1 = din("w_ff1", [DEPTH, 16, 128, 8 * WCOLS])
    w_ff2 = din("w_ff2", [DEPTH, 16, 128, 8 * WCOLS])
    sconv = din("sconv", [DEPTH, 32, D])
    ck = din("ck", [DEPTH, NSEQ, 128, 128])
    cv = din("cv", [DEPTH, NSEQ, 128, 128])

    yp = dout("yp", [16 * 128, D])
    ys = dout("ys", [128, D])
    convp = dout("convp", [DEPTH, 2, D])
    kp = dout("kp", [DEPTH, 128, 128])
    vp = dout("vp", [DEPTH, 128, 128])
    convs = dout("convs", [DEPTH, 32, D])
    ks = dout("ks", [DEPTH, NSEQ, 120, 128])
    vs = dout("vs", [DEPTH, NSEQ, 120, 128])
    ksn = dout("ksn", [DEPTH, 128, 128])
    vsn = dout("vsn", [DEPTH, 128, 128])

    S = Sched()
    es = ExitStack()

    def sb(name, shape, dt):
        return es.enter_context(nc.sbuf_tensor(name, list(shape), dt))

    TM = 512
    ident = sb("ident", [128, 128], F32)
    ones_m = sb("ones_m", [128, 128], BF16)
    ones64 = sb("ones64", [128, 64], BF16)
    identb = sb("identb", [128, 128], BF16)
    vecsT = sb("vecsT", [128, 208], F32)
    der = sb("der", [128, DEPTH, 6, 8, 17], F32)
    es_t = sb("es_t", [128, 16], F32)
    hmask = sb("hmask_t", [128, 1], F32)
    epsc = sb("epsc", [128, 1], F32)
    Etab = sb("Etab", [128, 2, 16, 128], BF16)
    Esn = sb("Esn", [128, 16, 128], BF16)
    diag = sb("diag", [128, 2, 3, 128], BF16)
    xT = sb("xT", [128, NCH, TM], F32)
    xtm = sb("xtm", [128, D], F32)
    xo = sb("xo", [128, D], F32)
    s8 = sb("s8", [128, NCH, TM], BF16)
    rstd = sb("rstd", [128, TM], F32)
    tmpr = sb("tmpr", [128, 2, TM], F32)
    rt = tmpr[:, 0, :]
    hb = sb("hb", [128, NCH, TM], BF16)
    big = sb("big", [128, 32, TM], BF16)
    u_p = sb("u_p", [128, NCH, 2 + TM], BF16)
    u_s = sb("u_s", [128, NCH, NSEQ, 10], BF16)
    utail = sb("utail", [128, NCH, 2], F32)
    ustl = sb("ustl", [128, NCH, 32], F32)
    ucarry = sb("ucarry", [128, DEPTH, NCH, 2], BF16)
    kT = sb("kT", [128, DEPTH, 128 + TM], BF16)
    kTs = sb("kTs", [128, 128], BF16)
    vtm = sb("vtm", [128, DEPTH, 5, 128], BF16)
    v_s = sb("v_s", [128, 128], BF16)
    kvout = sb("kvout", [128, 256], F32)
    expS = sb("expS", [128, 3, 512], F32)
    us_f = expS[:, 0:2, :].rearrange("p a (c t) -> p (a c) t", t=128)
    PT = sb("PT", [128, 2, 2, 16, 128], BF16)
    rD = sb("rD", [128, NCH, 128], F32)
    cstage = rD[:, 0:4, :]
    mixed = sb("mixed", [128, NCH, TM], F32)
    frev = mixed[0:33, 0, 0:383]
    relx = mixed[0:33, 1, 0:16]
    bdm = mixed[:, 2, 0:128]
    bada_tmp = mixed[:, 3:5, :].rearrange("p a b -> p (a b)")[:, 0:816].rearrange("p (m n) -> p m n", n=17)
    tstage = xo[0:32, :]
    wslot = sb("wslot", [128, NWSLOT, 8 * WCOLS], BF16)
    kcT = sb("kcT", [128, NSEQ, 128], BF16)
    vc = sb("vc", [128, NSEQ, 128], BF16)
    scT = sb("scT", [128, NCH, 17], BF16)
    pall = es.enter_context(nc.psum_tensor("pall", [128, 4096], F32))

    def bank(i, n=1):
        return pall[:, i * 512:(i + n) * 512]

    B_bank = [Buf("bank%d" % i, excl=True) for i in range(8)]
    B_xT = [Buf("xT%d" % c) for c in range(NCH)]
    B_s8 = [Buf("s8_%d" % c) for c in range(NCH)]
    B_hb = [Buf("hb%d" % c) for c in range(NCH)]
    B_big = [Buf("big%d" % c) for c in range(32)]
    B_mixed = [Buf("mixed%d" % c) for c in range(NCH)]
    B_up = [Buf("up%d" % c) for c in range(NCH)]
    B_us = [Buf("us%d" % c) for c in range(NCH)]
    B_utail = Buf("utail")
    B_ucarry = [Buf("ucarry%d" % l) for l in range(DEPTH)]
    B_kT = [Buf("kT%d" % l) for l in range(DEPTH)]
    B_kTs = Buf("kTs")
    B_vtm = [Buf("vtm%d" % l) for l in range(DEPTH)]
    B_vs = Buf("vs")
    B_kvout = Buf("kvout")
    B_exp = [Buf("exp0"), Buf("exp1"), Buf("exp2")]
    B_usf = [B_exp[c // 4] for c in range(NCH)]
    B_PT = [[Buf("PT00"), Buf("PT01")], [Buf("PT10"), Buf("PT11")]]
    B_rD = Buf("rD")
    B_cstage = B_rD
    B_rstd = Buf("rstd")
    B_tmp = [Buf("tmp0"), Buf("tmp1")]
    B_rt = B_tmp[0]
    B_w = [Buf("w%d" % i) for i in range(NWSLOT)]
    B_xtm_h = [Buf("xtm0"), Buf("xtm1")]
    B_xo = Buf("xo")
    B_const = Buf("const")
    B_vecsT = Buf("vecsT")
    B_der = Buf("der")
    B_E = Buf("E")
    B_diag = [Buf("diag0"), Buf("diag1")]
    B_kcT = Buf("kcT")
    B_vc = Buf("vc")
    B_tstage = B_xo
    B_misc = Buf("misc")
    B_scT = Buf("scT")
    B_dram_out = Buf("dram_out")

    ds_w = [S.new_dsem() for _ in range(NWSLOT)]
    ds_xtm_h = [S.new_dsem(), S.new_dsem()]
    ds_xo = S.new_dsem()
    ds_const = S.new_dsem()
    ds_kvout = S.new_dsem()
    ds_cst = S.new_dsem()
    ds_tst = S.new_dsem()
    ds_vc = S.new_dsem()
    ds_d2d = S.new_dsem()

    state = {"w": 0, "ring": 0, "exp": 0, "tmp": 0, "lb": 0, "sb": 0}

    def ACT(out, in_, func, reads, writes, **kw):
        S.op("act", lambda e: e.activation(out=out, in_=in_, func=func, **kw), reads, writes)

    def TT(out, in0, in1, op, reads, writes, eng="dve"):
        S.op(eng, lambda e: e.tensor_tensor(out=out, in0=in0, in1=in1, op=op), reads, writes)

    def TS(out, in0, s1, s2, op0, op1, reads, writes, eng="dve"):
        if s2 is None:
            S.op(eng, lambda e: e.tensor_scalar(out=out, in0=in0, scalar1=s1, scalar2=None, op0=op0), reads, writes)
        else:
            S.op(eng, lambda e: e.tensor_scalar(out=out, in0=in0, scalar1=s1, scalar2=s2, op0=op0, op1=op1), reads, writes)

    def CP(out, in_, reads, writes, eng="dve"):
        S.op(eng, lambda e: e.tensor_copy(out=out, in_=in_), reads, writes)

    def MM(out, lhsT, rhs, start, stop, reads, writes, skip=False):
        if skip:
            S.op("pe", lambda e: e.matmul(out, lhsT=lhsT, rhs=rhs, start=start, stop=stop, skip_group_check=True), reads, writes)
        else:
            S.op("pe", lambda e: e.matmul(out, lhsT=lhsT, rhs=rhs, start=start, stop=stop), reads, writes)

    def TR(out, in_, idt, reads, writes):
        S.op("pe", lambda e: e.transpose(out, in_, idt), reads, writes)

    def DMA(eng, out, in_, reads, writes, dsem, **kw):
        S.op(eng, lambda e: e.dma_start(out=out, in_=in_, **kw), reads, writes, dma=True, dsem=dsem)

    def next_ring():
        r = state["ring"] % 3
        state["ring"] += 1
        return r

    WSEQ = []
    for l_ in range(DEPTH):
        for sl_ in range(24):
            WSEQ.append((w_ada[l_, sl_], 8, WCOLS))
    for _t in range(5):
        for l_ in range(DEPTH):
            for sl_ in range(25):
                WSEQ.append((w_in[l_, sl_], 8, WCOLS))
            for wsrc in (w_brc, w_bra, w_o):
                for sl_ in range(4):
                    WSEQ.append((wsrc[l_, sl_], 8, WCOLS))
            for sl_ in range(16):
                WSEQ.append((w_ff1[l_, sl_], 8, WCOLS))
            for m_ in range(NCH):
                for kh_ in range(2):
                    WSEQ.append((w_ff2[l_, m_ * 2 + kh_], 16, 128))
    LOOKAHEAD = NWSLOT - 2
    state["wi"] = 0

    def _issue_w(k):
        src_ap, kch, ncols = WSEQ[k]
        sl = k % NWSLOT
        DMA("pool", wslot[:, sl, 0:kch * ncols], src_ap, [], [B_w[sl]], ds_w[sl])

    def load_w(src_ap, kch, ncols):
        k = state["w"]
        state["w"] += 1
        assert k < len(WSEQ) and str(WSEQ[k][0]) == str(src_ap) and WSEQ[k][1] == kch, ("weight order mismatch", k)
        while state["wi"] < len(WSEQ) and state["wi"] <= k + LOOKAHEAD:
            _issue_w(state["wi"])
            state["wi"] += 1
        sl = k % NWSLOT
        return sl, wslot[:, sl, 0:kch * ncols].rearrange("p (c e) -> p c e", c=kch)

    try:
        DMA("sp", ident[:], ident_d, [], [B_const], ds_const)
        DMA("sp", hmask[:], hmask_d, [], [B_const], ds_const)
        DMA("sp", es_t[:], sinkrep, [], [B_const], ds_const)
        DMA("sp", frev[:], frev_d, [], [B_const, B_mixed[0]], ds_const)
        DMA("sp", bdm[:], bdmask_d, [], [B_const, B_mixed[2]], ds_const)
        S.op("dve", lambda e: e.memset(ones_m[:], 1.0 / 1024.0), [], [B_const])
        S.op("dve", lambda e: e.memset(ones64[:], 1.0), [], [B_const])
        S.op("dve", lambda e: e.memset(epsc[:], EPS), [], [B_const])
        S.op("dve", lambda e: e.memset(relx[:], -1e30), [], [B_misc, B_mixed[1]])
        DMA("sp", relx[0:32, :], relt, [], [B_misc, B_mixed[1]], ds_const)
        CP(identb[:], ident[:], [B_const], [B_const])
        ACT(es_t[:], es_t[:], AF.Exp, [B_const], [B_const])

        for hlf in range(2):
            DMA("sp", xtm[0:104, 0:128], vecs[hlf * 104:(hlf + 1) * 104, :], [], B_xtm_h, ds_xtm_h[0])
            TR(bank(3)[:, 0:104], xtm[0:104, 0:128], ident[0:104, 0:104], B_xtm_h + [B_const], [B_bank[3]])
            CP(vecsT[:, hlf * 104:(hlf + 1) * 104], bank(3)[:, 0:104], [B_bank[3]], [B_vecsT])

        DMA("sp", xtm[0:17, :], cin, [], B_xtm_h, ds_xtm_h[0])
        for c in range(NCH):
            TR(bank(3)[:, c * 17:(c + 1) * 17], xtm[0:17, c * 128:(c + 1) * 128], ident[0:17, 0:17], B_xtm_h + [B_const], [B_bank[3]])
        ACT(scT[:].rearrange("p c n -> p (c n)"), bank(3)[:, 0:NCH * 17], AF.Silu, [B_bank[3]], [B_scT])

        ckpt(0)
        tiles = [(0, 4, False), (4, 4, False), (8, 4, False), (12, 4, False), (16, 2, True)]
        def tile_srcs(ti_):
            b0_, nbp_, has_s_ = tiles[ti_]
            return [xp[(b0_ + j) * 128:(b0_ + j + 1) * 128, :] for j in range(nbp_)] + ([xs] if has_s_ else [])

        xpre = set()

        def sample_prep(l):
            for g4 in range(4):
                DMA("sp", cstage[:], ck[l, g4 * 4:(g4 + 1) * 4].rearrange("s p d -> p s d"), [], [B_cstage], ds_cst)
                for sl in range(4):
                    TR(bank(3)[:, sl * 128:(sl + 1) * 128], cstage[:, sl, :], ident[:], [B_cstage, B_const], [B_bank[3]])
                CP(kcT[:, g4 * 4:(g4 + 1) * 4, :], bank(3).rearrange("p (s t) -> p s t", s=4), [B_bank[3]], [B_kcT])
            DMA("pool", vc[:], cv[l].rearrange("s p d -> p s d"), [], [B_vc], ds_vc)
            DMA("sp", tstage[:], sconv[l], [], [B_tstage], ds_tst)
            for c in range(NCH):
                TR(bank(3)[:, c * 32:(c + 1) * 32], tstage[:, c * 128:(c + 1) * 128], ident[0:32, 0:32], [B_tstage, B_const], [B_bank[3]])
            CP(u_s[:, :, :, 0:2], bank(3)[:, 0:256].rearrange("p (c s r) -> p c s r", c=NCH, r=2), [B_bank[3]], B_us)

        def stage_of(j, hf):
            if j % 2 == 0:
                return xtm, [B_xtm_h[hf]], ds_xtm_h[hf]
            return xo, [B_xo], ds_xo

        def xload(tix_):
            srcs = tile_srcs(tix_)
            for j, src in enumerate(srcs):
                for hf in range(2):
                    st, sbufs, sds = stage_of(j, hf)
                    if (tix_, j) not in xpre:
                        DMA("sp", st[:, hf * 512:(hf + 1) * 512], src[:, hf * 512:(hf + 1) * 512], [], sbufs, sds)
                    rb = state["lb"] % 4
                    state["lb"] += 1
                    for cc in range(4):
                        c = hf * 4 + cc
                        TR(bank(rb)[:, cc * 128:(cc + 1) * 128], st[:, c * 128:(c + 1) * 128], ident[:], sbufs + [B_const], [B_bank[rb]])
                    dstv = xT[:, hf * 4:(hf + 1) * 4, j * 128:(j + 1) * 128]
                    if hf == 0:
                        CP(dstv, bank(rb).rearrange("p (c t) -> p c t", c=4), [B_bank[rb]], B_xT[hf * 4:(hf + 1) * 4])
                    else:
                        ACT(dstv, bank(rb).rearrange("p (c t) -> p c t", c=4), AF.Copy, [B_bank[rb]], B_xT[hf * 4:(hf + 1) * 4])

        xload(0)
        sample_prep(0)

        pmod = pall[:, 4 * 512:7 * 512].rearrange("p (m n) -> p m n", n=32)

        def mod_slot(l, sl):
            s, wv = load_w(w_ada[l, sl], 8, WCOLS)
            for mt in range(2):
                m = sl * 2 + mt
                for c in range(NCH):
                    MM(pmod[:, m, 0:17], wv[:, c, mt * 128:(mt + 1) * 128], scT[:, c, :], c == 0, c == NCH - 1,
                       [B_w[s], B_scT], [B_bank[4 + m // 16]])

        def mod_finish(l):
            base = l * 104
            bada = vecsT[:, base + 56:base + 104]
            TT(bada_tmp[:], pmod[:, :, 0:17], bada.unsqueeze(2).to_broadcast([128, 48, 17]), ALU.add,
               [B_bank[4], B_bank[5], B_bank[6], B_vecsT], [B_misc, B_mixed[3], B_mixed[4]])
            for k, (gofs, scofs, mode) in enumerate([(0, 8, "a"), (None, 0, "b"), (8, 16, "g"), (16, 32, "a"), (None, 24, "b"), (24, 40, "g")]):
                dst = der[:, l, k]
                src = bada_tmp[:, scofs:scofs + 8, :]
                if mode == "b":
                    CP(dst, src, [B_misc, B_mixed[3], B_mixed[4]], [B_der])
                else:
                    gvec = vecsT[:, base + gofs:base + gofs + 8].unsqueeze(2).to_broadcast([128, 8, 17])
                    if mode == "a":
                        TS(dst, src, 1.0, None, ALU.add, None, [B_misc, B_mixed[3], B_mixed[4]], [B_der])
                        TT(dst, dst, gvec, ALU.mult, [B_der, B_vecsT], [B_der])
                    else:
                        TT(dst, src, gvec, ALU.mult, [B_misc, B_mixed[3], B_mixed[4], B_vecsT], [B_der])

        frevb0 = s8[0:33, 0, 0:382]
        frevb1 = s8[0:33, 1, 0:382]
        relh = s8[0:33, 2, 0:16]
        rell = s8[0:33, 2, 16:32]
        CP(frevb0, frev[:, 0:382], [B_const, B_mixed[0]], [B_s8[0]])
        CP(frevb1, frev[:, 1:383], [B_const, B_mixed[0]], [B_s8[1]])
        CP(relh, relx[:], [B_misc, B_mixed[1]], [B_s8[2]])
        TT(rell, relx[:], relh, ALU.subtract, [B_misc, B_mixed[1], B_s8[2]], [B_s8[2]])
        e_jobs = []
        for kb, off in ((0, 127), (1, 255)):
            for i in range(128):
                e_jobs.append((kb, off, i))

        def e_build(n):
            for _ in range(n):
                if not e_jobs:
                    return
                kb, off, i = e_jobs.pop(0)
                pE = pall[:, 0:2048].rearrange("p (i h) -> p i h", h=16)
                o = off - i
                lw = frevb0[:, o:o + 128] if o % 2 == 0 else frevb1[:, o - 1:o - 1 + 128]
                rds = [B_s8[0], B_s8[1], B_s8[2]]
                MM(pE[:, i, :], lw, relh, True, False, rds, [B_bank[i // 32]])
                MM(pE[:, i, :], lw, rell, False, True, rds, [B_bank[i // 32]])
                if i == 127:
                    ACT(Etab[:, kb].rearrange("p h i -> p i h"), pE, AF.Exp, [B_bank[j] for j in range(4)], [B_E])

        for sl in range(24):
            mod_slot(0, sl)
            e_build(6)
        mod_finish(0)
        for sl in range(24):
            mod_slot(1, sl)
            e_build(6)
        e_build(256)
        mod_finish(1)
        TT(Esn[:], Etab[:, 1], bdm[:].unsqueeze(1).to_broadcast([128, 16, 128]), ALU.mult, [B_E, B_const, B_mixed[2]], [B_E])

        ckpt(2)
        S.op("dve", lambda e: e.memset(ucarry[:], 0.0), [], B_ucarry)
        S.op("dve", lambda e: e.memset(kT[:, :, 0:128], 0.0), [], B_kT)
        S.op("dve", lambda e: e.memset(vtm[:, :, 0, :], 0.0), [], B_vtm)

        LO = {"v": 0}
        def norm_stats(T):
            for c in range(NCH):
                MM(bank(3)[:, LO["v"]:T], ones_m[:], s8[:, c, LO["v"]:T], c == 0, c == NCH - 1, [B_const, B_s8[c]], [B_bank[3]])
            ACT(rt[:, LO["v"]:T], bank(3)[:, LO["v"]:T], AF.Ln, [B_bank[3], B_const], [B_rt], bias=epsc[:, 0:1])
            ACT(rstd[:, LO["v"]:T], rt[:, LO["v"]:T], AF.Exp, [B_rt], [B_rstd], scale=-0.5)

        def pre_norm(l, T, Tp, ka, kbb, squares_done=False):
            if not squares_done:
                for c in range(NCH):
                    ACT(s8[:, c, LO["v"]:T], xT[:, c, LO["v"]:T], AF.Square, [B_xT[c]], [B_s8[c]])
            norm_stats(T)
            for c in range(NCH):
                ti = state["tmp"] % 2
                state["tmp"] += 1
                tv = tmpr[:, ti, LO["v"]:T]
                eng = "pool" if c in (1, 4, 6) else "dve"
                TT(tv, xT[:, c, LO["v"]:T], rstd[:, LO["v"]:T], ALU.mult, [B_xT[c], B_rstd], [B_tmp[ti]], eng=eng)
                if Tp > 0:
                    ACT(hb[:, c, LO["v"]:Tp], tmpr[:, ti, LO["v"]:Tp], AF.Identity, [B_tmp[ti], B_der], [B_hb[c]],
                        scale=der[:, l, ka, c, 0:1], bias=der[:, l, kbb, c, 0:1])
                if T > Tp:
                    sv = tmpr[:, ti, Tp:T].rearrange("p (s i) -> p s i", i=8)
                    TT(sv, sv, der[:, l, ka, c, 1:17].unsqueeze(2).to_broadcast([128, 16, 8]), ALU.mult,
                       [B_tmp[ti], B_der], [B_tmp[ti]])
                    TT(hb[:, c, Tp:T].rearrange("p (s i) -> p s i", i=8), sv,
                       der[:, l, kbb, c, 1:17].unsqueeze(2).to_broadcast([128, 16, 8]), ALU.add,
                       [B_tmp[ti], B_der], [B_hb[c]])

        def post_norm_residual(l, T, Tp, kg, squares_after=True, into_mixed=False):
            norm_stats(T)
            for c in range(NCH):
                eng = "pool" if c in (1, 4, 6) else "dve"
                TT(mixed[:, c, LO["v"]:T], mixed[:, c, LO["v"]:T], rstd[:, LO["v"]:T], ALU.mult, [B_mixed[c], B_rstd], [B_mixed[c]], eng=eng)
                if into_mixed:
                    TT(mixed[:, c, LO["v"]:T], mixed[:, c, LO["v"]:T], xT[:, c, LO["v"]:T], ALU.add, [B_xT[c], B_mixed[c]], [B_mixed[c]], eng=eng)
                    continue
                TT(xT[:, c, LO["v"]:T], xT[:, c, LO["v"]:T], mixed[:, c, LO["v"]:T], ALU.add, [B_xT[c], B_mixed[c]], [B_xT[c]], eng=eng)
                if squares_after:
                    ACT(s8[:, c, LO["v"]:T], xT[:, c, LO["v"]:T], AF.Square, [B_xT[c]], [B_s8[c]])

        def evac_scaled(l, m, rb, T, Tp, kg):
            pv = bank(rb)
            ACT(s8[:, m, LO["v"]:T], pv[:, LO["v"]:T], AF.Square, [B_bank[rb]], [B_s8[m]])
            if Tp > 0:
                ACT(mixed[:, m, LO["v"]:Tp], pv[:, LO["v"]:Tp], AF.Identity, [B_bank[rb], B_der], [B_mixed[m]],
                    scale=der[:, l, kg, m, 0:1])
            if T > Tp:
                TT(mixed[:, m, Tp:T].rearrange("p (s i) -> p s i", i=8), pv[:, Tp:T].rearrange("p (s i) -> p s i", i=8),
                   der[:, l, kg, m, 1:17].unsqueeze(2).to_broadcast([128, 16, 8]), ALU.mult,
                   [B_bank[rb], B_der], [B_mixed[m]])

        def fm_tile(wv, mt, s, rhs_fn, rhs_bufs, T, nk=NCH, first=True, last=True, rb=None, kofs=0):
            if rb is None:
                rb = next_ring()
            for c in range(nk):
                MM(bank(rb)[:, LO["v"]:T], wv[:, c, mt * 128:(mt + 1) * 128], rhs_fn(kofs + c), first and c == 0, last and c == nk - 1,
                   [B_w[s]] + rhs_bufs(kofs + c), [B_bank[rb]])
            return rb

        def fm_group(wvs, ss, mts, rhs_fn, rhs_bufs, T, banks):
            for c in range(NCH):
                for g in range(len(mts)):
                    MM(bank(banks[g])[:, LO["v"]:T], wvs[g][:, c, mts[g] * 128:(mts[g] + 1) * 128], rhs_fn(c), c == 0, c == NCH - 1,
                       [B_w[ss[g]]] + rhs_bufs(c), [B_bank[banks[g]]])

        pending = []

        def make_store(j, is_s, gblk):
            def fn():
                for hf in range(2):
                    rb = 4 + state["sb"] % 4
                    state["sb"] += 1
                    for cc in range(4):
                        c = hf * 4 + cc
                        TR(bank(rb)[:, cc * 128:(cc + 1) * 128], mixed[:, c, j * 128:(j + 1) * 128], ident[:], [B_mixed[c], B_const],
                           [B_bank[rb]])
                    if hf == 0:
                        CP(xo[:, 0:512], bank(rb), [B_bank[rb]], [B_xo])
                    else:
                        ACT(xo[:, 512:1024], bank(rb), AF.Copy, [B_bank[rb]], [B_xo])
                dst = ys if is_s else yp[(gblk - 2) * 128:(gblk - 1) * 128, :]
                DMA("sp", dst, xo[:], [B_xo], [B_dram_out], ds_xo)
            return fn

        for tix, (b0, nbp, has_s) in enumerate(tiles):
            Tp = nbp * 128
            T = Tp + (128 if has_s else 0)
            if tix > 0:
                xload(tix)
            ckpt(3)
            for l in range(DEPTH):
                base = l * 104
                if b0 == 0:
                    lo_u, lo_m = (0, 128) if l == 0 else (128, 256)
                else:
                    lo_u, lo_m = 0, 0
                LO["v"] = lo_u
                if l == 1 and tix + 1 < len(tiles):
                    for jn in range(2):
                        nsrc = tile_srcs(tix + 1)[jn]
                        for hf in range(2):
                            st, sbufs, sds = stage_of(jn, hf)
                            DMA("sp", st[:, hf * 512:(hf + 1) * 512], nsrc[:, hf * 512:(hf + 1) * 512], [], sbufs, sds)
                        xpre.add((tix + 1, jn))
                if tix == 2:
                    DMA("sp", ks[l], ck[l][:, 8:128, :], [], [B_dram_out], ds_d2d)
                    DMA("sp", vs[l], cv[l][:, 8:128, :], [], [B_dram_out], ds_d2d)
                CP(u_p[:, :, 0:2], ucarry[:, l], [B_ucarry[l]], B_up)

                pre_norm(l, T, Tp, 0, 1, squares_done=(l > 0))

                ckpt(4)
                hfn = lambda c: hb[:, c, LO["v"]:T]
                hbufs = lambda c: [B_hb[c]]
                kinds = _proj_tiles()
                cgst = {}

                def proj_evac(kind, m, rb):
                    pv = bank(rb)
                    if kind == "cg":
                        ti = state["tmp"] % 2
                        state["tmp"] += 1
                        cgst[m] = ti
                        ACT(tmpr[:, ti, LO["v"]:T], pv[:, LO["v"]:T], AF.Copy, [B_bank[rb]], [B_tmp[ti]])
                    elif kind == "xc":
                        ti = cgst[m]
                        TT(u_p[:, m, 2 + LO["v"]:2 + Tp], pv[:, LO["v"]:Tp], tmpr[:, ti, LO["v"]:Tp], ALU.mult, [B_bank[rb], B_tmp[ti]], [B_up[m]])
                        if b0 + nbp == NPB:
                            TT(utail[:, m, :], pv[:, Tp - 2:Tp], tmpr[:, ti, Tp - 2:Tp], ALU.mult, [B_bank[rb], B_tmp[ti]], [B_utail])
                        if has_s:
                            TT(us_f[:, m, :], pv[:, Tp:T], tmpr[:, ti, Tp:T], ALU.mult, [B_bank[rb], B_tmp[ti]], [B_usf[m]])
                            CP(u_s[:, m, :, 2:10], us_f[:, m, :].rearrange("p (s i) -> p s i", i=8), [B_usf[m]], [B_us[m]])
                            CP(ustl[:, m, :].rearrange("p (s r) -> p s r", r=2),
                               us_f[:, m, :].rearrange("p (s i) -> p s i", i=8)[:, :, 6:8], [B_usf[m]], [B_utail])
                    elif kind == "bg":
                        ACT(big[:, 16 + m, LO["v"]:T], pv[:, LO["v"]:T], AF.Copy, [B_bank[rb]], [B_big[16 + m]])
                    elif kind == "q":
                        ACT(big[:, 24 + m, LO["v"]:T], pv[:, LO["v"]:T], AF.Copy, [B_bank[rb]], [B_big[24 + m]], scale=0.125)
                    else:
                        ACT(big[:, m, LO["v"]:T], pv[:, LO["v"]:T], AF.Sigmoid, [B_bank[rb]], [B_big[m]])

                LO["v"] = lo_u
                s0, wv0 = load_w(w_in[l, 0], 8, WCOLS)
                s1, wv1 = load_w(w_in[l, 1], 8, WCOLS)
                hbanks = [next_ring(), next_ring(), next_ring(), 4]
                fm_group([wv0, wv0, wv1, wv1], [s0, s0, s1, s1], [0, 1, 0, 1], hfn, hbufs, T, hbanks)
                for g in range(4):
                    proj_evac(kinds[g][0], kinds[g][1], hbanks[g])
                for sl in range(2, 24):
                    s, wv = load_w(w_in[l, sl], 8, WCOLS)
                    if pending and sl in (3, 8, 13, 18):
                        pending.pop(0)()
                    for mt in range(2):
                        kind, m = kinds[sl * 2 + mt]
                        LO["v"] = lo_u if kind in ("cg", "xc") else lo_m
                        rb = fm_tile(wv, mt, s, hfn, hbufs, T)
                        proj_evac(kind, m, rb)
                ckpt(41)
                while pending:
                    pending.pop(0)()
                s, wv = load_w(w_in[l, 24], 8, WCOLS)
                LO["v"] = lo_u
                rb = fm_tile(wv, 0, s, hfn, hbufs, T)
                ACT(kT[:, l, 128 + LO["v"]:128 + Tp], bank(rb)[:, LO["v"]:Tp], AF.Copy, [B_bank[rb]], [B_kT[l]])
                if has_s:
                    ACT(kTs[:], bank(rb)[:, Tp:T], AF.Copy, [B_bank[rb]], [B_kTs])
                ckpt(42)
                nblk = nbp + (1 if has_s else 0)
                for j in range(nblk):
                    is_s = j >= nbp
                    if j * 128 < lo_u:
                        continue
                    for c in range(NCH):
                        MM(bank(3)[:, 0:256], hb[:, c, j * 128:(j + 1) * 128], wv[:, c, 0:256], c == 0, c == NCH - 1,
                           [B_hb[c], B_w[s]], [B_bank[3]])
                    if is_s:
                        CP(v_s[:], bank(3)[:, 128:256], [B_bank[3]], [B_vs])
                    else:
                        CP(vtm[:, l, 1 + j, :], bank(3)[:, 128:256], [B_bank[3]], [B_vtm[l]])
                    if is_s or (b0 + j == NPB - 1):
                        ACT(kvout[:], bank(3)[:, 0:256], AF.Copy, [B_bank[3]], [B_kvout])
                        if is_s:
                            DMA("sp", ksn[l], kvout[:, 0:128], [B_kvout], [B_dram_out], ds_kvout)
                            DMA("sp", vsn[l], kvout[:, 128:256], [B_kvout], [B_dram_out], ds_kvout)
                        else:
                            DMA("sp", kp[l], kvout[:, 0:128], [B_kvout], [B_dram_out], ds_kvout)
                            DMA("sp", vp[l], kvout[:, 128:256], [B_kvout], [B_dram_out], ds_kvout)

                ckpt(5)
                LO["v"] = lo_m

                def conv_stage():
                  if True:
                    if b0 == 0:
                        TS(u_p[:, :, 2 + 254:2 + 256], u_p[:, :, 2 + 254:2 + 256], hmask[:, 0:1], None, ALU.mult, None,
                           B_up + [B_const], B_up)
                    for m in range(NCH):
                        dr = m % 2
                        for j in range(3):
                            TS(diag[:, dr, j, :], identb[:], vecsT[:, base + 32 + j * 8 + m:base + 33 + j * 8 + m], None, ALU.mult, None,
                               [B_const, B_vecsT], [B_diag[dr]])
                        rb = next_ring()
                        for j in range(3):
                            MM(bank(rb)[:, LO["v"]:Tp], diag[:, dr, j, :], u_p[:, m, LO["v"] + j:j + Tp], j == 0, j == 2, [B_diag[dr], B_up[m]], [B_bank[rb]])
                        if has_s:
                            for j in range(3):
                                MM(bank(rb)[:, Tp:T].rearrange("p (s i) -> p s i", i=8), diag[:, dr, j, :], u_s[:, m, :, j:j + 8],
                                   j == 0, j == 2, [B_diag[dr], B_us[m]], [B_bank[rb]])
                        TT(s8[:, m, LO["v"]:T], bank(rb)[:, LO["v"]:T], big[:, 16 + m, LO["v"]:T], ALU.mult, [B_bank[rb], B_big[16 + m]], [B_s8[m]])
                    CP(ucarry[:, l], u_p[:, :, Tp:Tp + 2], B_up, [B_ucarry[l]])
                ckpt(6)
                pD = pall[:, 6 * 512:8 * 512].rearrange("p (c q) -> p c q", c=8)
                pO = pall[:, 4 * 512:6 * 512].rearrange("p (c q) -> p c q", c=8)

                def att_A(j):
                    is_s = j >= nbp
                    q0 = j * 128
                    pb = j % 2
                    for kb in range(2):
                        for g in range(2):
                            for quad in range(2):
                                rb = next_ring()
                                hs = 8 * g + 4 * quad
                                qv = big[g * 64:(g + 1) * 64, 24 + 4 * quad:24 + 4 * quad + 4, q0:q0 + 128]
                                qb = B_big[24 + 4 * quad:24 + 4 * quad + 4]
                                pv = bank(rb)
                                if not is_s:
                                    kcols = slice(j * 128 + kb * 128, j * 128 + kb * 128 + 128)
                                    MM(pv, kT[g * 64:(g + 1) * 64, l, kcols], qv, True, True, [B_kT[l]] + qb, [B_bank[rb]])
                                    ein = Etab[:, kb, hs:hs + 4, :]
                                elif kb == 1:
                                    MM(pv, kTs[g * 64:(g + 1) * 64, :], qv, True, True, [B_kTs] + qb, [B_bank[rb]])
                                    ein = Esn[:, hs:hs + 4, :]
                                else:
                                    for sq in range(NSEQ):
                                        MM(pv[:, sq * 32:(sq + 1) * 32], kcT[g * 64:(g + 1) * 64, sq, :],
                                           big[g * 64:(g + 1) * 64, 24 + 4 * quad:24 + 4 * quad + 4, q0 + sq * 8:q0 + sq * 8 + 8],
                                           True, True, [B_kcT] + qb, [B_bank[rb]], skip=True)
                                    ein = Etab[:, 0, hs:hs + 4, 0:8].unsqueeze(1).to_broadcast([128, 16, 4, 8])
                                xi = state["exp"] % 3
                                state["exp"] += 1
                                ACT(expS[:, xi, :], pv, AF.Exp, [B_bank[rb]], [B_exp[xi]])
                                meng = "pool" if quad == 1 else "dve"
                                if is_s and kb == 0:
                                    TT(PT[:, pb, kb, hs:hs + 4, :].rearrange("p h (s i) -> p s h i", i=8),
                                       expS[:, xi, :].rearrange("p (s h i) -> p s h i", h=4, i=8), ein, ALU.mult,
                                       [B_exp[xi], B_E], [B_PT[pb][kb]], eng=meng)
                                else:
                                    TT(PT[:, pb, kb, hs:hs + 4, :], expS[:, xi, :].rearrange("p (h q) -> p h q", h=4), ein, ALU.mult,
                                       [B_exp[xi], B_E], [B_PT[pb][kb]], eng=meng)
                    if (not is_s) and (b0 + j) == 2:
                        TS(PT[:, pb, 0], PT[:, pb, 0], hmask[:, 0:1], None, ALU.mult, None, [B_PT[pb][0], B_const], [B_PT[pb][0]])

                def att_B(j):
                    is_s = j >= nbp
                    q0 = j * 128
                    pb = j % 2
                    for g in range(2):
                        for quad in range(2):
                            hs = 8 * g + 4 * quad
                            for kb in range(2):
                                MM(pD[g * 64:(g + 1) * 64, 4 * quad:4 * quad + 4, :], ones64[:], PT[:, pb, kb, hs:hs + 4, :], kb == 0, kb == 1,
                                   [B_const, B_PT[pb][kb]], [B_bank[6 + quad]])
                    if not is_s:
                        for g in range(2):
                            for c in range(8):
                                for kb in range(2):
                                    MM(pO[g * 64:(g + 1) * 64, c, :], vtm[:, l, j + kb, g * 64:(g + 1) * 64], PT[:, pb, kb, 8 * g + c, :],
                                       kb == 0, kb == 1, [B_vtm[l], B_PT[pb][kb]], [B_bank[4 + c // 4]])
                    else:
                        for g in range(2):
                            for c in range(8):
                                MM(pO[g * 64:(g + 1) * 64, c, :], v_s[:, g * 64:(g + 1) * 64], PT[:, pb, 1, 8 * g + c, :],
                                   True, True, [B_vs, B_PT[pb][1]], [B_bank[4 + c // 4]])
                        for quad in range(2):
                            for g in range(2):
                                for sq in range(NSEQ):
                                    MM(bank(quad)[g * 64:(g + 1) * 64, sq * 32:(sq + 1) * 32],
                                       vc[:, sq, g * 64:(g + 1) * 64],
                                       PT[:, pb, 0, 8 * g + 4 * quad:8 * g + 4 * quad + 4, sq * 8:(sq + 1) * 8],
                                       True, True, [B_vc, B_PT[pb][0]], [B_bank[quad]], skip=True)
                            ACT(mixed[:, 4 * quad:4 * quad + 4, 0:128].rearrange("p c (s i) -> p s c i", i=8),
                                bank(quad).rearrange("p (s c i) -> p s c i", c=4, i=8), AF.Copy, [B_bank[quad]],
                                B_mixed[4 * quad:4 * quad + 4])
                    TT(rD[:], pD, es_t[:, l * 8:(l + 1) * 8].unsqueeze(2).to_broadcast([128, 8, 128]), ALU.add,
                       [B_bank[6], B_bank[7], B_const], [B_rD])
                    ACT(rD[:], rD[:], AF.Ln, [B_rD], [B_rD])
                    ACT(rD[:], rD[:], AF.Exp, [B_rD], [B_rD], scale=-1.0)
                    if not is_s:
                        TT(hb[:, :, q0:q0 + 128], pO, rD[:], ALU.mult, [B_bank[4], B_bank[5], B_rD], B_hb)
                    else:
                        TT(mixed[:, :, 0:128], mixed[:, :, 0:128], pO, ALU.add, B_mixed + [B_bank[4], B_bank[5]], B_mixed)
                        TT(hb[:, :, q0:q0 + 128], mixed[:, :, 0:128], rD[:], ALU.mult, B_mixed + [B_rD], B_hb)

                brc_state = {"m": 0, "wv": None, "s": None}

                def brc_tiles(n):
                    for _ in range(n):
                        m = brc_state["m"]
                        if m >= NCH:
                            return
                        if m % 2 == 0:
                            brc_state["s"], brc_state["wv"] = load_w(w_brc[l, m // 2], 8, WCOLS)
                        rb = fm_tile(brc_state["wv"], m % 2, brc_state["s"], lambda c: s8[:, c, LO["v"]:T], lambda c: [B_s8[c]], T)
                        TT(big[:, 16 + m, LO["v"]:T], bank(rb)[:, LO["v"]:T], big[:, m, LO["v"]:T], ALU.mult, [B_bank[rb], B_big[m]], [B_big[16 + m]])
                        brc_state["m"] = m + 1

                ablk = [j for j in range(nblk) if j * 128 >= lo_m]
                per = NCH // (len(ablk) + 1)
                att_A(ablk[0])
                conv_stage()
                for ii, j in enumerate(ablk):
                    if ii + 1 < len(ablk):
                        att_A(ablk[ii + 1])
                    brc_tiles(per)
                    att_B(j)
                brc_tiles(NCH)
                if not has_s:
                    CP(kT[:, l, 0:128], kT[:, l, Tp:Tp + 128], [B_kT[l]], [B_kT[l]])
                    CP(vtm[:, l, 0, :], vtm[:, l, nbp, :], [B_vtm[l]], [B_vtm[l]])

                ckpt(7)
                if has_s and l == 0:
                    sample_prep(1)
                for half in range(4):
                    s, wv = load_w(w_bra[l, half], 8, WCOLS)
                    for mt in range(2):
                        m = half * 2 + mt
                        rb = fm_tile(wv, mt, s, lambda c: hb[:, c, LO["v"]:T], lambda c: [B_hb[c]], T)
                        TT(big[:, 24 + m, LO["v"]:T], bank(rb)[:, LO["v"]:T], big[:, 8 + m, LO["v"]:T], ALU.mult, [B_bank[rb], B_big[8 + m]], [B_big[24 + m]])
                        TT(big[:, 16 + m, LO["v"]:T], big[:, 16 + m, LO["v"]:T], big[:, 24 + m, LO["v"]:T], ALU.add, [B_big[16 + m], B_big[24 + m]],
                           [B_big[16 + m]], eng="pool")
                for half in range(4):
                    s, wv = load_w(w_o[l, half], 8, WCOLS)
                    for mt in range(2):
                        m = half * 2 + mt
                        rb = fm_tile(wv, mt, s, lambda c: big[:, 16 + c, LO["v"]:T], lambda c: [B_big[16 + c]], T)
                        evac_scaled(l, m, rb, T, Tp, 2)
                post_norm_residual(l, T, Tp, 2, squares_after=True)

                ckpt(8)
                pre_norm(l, T, Tp, 3, 4, squares_done=True)
                def ff1_evac(jx, rb):
                    ACT(big[:, jx, LO["v"]:T], bank(rb)[:, LO["v"]:T], AF.Relu, [B_bank[rb]], [B_big[jx]])
                    TT(big[:, jx, LO["v"]:T], big[:, jx, LO["v"]:T], big[:, jx, LO["v"]:T], ALU.mult, [B_big[jx]], [B_big[jx]],
                       eng=("pool" if jx % 2 else "dve"))

                s0, wv0 = load_w(w_ff1[l, 0], 8, WCOLS)
                s1, wv1 = load_w(w_ff1[l, 1], 8, WCOLS)
                hbanks = [next_ring(), next_ring(), next_ring(), 4]
                fm_group([wv0, wv0, wv1, wv1], [s0, s0, s1, s1], [0, 1, 0, 1], lambda c: hb[:, c, LO["v"]:T], lambda c: [B_hb[c]], T, hbanks)
                for g in range(4):
                    ff1_evac(g, hbanks[g])
                for sl in range(2, 16):
                    s, wv = load_w(w_ff1[l, sl], 8, WCOLS)
                    for mt in range(2):
                        jx = sl * 2 + mt
                        rb = fm_tile(wv, mt, s, lambda c: hb[:, c, LO["v"]:T], lambda c: [B_hb[c]], T)
                        ff1_evac(jx, rb)
                for m in range(NCH):
                    rb = next_ring()
                    for kh in range(2):
                        s, wv = load_w(w_ff2[l, m * 2 + kh], 16, 128)
                        fm_tile(wv, 0, s, lambda c: big[:, c, LO["v"]:T], lambda c: [B_big[c]], T, nk=16, first=(kh == 0), last=(kh == 1),
                                rb=rb, kofs=kh * 16)
                    evac_scaled(l, m, rb, T, Tp, 5)
                post_norm_residual(l, T, Tp, 5, squares_after=(l + 1 < DEPTH), into_mixed=(l + 1 == DEPTH))

                ckpt(9)
                if b0 + nbp == NPB:
                    for hf in range(2):
                        for cc in range(4):
                            c = hf * 4 + cc
                            TR(bank(3)[0:2, cc * 128:(cc + 1) * 128], utail[:, c, :], ident[:], [B_utail, B_const], [B_bank[3]])
                        CP(tstage[0:2, hf * 512:(hf + 1) * 512], bank(3)[0:2, :], [B_bank[3]], [B_tstage])
                    DMA("sp", convp[l], tstage[0:2, :], [B_tstage], [B_dram_out], ds_tst)
                if has_s:
                    for hf in range(2):
                        for cc in range(4):
                            c = hf * 4 + cc
                            TR(bank(3)[0:32, cc * 128:(cc + 1) * 128],
                               ustl[:, c, :], ident[:], [B_utail, B_const], [B_bank[3]])
                        CP(tstage[:, hf * 512:(hf + 1) * 512], bank(3)[0:32, :], [B_bank[3]], [B_tstage])
                    DMA("sp", convs[l], tstage[:], [B_tstage], [B_dram_out], ds_tst)

            for j in range(nbp + (1 if has_s else 0)):
                is_s = j >= nbp
                gblk = b0 + j
                if (not is_s) and gblk < 2:
                    continue
                pending.append(make_store(j, is_s, gblk))
        while pending:
            pending.pop(0)()
    except _Stop:
        pass
    S.emit(nc)
    es.close()
    return nc


_CACHE = {}


def _prep(x_prompt, x_sample, c_prompt, c_sample, state_conv, cache_k, cache_v,
          w_ada, b_ada, g_pre1, w_in, conv_w, w_br_conv, w_br_attn, w_o, sinks,
          g_post1, g_pre2, w_ff1, w_ff2, g_post2, rel_table):
    f = lambda a: np.ascontiguousarray(np.asarray(a, dtype=np.float32))
    x_prompt, x_sample, c_prompt, c_sample = f(x_prompt), f(x_sample), f(c_prompt), f(c_sample)
    state_conv, cache_k, cache_v = f(state_conv), f(cache_k), f(cache_v)
    w_ada, b_ada, g_pre1, w_in, conv_w = f(w_ada), f(b_ada), f(g_pre1), f(w_in), f(conv_w)
    w_br_conv, w_br_attn, w_o, sinks = f(w_br_conv), f(w_br_attn), f(w_o), f(sinks)
    g_post1, g_pre2, w_ff1, w_ff2, g_post2, rel_table = f(g_post1), f(g_pre2), f(w_ff1), f(w_ff2), f(g_post2), f(rel_table)

    def wtile(w):
        L, K, N = w.shape
        return np.ascontiguousarray(w.reshape(L, K // 128, 128, N // WCOLS, WCOLS).transpose(0, 3, 2, 1, 4)).reshape(
            L, N // WCOLS, 128, (K // 128) * WCOLS)

    w_in_p = wtile(w_in[:, :, _win_perm()])
    w_bra_p = wtile(w_br_attn[:, _attn_row_perm(), :])
    w_ada_t, w_brc_t, w_o_t, w_ff1_t = wtile(w_ada), wtile(w_br_conv), wtile(w_o), wtile(w_ff1)
    w_ff2_t = np.ascontiguousarray(w_ff2.reshape(DEPTH, 2, 16, 128, 8, 128).transpose(0, 4, 1, 3, 2, 5)).reshape(DEPTH, 16, 128, 2048)
    vecs = np.zeros((208, 128), np.float32)
    for l in range(DEPTH):
        b = l * 104
        vecs[b + 0:b + 8] = g_pre1[l].reshape(8, 128)
        vecs[b + 8:b + 16] = g_post1[l].reshape(8, 128)
        vecs[b + 16:b + 24] = g_pre2[l].reshape(8, 128)
        vecs[b + 24:b + 32] = g_post2[l].reshape(8, 128)
        vecs[b + 32:b + 56] = conv_w[l].reshape(24, 128)
        vecs[b + 56:b + 104] = b_ada[l].reshape(48, 128)
    sinkrep = np.zeros((128, 16), np.float32)
    for l in range(DEPTH):
        for c in range(8):
            sinkrep[0:64, l * 8 + c] = sinks[l, c]
            sinkrep[64:128, l * 8 + c] = sinks[l, 8 + c]
    frev = _frev()
    jj = np.arange(128)
    bdmask = (jj[:, None] // 8 == jj[None, :] // 8).astype(np.float32)
    identf = np.eye(128, dtype=np.float32)

    in_maps = []
    for core in range(NCORES):
        b, half = core // 2, core % 2
        xpc = np.zeros((NPB * 128, D), np.float32)
        if half == 0:
            xpc[256:] = x_prompt[b, 0:2048]
        else:
            xpc[:] = x_prompt[b, 2048 - 256:4096]
        ss = slice(core * NSEQ, (core + 1) * NSEQ)
        cinp = np.concatenate([c_prompt[b:b + 1], c_sample[ss]], axis=0)
        in_maps.append({
            "xp": xpc,
            "xs": np.ascontiguousarray(x_sample[ss].reshape(128, D)),
            "cin": np.ascontiguousarray(cinp),
            "vecs": vecs,
            "sinkrep": sinkrep,
            "hmask": np.full((128, 1), float(half), np.float32),
            "relt": rel_table,
            "frev": frev,
            "bdmask": bdmask,
            "identf": identf,
            "w_ada": w_ada_t, "w_in": w_in_p, "w_brc": w_brc_t, "w_bra": w_bra_p, "w_o": w_o_t,
            "w_ff1": w_ff1_t, "w_ff2": w_ff2_t,
            "sconv": np.ascontiguousarray(state_conv[:, ss].reshape(DEPTH, 32, D)),
            "ck": np.ascontiguousarray(cache_k[:, ss].reshape(DEPTH, NSEQ, 128, 128)),
            "cv": np.ascontiguousarray(cache_v[:, ss].reshape(DEPTH, NSEQ, 128, 128)),
        })
    return in_maps


def _assemble(R):

    y_prompt = np.zeros((4, 4096, D), np.float32)
    y_sample = np.zeros((128, 8, D), np.float32)
    conv_prompt = np.zeros((DEPTH, 4, 2, D), np.float32)
    k_prompt = np.zeros((DEPTH, 4, 128, 2, 64), np.float32)
    v_prompt = np.zeros((DEPTH, 4, 128, 2, 64), np.float32)
    conv_sample = np.zeros((DEPTH, 128, 2, D), np.float32)
    k_sample = np.zeros((DEPTH, 128, 128, 2, 64), np.float32)
    v_sample = np.zeros((DEPTH, 128, 128, 2, 64), np.float32)
    for core in range(NCORES):
        b, half = core // 2, core % 2
        r = R[core]
        y_prompt[b, half * 2048:(half + 1) * 2048] = r["yp"]
        ss = slice(core * NSEQ, (core + 1) * NSEQ)
        y_sample[ss] = r["ys"].reshape(NSEQ, 8, D)
        conv_sample[:, ss] = r["convs"].reshape(DEPTH, NSEQ, 2, D)
        k_sample[:, ss, 0:120] = r["ks"].reshape(DEPTH, NSEQ, 120, 2, 64)
        v_sample[:, ss, 0:120] = r["vs"].reshape(DEPTH, NSEQ, 120, 2, 64)
        k_sample[:, ss, 120:128] = r["ksn"].reshape(DEPTH, NSEQ, 8, 2, 64)
        v_sample[:, ss, 120:128] = r["vsn"].reshape(DEPTH, NSEQ, 8, 2, 64)
        if half == 1:
            conv_prompt[:, b] = r["convp"]
            k_prompt[:, b] = r["kp"].reshape(DEPTH, 128, 2, 64)
            v_prompt[:, b] = r["vp"].reshape(DEPTH, 128, 2, 64)
    return (y_prompt, y_sample, conv_prompt, k_prompt, v_prompt, conv_sample, k_sample, v_sample)


def kernel(**inputs):
    in_maps = _prep(**inputs)
    if "nc" not in _CACHE:
        _CACHE["nc"] = build_program()
    res = run_bass_kernel_spmd(_CACHE["nc"], in_maps, core_ids=list(range(NCORES)))
    return _assemble(res.results)
```

```python
import math
from contextlib import ExitStack

import numpy as np
import concourse.bass as bass
import concourse.mybir as mybir
from concourse.bass_utils import run_bass_kernel_spmd

F32 = mybir.dt.float32
BF16 = mybir.dt.bfloat16
AF = mybir.ActivationFunctionType
ALU = mybir.AluOpType

NCORES = 8
D = 1024
NCH = 8
DEPTH = 2
NPB = 18
NSEQ = 16
PROJ = 6400
DFF = 4096
EPS = 1e-6
NWSLOT = 7
WCOLS = 256
ENGS = ("pe", "act", "dve", "pool", "sp")


class Buf:
    __slots__ = ("name", "last_w", "readers", "excl")

    def __init__(self, name, excl=False):
        self.name = name
        self.last_w = None
        self.readers = []
        self.excl = excl


class Op:
    __slots__ = ("eng", "fn", "deps", "is_dma", "dsem", "needed", "inc_val")

    def __init__(self, eng, fn, deps, is_dma, dsem):
        self.eng = eng
        self.fn = fn
        self.deps = deps
        self.is_dma = is_dma
        self.dsem = dsem
        self.needed = False
        self.inc_val = None


class Sched:
    def __init__(self):
        self.ops = []
        self.n_dsem = 0

    def new_dsem(self):
        self.n_dsem += 1
        return self.n_dsem - 1

    def op(self, eng, fn, reads=(), writes=(), dma=False, dsem=None):
        ex = [b for b in reads if b.excl]
        if ex:
            writes = list(writes) + [b for b in ex if b not in writes]
            reads = [b for b in reads if not b.excl]
        deps = set()
        for b in reads:
            if b.last_w is not None:
                deps.add(b.last_w)
        for b in writes:
            if b.last_w is not None:
                deps.add(b.last_w)
            deps.update(b.readers)
        oid = len(self.ops)
        if eng == "pe":
            deps = {d for d in deps if self.ops[d].eng != "pe" or self.ops[d].is_dma}
        self.ops.append(Op(eng, fn, deps, dma, dsem))
        for b in writes:
            b.last_w = oid
            b.readers = []
        for b in reads:
            b.readers.append(oid)
        return oid

    def emit(self, nc, final_wait_eng="sp"):
        ops = self.ops
        for o in ops:
            latest = {}
            keep = set()
            for d in o.deps:
                p = ops[d]
                if p.is_dma:
                    keep.add(d)
                elif d > latest.get(p.eng, -1):
                    latest[p.eng] = d
            keep.update(latest.values())
            o.deps = keep
            for d in keep:
                ops[d].needed = True
        eng_cnt = {e: 0 for e in ENGS}
        dsem_cnt = {}
        for o in ops:
            if o.is_dma:
                dsem_cnt[o.dsem] = dsem_cnt.get(o.dsem, 0) + 16
                o.inc_val = dsem_cnt[o.dsem]
            elif o.needed:
                eng_cnt[o.eng] += 1
                o.inc_val = eng_cnt[o.eng]
        streams = {e: [] for e in ENGS}
        waited_e = {e: {f: 0 for f in ENGS} for e in ENGS}
        waited_d = {e: {} for e in ENGS}
        for o in ops:
            waits = []
            need_e = {}
            need_d = {}
            for d in o.deps:
                p = ops[d]
                if p.is_dma:
                    need_d[p.dsem] = max(need_d.get(p.dsem, 0), p.inc_val)
                else:
                    need_e[p.eng] = max(need_e.get(p.eng, 0), p.inc_val)
            for f, v in need_e.items():
                if v > waited_e[o.eng][f]:
                    waited_e[o.eng][f] = v
                    waits.append(("e", f, v))
            for s, v in need_d.items():
                if v > waited_d[o.eng].get(s, 0):
                    waited_d[o.eng][s] = v
                    waits.append(("d", s, v))
            streams[o.eng].append((waits, o))
        final_waits = []
        for s, v in dsem_cnt.items():
            if v > waited_d[final_wait_eng].get(s, 0):
                final_waits.append(("d", s, v))
        for f in ENGS:
            if eng_cnt[f] > waited_e[final_wait_eng][f]:
                final_waits.append(("e", f, eng_cnt[f]))

        with ExitStack() as es:
            esem = {e: es.enter_context(nc.semaphore("s_" + e)) for e in ENGS}
            dsems = [es.enter_context(nc.semaphore("d%d" % i)) for i in range(self.n_dsem)]
            block = es.enter_context(nc.Block())

            def run_stream(engname, eng):
                for waits, o in streams[engname]:
                    for kind, key, v in waits:
                        eng.wait_ge(esem[key] if kind == "e" else dsems[key], v)
                    ins = o.fn(eng)
                    if o.is_dma:
                        ins.then_inc(dsems[o.dsem], 16)
                    elif o.needed:
                        ins.then_inc(esem[o.eng], 1)
                if engname == final_wait_eng:
                    for kind, key, v in final_waits:
                        eng.wait_ge(esem[key] if kind == "e" else dsems[key], v)

            @block.tensor
            def _(e):
                run_stream("pe", e)

            @block.scalar
            def _(e):
                run_stream("act", e)

            @block.vector
            def _(e):
                run_stream("dve", e)

            @block.gpsimd
            def _(e):
                run_stream("pool", e)

            @block.sync
            def _(e):
                run_stream("sp", e)
        return eng_cnt


def _rel_bucket_np(dist):
    n = np.maximum(dist, 0)
    max_exact = 16
    nf = np.maximum(n, 1).astype(np.float32)
    large = max_exact + (np.log(nf / np.float32(max_exact)) / np.float32(math.log(128 / max_exact))
                         * np.float32(32 - max_exact)).astype(np.int32)
    large = np.minimum(large, 31)
    return np.where(n < max_exact, n, large)


def _frev():
    f = np.zeros((33, 383), np.float32)
    for y in range(383):
        dist = 255 - y
        if 0 <= dist < 128:
            f[int(_rel_bucket_np(np.array(dist))), y] = 1.0
        else:
            f[32, y] = 1.0
    return f


def _win_perm():
    cg0, xc0, bg0, q0, k0, v0, gt0 = 1024, 2048, 0, 3072, 4096, 4224, 4352
    cols = []
    for m in range(8):
        cols += list(range(cg0 + m * 128, cg0 + (m + 1) * 128))
        cols += list(range(xc0 + m * 128, xc0 + (m + 1) * 128))
    cols += list(range(bg0, bg0 + 1024))
    for c in range(8):
        cols += list(range(q0 + c * 64, q0 + (c + 1) * 64))
        cols += list(range(q0 + (8 + c) * 64, q0 + (9 + c) * 64))
    cols += list(range(gt0, gt0 + 2048))
    cols += list(range(k0, k0 + 128))
    cols += list(range(v0, v0 + 128))
    assert len(cols) == PROJ
    return np.array(cols)


def _attn_row_perm():
    rows = []
    for c in range(8):
        rows += list(range(c * 64, (c + 1) * 64))
        rows += list(range((8 + c) * 64, (9 + c) * 64))
    return np.array(rows)


def _proj_tiles():
    kinds = []
    for m in range(8):
        kinds += [("cg", m), ("xc", m)]
    kinds += [("bg", m) for m in range(8)]
    kinds += [("q", m) for m in range(8)]
    kinds += [("gate", m) for m in range(16)]
    return kinds


class _Stop(Exception):
    pass


def build_program(stop=None):
    nc = bass.Bass("TRN2", target_bir_lowering=False)

    ckstate = {"n": 0}

    def ckpt(n):
        ckstate["n"] += 1
        if stop is not None and ckstate["n"] == stop:
            print("STOP at checkpoint #%d (label %d)" % (stop, n))
            raise _Stop()

    def din(name, shape):
        return nc.dram_tensor(name, list(shape), F32, kind="ExternalInput").ap()

    def dout(name, shape):
        return nc.dram_tensor(name, list(shape), F32, kind="ExternalOutput").ap()

    xp = din("xp", [NPB * 128, D])
    xs = din("xs", [128, D])
    cin = din("cin", [17, D])
    vecs = din("vecs", [208, 128])
    sinkrep = din("sinkrep", [128, 16])
    hmask_d = din("hmask", [128, 1])
    relt = din("relt", [32, 16])
    frev_d = din("frev", [33, 383])
    bdmask_d = din("bdmask", [128, 128])
    ident_d = din("identf", [128, 128])
    w_ada = din("w_ada", [DEPTH, 24, 128, 8 * WCOLS])
    w_in = din("w_in", [DEPTH, 25, 128, 8 * WCOLS])
    w_brc = din("w_brc", [DEPTH, 4, 128, 8 * WCOLS])
    w_bra = din("w_bra", [DEPTH, 4, 128, 8 * WCOLS])
    w_o = din("w_o", [DEPTH, 4, 128, 8 * WCOLS])
    w_ff1 = din("w_ff1", [DEPTH, 16, 128, 8 * WCOLS])
    w_ff2 = din("w_ff2", [DEPTH, 16, 128, 8 * WCOLS])
    sconv = din("sconv", [DEPTH, 32, D])
    ck = din("ck", [DEPTH, NSEQ, 128, 128])
    cv = din("cv", [DEPTH, NSEQ, 128, 128])

    yp = dout("yp", [16 * 128, D])
    ys = dout("ys", [128, D])
    convp = dout("convp", [DEPTH, 2, D])
    kp = dout("kp", [DEPTH, 128, 128])
    vp = dout("vp", [DEPTH, 128, 128])
    convs = dout("convs", [DEPTH, 32, D])
    ks = dout("ks", [DEPTH, NSEQ, 120, 128])
    vs = dout("vs", [DEPTH, NSEQ, 120, 128])
    ksn = dout("ksn", [DEPTH, 128, 128])
    vsn = dout("vsn", [DEPTH, 128, 128])

    S = Sched()
    es = ExitStack()

    def sb(name, shape, dt):
        return es.enter_context(nc.sbuf_tensor(name, list(shape), dt))

    TM = 512
    ident = sb("ident", [128, 128], F32)
    ones_m = sb("ones_m", [128, 128], BF16)
    ones64 = sb("ones64", [128, 64], BF16)
    identb = sb("identb", [128, 128], BF16)
    vecsT = sb("vecsT", [128, 208], F32)
    der = sb("der", [128, DEPTH, 6, 8, 17], F32)
    es_t = sb("es_t", [128, 16], F32)
    hmask = sb("hmask_t", [128, 1], F32)
    epsc = sb("epsc", [128, 1], F32)
    Etab = sb("Etab", [128, 2, 16, 128], BF16)
    Esn = sb("Esn", [128, 16, 128], BF16)
    diag = sb("diag", [128, 24, 128], BF16)
    xT = sb("xT", [128, NCH, TM], F32)
    xtm = sb("xtm", [128, D], F32)
    xo = sb("xo", [128, D], F32)
    s8 = sb("s8", [128, NCH, TM], BF16)
    rstd = sb("rstd", [128, TM], F32)
    rt = sb("rt", [128, TM], F32)
    tmpr = sb("tmpr", [128, 2, TM], F32)
    hb = sb("hb", [128, NCH, TM], BF16)
    big = sb("big", [128, 32, TM], BF16)
    u_p = sb("u_p", [128, NCH, 2 + TM], BF16)
    u_s = sb("u_s", [128, NCH, NSEQ, 10], BF16)
    us_f = sb("us_f", [128, NCH, 128], F32)
    utail = sb("utail", [128, NCH, 2], F32)
    ustl = sb("ustl", [128, NCH, 32], F32)
    ucarry = sb("ucarry", [128, DEPTH, NCH, 2], BF16)
    kT = sb("kT", [128, DEPTH, 128 + TM], BF16)
    kTs = sb("kTs", [128, 128], BF16)
    vtm = sb("vtm", [128, DEPTH, 5, 128], BF16)
    v_s = sb("v_s", [128, 128], BF16)
    kvout = sb("kvout", [128, 256], F32)
    expS = sb("expS", [128, 3, 512], F32)
    PT = sb("PT", [128, 2, 2, 16, 128], BF16)
    rD = sb("rD", [128, NCH, 128], F32)
    cstage = rD[:, 0:4, :]
    mixed = sb("mixed", [128, NCH, TM], F32)
    frev = mixed[0:33, 0, 0:383]
    relx = mixed[0:33, 1, 0:16]
    bdm = mixed[:, 2, 0:128]
    bada_tmp = mixed[:, 3:5, :].rearrange("p a b -> p (a b)")[:, 0:816].rearrange("p (m n) -> p m n", n=17)
    tstage = xo[0:32, :]
    wslot = sb("wslot", [128, NWSLOT, 8 * WCOLS], BF16)
    kcT = sb("kcT", [128, NSEQ, 128], BF16)
    vc = sb("vc", [128, NSEQ, 128], BF16)
    scT = sb("scT", [128, NCH, 17], BF16)
    pall = es.enter_context(nc.psum_tensor("pall", [128, 4096], F32))

    def bank(i, n=1):
        return pall[:, i * 512:(i + n) * 512]

    B_bank = [Buf("bank%d" % i, excl=True) for i in range(8)]
    B_xT = [Buf("xT%d" % c) for c in range(NCH)]
    B_s8 = [Buf("s8_%d" % c) for c in range(NCH)]
    B_hb = [Buf("hb%d" % c) for c in range(NCH)]
    B_big = [Buf("big%d" % c) for c in range(32)]
    B_mixed = [Buf("mixed%d" % c) for c in range(NCH)]
    B_up = [Buf("up%d" % c) for c in range(NCH)]
    B_us = [Buf("us%d" % c) for c in range(NCH)]
    B_usf = [Buf("usf%d" % c) for c in range(NCH)]
    B_utail = Buf("utail")
    B_ucarry = [Buf("ucarry%d" % l) for l in range(DEPTH)]
    B_kT = [Buf("kT%d" % l) for l in range(DEPTH)]
    B_kTs = Buf("kTs")
    B_vtm = [Buf("vtm%d" % l) for l in range(DEPTH)]
    B_vs = Buf("vs")
    B_kvout = Buf("kvout")
    B_exp = [Buf("exp0"), Buf("exp1"), Buf("exp2")]
    B_PT = [[Buf("PT00"), Buf("PT01")], [Buf("PT10"), Buf("PT11")]]
    B_rD = Buf("rD")
    B_cstage = B_rD
    B_rstd = Buf("rstd")
    B_rt = Buf("rt")
    B_tmp = [Buf("tmp0"), Buf("tmp1")]
    B_w = [Buf("w%d" % i) for i in range(NWSLOT)]
    B_xtm_h = [Buf("xtm0"), Buf("xtm1")]
    B_xo = Buf("xo")
    B_const = Buf("const")
    B_vecsT = Buf("vecsT")
    B_der = Buf("der")
    B_E = Buf("E")
    B_diag = Buf("diag")
    B_kcT = Buf("kcT")
    B_vc = Buf("vc")
    B_tstage = B_xo
    B_misc = Buf("misc")
    B_scT = Buf("scT")
    B_dram_out = Buf("dram_out")

    ds_w = [S.new_dsem() for _ in range(NWSLOT)]
    ds_xtm_h = [S.new_dsem(), S.new_dsem()]
    ds_xo = S.new_dsem()
    ds_const = S.new_dsem()
    ds_kvout = S.new_dsem()
    ds_cst = S.new_dsem()
    ds_tst = S.new_dsem()
    ds_vc = S.new_dsem()
    ds_d2d = S.new_dsem()

    state = {"w": 0, "ring": 0, "exp": 0, "tmp": 0, "lb": 0, "sb": 0}

    def ACT(out, in_, func, reads, writes, **kw):
        S.op("act", lambda e: e.activation(out=out, in_=in_, func=func, **kw), reads, writes)

    def TT(out, in0, in1, op, reads, writes, eng="dve"):
        S.op(eng, lambda e: e.tensor_tensor(out=out, in0=in0, in1=in1, op=op), reads, writes)

    def TS(out, in0, s1, s2, op0, op1, reads, writes, eng="dve"):
        if s2 is None:
            S.op(eng, lambda e: e.tensor_scalar(out=out, in0=in0, scalar1=s1, scalar2=None, op0=op0), reads, writes)
        else:
            S.op(eng, lambda e: e.tensor_scalar(out=out, in0=in0, scalar1=s1, scalar2=s2, op0=op0, op1=op1), reads, writes)

    def CP(out, in_, reads, writes, eng="dve"):
        S.op(eng, lambda e: e.tensor_copy(out=out, in_=in_), reads, writes)

    def MM(out, lhsT, rhs, start, stop, reads, writes, skip=False):
        if skip:
            S.op("pe", lambda e: e.matmul(out, lhsT=lhsT, rhs=rhs, start=start, stop=stop, skip_group_check=True), reads, writes)
        else:
            S.op("pe", lambda e: e.matmul(out, lhsT=lhsT, rhs=rhs, start=start, stop=stop), reads, writes)

    def TR(out, in_, idt, reads, writes):
        S.op("pe", lambda e: e.transpose(out, in_, idt), reads, writes)

    def DMA(eng, out, in_, reads, writes, dsem, **kw):
        S.op(eng, lambda e: e.dma_start(out=out, in_=in_, **kw), reads, writes, dma=True, dsem=dsem)

    def next_ring():
        r = state["ring"] % 3
        state["ring"] += 1
        return r

    WSEQ = []
    for l_ in range(DEPTH):
        for sl_ in range(24):
            WSEQ.append((w_ada[l_, sl_], 8, WCOLS))
    for _t in range(5):
        for l_ in range(DEPTH):
            for sl_ in range(25):
                WSEQ.append((w_in[l_, sl_], 8, WCOLS))
            for wsrc in (w_brc, w_bra, w_o):
                for sl_ in range(4):
                    WSEQ.append((wsrc[l_, sl_], 8, WCOLS))
            for sl_ in range(16):
                WSEQ.append((w_ff1[l_, sl_], 8, WCOLS))
            for m_ in range(NCH):
                for kh_ in range(2):
                    WSEQ.append((w_ff2[l_, m_ * 2 + kh_], 16, 128))
    LOOKAHEAD = NWSLOT - 2
    state["wi"] = 0

    def _issue_w(k):
        src_ap, kch, ncols = WSEQ[k]
        sl = k % NWSLOT
        DMA("pool", wslot[:, sl, 0:kch * ncols], src_ap, [], [B_w[sl]], ds_w[sl])

    def load_w(src_ap, kch, ncols):
        k = state["w"]
        state["w"] += 1
        assert k < len(WSEQ) and str(WSEQ[k][0]) == str(src_ap) and WSEQ[k][1] == kch, ("weight order mismatch", k)
        while state["wi"] < len(WSEQ) and state["wi"] <= k + LOOKAHEAD:
            _issue_w(state["wi"])
            state["wi"] += 1
        sl = k % NWSLOT
        return sl, wslot[:, sl, 0:kch * ncols].rearrange("p (c e) -> p c e", c=kch)

    try:
        DMA("sp", ident[:], ident_d, [], [B_const], ds_const)
        DMA("sp", hmask[:], hmask_d, [], [B_const], ds_const)
        DMA("sp", es_t[:], sinkrep, [], [B_const], ds_const)
        DMA("sp", frev[:], frev_d, [], [B_const, B_mixed[0]], ds_const)
        DMA("sp", bdm[:], bdmask_d, [], [B_const, B_mixed[2]], ds_const)
        S.op("dve", lambda e: e.memset(ones_m[:], 1.0 / 1024.0), [], [B_const])
        S.op("dve", lambda e: e.memset(ones64[:], 1.0), [], [B_const])
        S.op("dve", lambda e: e.memset(epsc[:], EPS), [], [B_const])
        S.op("dve", lambda e: e.memset(relx[:], -1e30), [], [B_misc, B_mixed[1]])
        DMA("sp", relx[0:32, :], relt, [], [B_misc, B_mixed[1]], ds_const)
        CP(identb[:], ident[:], [B_const], [B_const])
        ACT(es_t[:], es_t[:], AF.Exp, [B_const], [B_const])

        for hlf in range(2):
            DMA("sp", xtm[0:104, 0:128], vecs[hlf * 104:(hlf + 1) * 104, :], [], B_xtm_h, ds_xtm_h[0])
            TR(bank(3)[:, 0:104], xtm[0:104, 0:128], ident[0:104, 0:104], B_xtm_h + [B_const], [B_bank[3]])
            CP(vecsT[:, hlf * 104:(hlf + 1) * 104], bank(3)[:, 0:104], [B_bank[3]], [B_vecsT])

        DMA("sp", xtm[0:17, :], cin, [], B_xtm_h, ds_xtm_h[0])
        for c in range(NCH):
            TR(bank(3)[:, c * 17:(c + 1) * 17], xtm[0:17, c * 128:(c + 1) * 128], ident[0:17, 0:17], B_xtm_h + [B_const], [B_bank[3]])
        ACT(scT[:].rearrange("p c n -> p (c n)"), bank(3)[:, 0:NCH * 17], AF.Silu, [B_bank[3]], [B_scT])

        ckpt(0)
        tiles = [(0, 4, False), (4, 4, False), (8, 4, False), (12, 4, False), (16, 2, True)]
        def tile_srcs(ti_):
            b0_, nbp_, has_s_ = tiles[ti_]
            return [xp[(b0_ + j) * 128:(b0_ + j + 1) * 128, :] for j in range(nbp_)] + ([xs] if has_s_ else [])

        xpre = set()

        def sample_prep(l):
            for g4 in range(4):
                DMA("sp", cstage[:], ck[l, g4 * 4:(g4 + 1) * 4].rearrange("s p d -> p s d"), [], [B_cstage], ds_cst)
                for sl in range(4):
                    TR(bank(3)[:, sl * 128:(sl + 1) * 128], cstage[:, sl, :], ident[:], [B_cstage, B_const], [B_bank[3]])
                CP(kcT[:, g4 * 4:(g4 + 1) * 4, :], bank(3).rearrange("p (s t) -> p s t", s=4), [B_bank[3]], [B_kcT])
            DMA("pool", vc[:], cv[l].rearrange("s p d -> p s d"), [], [B_vc], ds_vc)
            DMA("sp", tstage[:], sconv[l], [], [B_tstage], ds_tst)
            for c in range(NCH):
                TR(bank(3)[:, c * 32:(c + 1) * 32], tstage[:, c * 128:(c + 1) * 128], ident[0:32, 0:32], [B_tstage, B_const], [B_bank[3]])
            CP(u_s[:, :, :, 0:2], bank(3)[:, 0:256].rearrange("p (c s r) -> p c s r", c=NCH, r=2), [B_bank[3]], B_us)

        def stage_of(j, hf):
            if j % 2 == 0:
                return xtm, [B_xtm_h[hf]], ds_xtm_h[hf]
            return xo, [B_xo], ds_xo

        def xload(tix_):
            srcs = tile_srcs(tix_)
            for j, src in enumerate(srcs):
                for hf in range(2):
                    st, sbufs, sds = stage_of(j, hf)
                    if (tix_, j) not in xpre:
                        DMA("sp", st[:, hf * 512:(hf + 1) * 512], src[:, hf * 512:(hf + 1) * 512], [], sbufs, sds)
                    rb = state["lb"] % 4
                    state["lb"] += 1
                    for cc in range(4):
                        c = hf * 4 + cc
                        TR(bank(rb)[:, cc * 128:(cc + 1) * 128], st[:, c * 128:(c + 1) * 128], ident[:], sbufs + [B_const], [B_bank[rb]])
                    dstv = xT[:, hf * 4:(hf + 1) * 4, j * 128:(j + 1) * 128]
                    if hf == 0:
                        CP(dstv, bank(rb).rearrange("p (c t) -> p c t", c=4), [B_bank[rb]], B_xT[hf * 4:(hf + 1) * 4])
                    else:
                        ACT(dstv, bank(rb).rearrange("p (c t) -> p c t", c=4), AF.Copy, [B_bank[rb]], B_xT[hf * 4:(hf + 1) * 4])

        xload(0)
        sample_prep(0)

        pmod = pall[:, 4 * 512:7 * 512].rearrange("p (m n) -> p m n", n=32)

        def mod_slot(l, sl):
            s, wv = load_w(w_ada[l, sl], 8, WCOLS)
            for mt in range(2):
                m = sl * 2 + mt
                for c in range(NCH):
                    MM(pmod[:, m, 0:17], wv[:, c, mt * 128:(mt + 1) * 128], scT[:, c, :], c == 0, c == NCH - 1,
                       [B_w[s], B_scT], [B_bank[4 + m // 16]])

        def mod_finish(l):
            base = l * 104
            bada = vecsT[:, base + 56:base + 104]
            TT(bada_tmp[:], pmod[:, :, 0:17], bada.unsqueeze(2).to_broadcast([128, 48, 17]), ALU.add,
               [B_bank[4], B_bank[5], B_bank[6], B_vecsT], [B_misc, B_mixed[3], B_mixed[4]])
            for k, (gofs, scofs, mode) in enumerate([(0, 8, "a"), (None, 0, "b"), (8, 16, "g"), (16, 32, "a"), (None, 24, "b"), (24, 40, "g")]):
                dst = der[:, l, k]
                src = bada_tmp[:, scofs:scofs + 8, :]
                if mode == "b":
                    CP(dst, src, [B_misc, B_mixed[3], B_mixed[4]], [B_der])
                else:
                    gvec = vecsT[:, base + gofs:base + gofs + 8].unsqueeze(2).to_broadcast([128, 8, 17])
                    if mode == "a":
                        TS(dst, src, 1.0, None, ALU.add, None, [B_misc, B_mixed[3], B_mixed[4]], [B_der])
                        TT(dst, dst, gvec, ALU.mult, [B_der, B_vecsT], [B_der])
                    else:
                        TT(dst, src, gvec, ALU.mult, [B_misc, B_mixed[3], B_mixed[4], B_vecsT], [B_der])

        frevb0 = s8[0:33, 0, 0:382]
        frevb1 = s8[0:33, 1, 0:382]
        relh = s8[0:33, 2, 0:16]
        rell = s8[0:33, 2, 16:32]
        CP(frevb0, frev[:, 0:382], [B_const, B_mixed[0]], [B_s8[0]])
        CP(frevb1, frev[:, 1:383], [B_const, B_mixed[0]], [B_s8[1]])
        CP(relh, relx[:], [B_misc, B_mixed[1]], [B_s8[2]])
        TT(rell, relx[:], relh, ALU.subtract, [B_misc, B_mixed[1], B_s8[2]], [B_s8[2]])
        e_jobs = []
        for kb, off in ((0, 127), (1, 255)):
            for i in range(128):
                e_jobs.append((kb, off, i))

        def e_build(n):
            for _ in range(n):
                if not e_jobs:
                    return
                kb, off, i = e_jobs.pop(0)
                pE = pall[:, 0:2048].rearrange("p (i h) -> p i h", h=16)
                o = off - i
                lw = frevb0[:, o:o + 128] if o % 2 == 0 else frevb1[:, o - 1:o - 1 + 128]
                rds = [B_s8[0], B_s8[1], B_s8[2]]
                MM(pE[:, i, :], lw, relh, True, False, rds, [B_bank[i // 32]])
                MM(pE[:, i, :], lw, rell, False, True, rds, [B_bank[i // 32]])
                if i == 127:
                    ACT(Etab[:, kb].rearrange("p h i -> p i h"), pE, AF.Exp, [B_bank[j] for j in range(4)], [B_E])

        for sl in range(24):
            mod_slot(0, sl)
            e_build(6)
        mod_finish(0)
        for sl in range(24):
            mod_slot(1, sl)
            e_build(6)
        e_build(256)
        mod_finish(1)
        TT(Esn[:], Etab[:, 1], bdm[:].unsqueeze(1).to_broadcast([128, 16, 128]), ALU.mult, [B_E, B_const, B_mixed[2]], [B_E])

        ckpt(2)
        S.op("dve", lambda e: e.memset(ucarry[:], 0.0), [], B_ucarry)
        S.op("dve", lambda e: e.memset(kT[:, :, 0:128], 0.0), [], B_kT)
        S.op("dve", lambda e: e.memset(vtm[:, :, 0, :], 0.0), [], B_vtm)

        LO = {"v": 0}
        def norm_stats(T):
            for c in range(NCH):
                MM(bank(3)[:, LO["v"]:T], ones_m[:], s8[:, c, LO["v"]:T], c == 0, c == NCH - 1, [B_const, B_s8[c]], [B_bank[3]])
            ACT(rt[:, LO["v"]:T], bank(3)[:, LO["v"]:T], AF.Ln, [B_bank[3], B_const], [B_rt], bias=epsc[:, 0:1])
            ACT(rstd[:, LO["v"]:T], rt[:, LO["v"]:T], AF.Exp, [B_rt], [B_rstd], scale=-0.5)

        def pre_norm(l, T, Tp, ka, kbb, squares_done=False):
            if not squares_done:
                for c in range(NCH):
                    ACT(s8[:, c, LO["v"]:T], xT[:, c, LO["v"]:T], AF.Square, [B_xT[c]], [B_s8[c]])
            norm_stats(T)
            for c in range(NCH):
                ti = state["tmp"] % 2
                state["tmp"] += 1
                tv = tmpr[:, ti, LO["v"]:T]
                eng = "pool" if c in (1, 4, 6) else "dve"
                TT(tv, xT[:, c, LO["v"]:T], rstd[:, LO["v"]:T], ALU.mult, [B_xT[c], B_rstd], [B_tmp[ti]], eng=eng)
                if Tp > 0:
                    ACT(hb[:, c, LO["v"]:Tp], tmpr[:, ti, LO["v"]:Tp], AF.Identity, [B_tmp[ti], B_der], [B_hb[c]],
                        scale=der[:, l, ka, c, 0:1], bias=der[:, l, kbb, c, 0:1])
                if T > Tp:
                    sv = tmpr[:, ti, Tp:T].rearrange("p (s i) -> p s i", i=8)
                    TT(sv, sv, der[:, l, ka, c, 1:17].unsqueeze(2).to_broadcast([128, 16, 8]), ALU.mult,
                       [B_tmp[ti], B_der], [B_tmp[ti]])
                    TT(hb[:, c, Tp:T].rearrange("p (s i) -> p s i", i=8), sv,
                       der[:, l, kbb, c, 1:17].unsqueeze(2).to_broadcast([128, 16, 8]), ALU.add,
                       [B_tmp[ti], B_der], [B_hb[c]])

        def post_norm_residual(l, T, Tp, kg, squares_after=True, into_mixed=False):
            norm_stats(T)
            for c in range(NCH):
                eng = "pool" if c in (1, 4, 6) else "dve"
                TT(mixed[:, c, LO["v"]:T], mixed[:, c, LO["v"]:T], rstd[:, LO["v"]:T], ALU.mult, [B_mixed[c], B_rstd], [B_mixed[c]], eng=eng)
                if into_mixed:
                    TT(mixed[:, c, LO["v"]:T], mixed[:, c, LO["v"]:T], xT[:, c, LO["v"]:T], ALU.add, [B_xT[c], B_mixed[c]], [B_mixed[c]], eng=eng)
                    continue
                TT(xT[:, c, LO["v"]:T], xT[:, c, LO["v"]:T], mixed[:, c, LO["v"]:T], ALU.add, [B_xT[c], B_mixed[c]], [B_xT[c]], eng=eng)
                if squares_after:
                    ACT(s8[:, c, LO["v"]:T], xT[:, c, LO["v"]:T], AF.Square, [B_xT[c]], [B_s8[c]])

        def evac_scaled(l, m, rb, T, Tp, kg):
            pv = bank(rb)
            ACT(s8[:, m, LO["v"]:T], pv[:, LO["v"]:T], AF.Square, [B_bank[rb]], [B_s8[m]])
            if Tp > 0:
                ACT(mixed[:, m, LO["v"]:Tp], pv[:, LO["v"]:Tp], AF.Identity, [B_bank[rb], B_der], [B_mixed[m]],
                    scale=der[:, l, kg, m, 0:1])
            if T > Tp:
                TT(mixed[:, m, Tp:T].rearrange("p (s i) -> p s i", i=8), pv[:, Tp:T].rearrange("p (s i) -> p s i", i=8),
                   der[:, l, kg, m, 1:17].unsqueeze(2).to_broadcast([128, 16, 8]), ALU.mult,
                   [B_bank[rb], B_der], [B_mixed[m]])

        def fm_tile(wv, mt, s, rhs_fn, rhs_bufs, T, nk=NCH, first=True, last=True, rb=None, kofs=0):
            if rb is None:
                rb = next_ring()
            for c in range(nk):
                MM(bank(rb)[:, LO["v"]:T], wv[:, c, mt * 128:(mt + 1) * 128], rhs_fn(kofs + c), first and c == 0, last and c == nk - 1,
                   [B_w[s]] + rhs_bufs(kofs + c), [B_bank[rb]])
            return rb

        def fm_group(wvs, ss, mts, rhs_fn, rhs_bufs, T, banks):
            for c in range(NCH):
                for g in range(len(mts)):
                    MM(bank(banks[g])[:, LO["v"]:T], wvs[g][:, c, mts[g] * 128:(mts[g] + 1) * 128], rhs_fn(c), c == 0, c == NCH - 1,
                       [B_w[ss[g]]] + rhs_bufs(c), [B_bank[banks[g]]])

        pending = []

        def make_store(j, is_s, gblk):
            def fn():
                for hf in range(2):
                    rb = 4 + state["sb"] % 4
                    state["sb"] += 1
                    for cc in range(4):
                        c = hf * 4 + cc
                        TR(bank(rb)[:, cc * 128:(cc + 1) * 128], mixed[:, c, j * 128:(j + 1) * 128], ident[:], [B_mixed[c], B_const],
                           [B_bank[rb]])
                    if hf == 0:
                        CP(xo[:, 0:512], bank(rb), [B_bank[rb]], [B_xo])
                    else:
                        ACT(xo[:, 512:1024], bank(rb), AF.Copy, [B_bank[rb]], [B_xo])
                dst = ys if is_s else yp[(gblk - 2) * 128:(gblk - 1) * 128, :]
                DMA("sp", dst, xo[:], [B_xo], [B_dram_out], ds_xo)
            return fn

        for tix, (b0, nbp, has_s) in enumerate(tiles):
            Tp = nbp * 128
            T = Tp + (128 if has_s else 0)
            if tix > 0:
                xload(tix)
            ckpt(3)
            for l in range(DEPTH):
                base = l * 104
                if b0 == 0:
                    lo_u, lo_m = (0, 128) if l == 0 else (128, 256)
                else:
                    lo_u, lo_m = 0, 0
                LO["v"] = lo_u
                if l == 1 and tix + 1 < len(tiles):
                    for jn in range(2):
                        nsrc = tile_srcs(tix + 1)[jn]
                        for hf in range(2):
                            st, sbufs, sds = stage_of(jn, hf)
                            DMA("sp", st[:, hf * 512:(hf + 1) * 512], nsrc[:, hf * 512:(hf + 1) * 512], [], sbufs, sds)
                        xpre.add((tix + 1, jn))
                if tix == 2:
                    DMA("sp", ks[l], ck[l][:, 8:128, :], [], [B_dram_out], ds_d2d)
                    DMA("sp", vs[l], cv[l][:, 8:128, :], [], [B_dram_out], ds_d2d)
                CP(u_p[:, :, 0:2], ucarry[:, l], [B_ucarry[l]], B_up)
                for j in range(3):
                    for c in range(NCH):
                        TS(diag[:, j * 8 + c, :], identb[:], vecsT[:, base + 32 + j * 8 + c:base + 33 + j * 8 + c], None, ALU.mult, None,
                           [B_const, B_vecsT], [B_diag])

                pre_norm(l, T, Tp, 0, 1, squares_done=(l > 0))

                ckpt(4)
                hfn = lambda c: hb[:, c, LO["v"]:T]
                hbufs = lambda c: [B_hb[c]]
                kinds = _proj_tiles()
                cgst = {}

                def proj_evac(kind, m, rb):
                    pv = bank(rb)
                    if kind == "cg":
                        ti = state["tmp"] % 2
                        state["tmp"] += 1
                        cgst[m] = ti
                        ACT(tmpr[:, ti, LO["v"]:T], pv[:, LO["v"]:T], AF.Copy, [B_bank[rb]], [B_tmp[ti]])
                    elif kind == "xc":
                        ti = cgst[m]
                        TT(u_p[:, m, 2 + LO["v"]:2 + Tp], pv[:, LO["v"]:Tp], tmpr[:, ti, LO["v"]:Tp], ALU.mult, [B_bank[rb], B_tmp[ti]], [B_up[m]])
                        if b0 + nbp == NPB:
                            TT(utail[:, m, :], pv[:, Tp - 2:Tp], tmpr[:, ti, Tp - 2:Tp], ALU.mult, [B_bank[rb], B_tmp[ti]], [B_utail])
                        if has_s:
                            TT(us_f[:, m, :], pv[:, Tp:T], tmpr[:, ti, Tp:T], ALU.mult, [B_bank[rb], B_tmp[ti]], [B_usf[m]])
                            CP(u_s[:, m, :, 2:10], us_f[:, m, :].rearrange("p (s i) -> p s i", i=8), [B_usf[m]], [B_us[m]])
                            CP(ustl[:, m, :].rearrange("p (s r) -> p s r", r=2),
                               us_f[:, m, :].rearrange("p (s i) -> p s i", i=8)[:, :, 6:8], [B_usf[m]], [B_utail])
                    elif kind == "bg":
                        ACT(big[:, 16 + m, LO["v"]:T], pv[:, LO["v"]:T], AF.Copy, [B_bank[rb]], [B_big[16 + m]])
                    elif kind == "q":
                        ACT(big[:, 24 + m, LO["v"]:T], pv[:, LO["v"]:T], AF.Copy, [B_bank[rb]], [B_big[24 + m]], scale=0.125)
                    else:
                        ACT(big[:, m, LO["v"]:T], pv[:, LO["v"]:T], AF.Sigmoid, [B_bank[rb]], [B_big[m]])

                LO["v"] = lo_u
                s0, wv0 = load_w(w_in[l, 0], 8, WCOLS)
                s1, wv1 = load_w(w_in[l, 1], 8, WCOLS)
                hbanks = [next_ring(), next_ring(), next_ring(), 4]
                fm_group([wv0, wv0, wv1, wv1], [s0, s0, s1, s1], [0, 1, 0, 1], hfn, hbufs, T, hbanks)
                for g in range(4):
                    proj_evac(kinds[g][0], kinds[g][1], hbanks[g])
                for sl in range(2, 24):
                    s, wv = load_w(w_in[l, sl], 8, WCOLS)
                    if pending and sl in (3, 8, 13, 18):
                        pending.pop(0)()
                    for mt in range(2):
                        kind, m = kinds[sl * 2 + mt]
                        LO["v"] = lo_u if kind in ("cg", "xc") else lo_m
                        rb = fm_tile(wv, mt, s, hfn, hbufs, T)
                        proj_evac(kind, m, rb)
                ckpt(41)
                while pending:
                    pending.pop(0)()
                s, wv = load_w(w_in[l, 24], 8, WCOLS)
                LO["v"] = lo_u
                rb = fm_tile(wv, 0, s, hfn, hbufs, T)
                ACT(kT[:, l, 128 + LO["v"]:128 + Tp], bank(rb)[:, LO["v"]:Tp], AF.Copy, [B_bank[rb]], [B_kT[l]])
                if has_s:
                    ACT(kTs[:], bank(rb)[:, Tp:T], AF.Copy, [B_bank[rb]], [B_kTs])
                ckpt(42)
                nblk = nbp + (1 if has_s else 0)
                for j in range(nblk):
                    is_s = j >= nbp
                    if j * 128 < lo_u:
                        continue
                    for c in range(NCH):
                        MM(bank(3)[:, 0:256], hb[:, c, j * 128:(j + 1) * 128], wv[:, c, 0:256], c == 0, c == NCH - 1,
                           [B_hb[c], B_w[s]], [B_bank[3]])
                    if is_s:
                        CP(v_s[:], bank(3)[:, 128:256], [B_bank[3]], [B_vs])
                    else:
                        CP(vtm[:, l, 1 + j, :], bank(3)[:, 128:256], [B_bank[3]], [B_vtm[l]])
                    if is_s or (b0 + j == NPB - 1):
                        ACT(kvout[:], bank(3)[:, 0:256], AF.Copy, [B_bank[3]], [B_kvout])
                        if is_s:
                            DMA("sp", ksn[l], kvout[:, 0:128], [B_kvout], [B_dram_out], ds_kvout)
                            DMA("sp", vsn[l], kvout[:, 128:256], [B_kvout], [B_dram_out], ds_kvout)
                        else:
                            DMA("sp", kp[l], kvout[:, 0:128], [B_kvout], [B_dram_out], ds_kvout)
                            DMA("sp", vp[l], kvout[:, 128:256], [B_kvout], [B_dram_out], ds_kvout)

                ckpt(5)
                LO["v"] = lo_m

                def conv_stage():
                  if True:
                    if b0 == 0:
                        TS(u_p[:, :, 2 + 254:2 + 256], u_p[:, :, 2 + 254:2 + 256], hmask[:, 0:1], None, ALU.mult, None,
                           B_up + [B_const], B_up)
                    for m in range(NCH):
                        rb = next_ring()
                        for j in range(3):
                            MM(bank(rb)[:, LO["v"]:Tp], diag[:, j * 8 + m, :], u_p[:, m, LO["v"] + j:j + Tp], j == 0, j == 2, [B_diag, B_up[m]], [B_bank[rb]])
                        if has_s:
                            for j in range(3):
                                MM(bank(rb)[:, Tp:T].rearrange("p (s i) -> p s i", i=8), diag[:, j * 8 + m, :], u_s[:, m, :, j:j + 8],
                                   j == 0, j == 2, [B_diag, B_us[m]], [B_bank[rb]])
                        TT(s8[:, m, LO["v"]:T], bank(rb)[:, LO["v"]:T], big[:, 16 + m, LO["v"]:T], ALU.mult, [B_bank[rb], B_big[16 + m]], [B_s8[m]])
                    CP(ucarry[:, l], u_p[:, :, Tp:Tp + 2], B_up, [B_ucarry[l]])
                ckpt(6)
                pD = pall[:, 6 * 512:8 * 512].rearrange("p (c q) -> p c q", c=8)
                pO = pall[:, 4 * 512:6 * 512].rearrange("p (c q) -> p c q", c=8)

                def att_A(j):
                    is_s = j >= nbp
                    q0 = j * 128
                    pb = j % 2
                    for kb in range(2):
                        for g in range(2):
                            for quad in range(2):
                                rb = next_ring()
                                hs = 8 * g + 4 * quad
                                qv = big[g * 64:(g + 1) * 64, 24 + 4 * quad:24 + 4 * quad + 4, q0:q0 + 128]
                                qb = B_big[24 + 4 * quad:24 + 4 * quad + 4]
                                pv = bank(rb)
                                if not is_s:
                                    kcols = slice(j * 128 + kb * 128, j * 128 + kb * 128 + 128)
                                    MM(pv, kT[g * 64:(g + 1) * 64, l, kcols], qv, True, True, [B_kT[l]] + qb, [B_bank[rb]])
                                    ein = Etab[:, kb, hs:hs + 4, :]
                                elif kb == 1:
                                    MM(pv, kTs[g * 64:(g + 1) * 64, :], qv, True, True, [B_kTs] + qb, [B_bank[rb]])
                                    ein = Esn[:, hs:hs + 4, :]
                                else:
                                    for sq in range(NSEQ):
                                        MM(pv[:, sq * 32:(sq + 1) * 32], kcT[g * 64:(g + 1) * 64, sq, :],
                                           big[g * 64:(g + 1) * 64, 24 + 4 * quad:24 + 4 * quad + 4, q0 + sq * 8:q0 + sq * 8 + 8],
                                           True, True, [B_kcT] + qb, [B_bank[rb]], skip=True)
                                    ein = Etab[:, 0, hs:hs + 4, 0:8].unsqueeze(1).to_broadcast([128, 16, 4, 8])
                                xi = state["exp"] % 3
                                state["exp"] += 1
                                ACT(expS[:, xi, :], pv, AF.Exp, [B_bank[rb]], [B_exp[xi]])
                                meng = "pool" if quad == 1 else "dve"
                                if is_s and kb == 0:
                                    TT(PT[:, pb, kb, hs:hs + 4, :].rearrange("p h (s i) -> p s h i", i=8),
                                       expS[:, xi, :].rearrange("p (s h i) -> p s h i", h=4, i=8), ein, ALU.mult,
                                       [B_exp[xi], B_E], [B_PT[pb][kb]], eng=meng)
                                else:
                                    TT(PT[:, pb, kb, hs:hs + 4, :], expS[:, xi, :].rearrange("p (h q) -> p h q", h=4), ein, ALU.mult,
                                       [B_exp[xi], B_E], [B_PT[pb][kb]], eng=meng)
                    if (not is_s) and (b0 + j) == 2:
                        TS(PT[:, pb, 0], PT[:, pb, 0], hmask[:, 0:1], None, ALU.mult, None, [B_PT[pb][0], B_const], [B_PT[pb][0]])

                def att_B(j):
                    is_s = j >= nbp
                    q0 = j * 128
                    pb = j % 2
                    for g in range(2):
                        for quad in range(2):
                            hs = 8 * g + 4 * quad
                            for kb in range(2):
                                MM(pD[g * 64:(g + 1) * 64, 4 * quad:4 * quad + 4, :], ones64[:], PT[:, pb, kb, hs:hs + 4, :], kb == 0, kb == 1,
                                   [B_const, B_PT[pb][kb]], [B_bank[6 + quad]])
                    if not is_s:
                        for g in range(2):
                            for quad in range(2):
                                hs = 8 * g + 4 * quad
                                for kb in range(2):
                                    MM(pO[g * 64:(g + 1) * 64, 4 * quad:4 * quad + 4, :], vtm[:, l, j + kb, g * 64:(g + 1) * 64],
                                       PT[:, pb, kb, hs:hs + 4, :], kb == 0, kb == 1, [B_vtm[l], B_PT[pb][kb]], [B_bank[4 + quad]])
                    else:
                        for g in range(2):
                            for quad in range(2):
                                hs = 8 * g + 4 * quad
                                MM(pO[g * 64:(g + 1) * 64, 4 * quad:4 * quad + 4, :], v_s[:, g * 64:(g + 1) * 64],
                                   PT[:, pb, 1, hs:hs + 4, :], True, True, [B_vs, B_PT[pb][1]], [B_bank[4 + quad]])
                        for quad in range(2):
                            for g in range(2):
                                for sq in range(NSEQ):
                                    MM(bank(quad)[g * 64:(g + 1) * 64, sq * 32:(sq + 1) * 32],
                                       vc[:, sq, g * 64:(g + 1) * 64],
                                       PT[:, pb, 0, 8 * g + 4 * quad:8 * g + 4 * quad + 4, sq * 8:(sq + 1) * 8],
                                       True, True, [B_vc, B_PT[pb][0]], [B_bank[quad]], skip=True)
                            ACT(mixed[:, 4 * quad:4 * quad + 4, 0:128].rearrange("p c (s i) -> p s c i", i=8),
                                bank(quad).rearrange("p (s c i) -> p s c i", c=4, i=8), AF.Copy, [B_bank[quad]],
                                B_mixed[4 * quad:4 * quad + 4])
                    TT(rD[:], pD, es_t[:, l * 8:(l + 1) * 8].unsqueeze(2).to_broadcast([128, 8, 128]), ALU.add,
                       [B_bank[6], B_bank[7], B_const], [B_rD])
                    ACT(rD[:], rD[:], AF.Ln, [B_rD], [B_rD])
                    ACT(rD[:], rD[:], AF.Exp, [B_rD], [B_rD], scale=-1.0)
                    if not is_s:
                        TT(hb[:, :, q0:q0 + 128], pO, rD[:], ALU.mult, [B_bank[4], B_bank[5], B_rD], B_hb)
                    else:
                        TT(mixed[:, :, 0:128], mixed[:, :, 0:128], pO, ALU.add, B_mixed + [B_bank[4], B_bank[5]], B_mixed)
                        TT(hb[:, :, q0:q0 + 128], mixed[:, :, 0:128], rD[:], ALU.mult, B_mixed + [B_rD], B_hb)

                brc_state = {"m": 0, "wv": None, "s": None}

                def brc_tiles(n):
                    for _ in range(n):
                        m = brc_state["m"]
                        if m >= NCH:
                            return
                        if m % 2 == 0:
                            brc_state["s"], brc_state["wv"] = load_w(w_brc[l, m // 2], 8, WCOLS)
                        rb = fm_tile(brc_state["wv"], m % 2, brc_state["s"], lambda c: s8[:, c, LO["v"]:T], lambda c: [B_s8[c]], T)
                        TT(big[:, 16 + m, LO["v"]:T], bank(rb)[:, LO["v"]:T], big[:, m, LO["v"]:T], ALU.mult, [B_bank[rb], B_big[m]], [B_big[16 + m]])
                        brc_state["m"] = m + 1

                ablk = [j for j in range(nblk) if j * 128 >= lo_m]
                per = NCH // (len(ablk) + 1)
                att_A(ablk[0])
                conv_stage()
                for ii, j in enumerate(ablk):
                    if ii + 1 < len(ablk):
                        att_A(ablk[ii + 1])
                    brc_tiles(per)
                    att_B(j)
                brc_tiles(NCH)
                if not has_s:
                    CP(kT[:, l, 0:128], kT[:, l, Tp:Tp + 128], [B_kT[l]], [B_kT[l]])
                    CP(vtm[:, l, 0, :], vtm[:, l, nbp, :], [B_vtm[l]], [B_vtm[l]])

                ckpt(7)
                if has_s and l == 0:
                    sample_prep(1)
                for half in range(4):
                    s, wv = load_w(w_bra[l, half], 8, WCOLS)
                    for mt in range(2):
                        m = half * 2 + mt
                        rb = fm_tile(wv, mt, s, lambda c: hb[:, c, LO["v"]:T], lambda c: [B_hb[c]], T)
                        TT(big[:, 24 + m, LO["v"]:T], bank(rb)[:, LO["v"]:T], big[:, 8 + m, LO["v"]:T], ALU.mult, [B_bank[rb], B_big[8 + m]], [B_big[24 + m]])
                        TT(big[:, 16 + m, LO["v"]:T], big[:, 16 + m, LO["v"]:T], big[:, 24 + m, LO["v"]:T], ALU.add, [B_big[16 + m], B_big[24 + m]],
                           [B_big[16 + m]], eng="pool")
                for half in range(4):
                    s, wv = load_w(w_o[l, half], 8, WCOLS)
                    for mt in range(2):
                        m = half * 2 + mt
                        rb = fm_tile(wv, mt, s, lambda c: big[:, 16 + c, LO["v"]:T], lambda c: [B_big[16 + c]], T)
                        evac_scaled(l, m, rb, T, Tp, 2)
                post_norm_residual(l, T, Tp, 2, squares_after=True)

                ckpt(8)
                pre_norm(l, T, Tp, 3, 4, squares_done=True)
                def ff1_evac(jx, rb):
                    ACT(big[:, jx, LO["v"]:T], bank(rb)[:, LO["v"]:T], AF.Relu, [B_bank[rb]], [B_big[jx]])
                    TT(big[:, jx, LO["v"]:T], big[:, jx, LO["v"]:T], big[:, jx, LO["v"]:T], ALU.mult, [B_big[jx]], [B_big[jx]],
                       eng=("pool" if jx % 2 else "dve"))

                s0, wv0 = load_w(w_ff1[l, 0], 8, WCOLS)
                s1, wv1 = load_w(w_ff1[l, 1], 8, WCOLS)
                hbanks = [next_ring(), next_ring(), next_ring(), 4]
                fm_group([wv0, wv0, wv1, wv1], [s0, s0, s1, s1], [0, 1, 0, 1], lambda c: hb[:, c, LO["v"]:T], lambda c: [B_hb[c]], T, hbanks)
                for g in range(4):
                    ff1_evac(g, hbanks[g])
                for sl in range(2, 16):
                    s, wv = load_w(w_ff1[l, sl], 8, WCOLS)
                    for mt in range(2):
                        jx = sl * 2 + mt
                        rb = fm_tile(wv, mt, s, lambda c: hb[:, c, LO["v"]:T], lambda c: [B_hb[c]], T)
                        ff1_evac(jx, rb)
                for m in range(NCH):
                    rb = next_ring()
                    for kh in range(2):
                        s, wv = load_w(w_ff2[l, m * 2 + kh], 16, 128)
                        fm_tile(wv, 0, s, lambda c: big[:, c, LO["v"]:T], lambda c: [B_big[c]], T, nk=16, first=(kh == 0), last=(kh == 1),
                                rb=rb, kofs=kh * 16)
                    evac_scaled(l, m, rb, T, Tp, 5)
                post_norm_residual(l, T, Tp, 5, squares_after=(l + 1 < DEPTH), into_mixed=(l + 1 == DEPTH))

                ckpt(9)
                if b0 + nbp == NPB:
                    for hf in range(2):
                        for cc in range(4):
                            c = hf * 4 + cc
                            TR(bank(3)[0:2, cc * 128:(cc + 1) * 128], utail[:, c, :], ident[:], [B_utail, B_const], [B_bank[3]])
                        CP(tstage[0:2, hf * 512:(hf + 1) * 512], bank(3)[0:2, :], [B_bank[3]], [B_tstage])
                    DMA("sp", convp[l], tstage[0:2, :], [B_tstage], [B_dram_out], ds_tst)
                if has_s:
                    for hf in range(2):
                        for cc in range(4):
                            c = hf * 4 + cc
                            TR(bank(3)[0:32, cc * 128:(cc + 1) * 128],
                               ustl[:, c, :], ident[:], [B_utail, B_const], [B_bank[3]])
                        CP(tstage[:, hf * 512:(hf + 1) * 512], bank(3)[0:32, :], [B_bank[3]], [B_tstage])
                    DMA("sp", convs[l], tstage[:], [B_tstage], [B_dram_out], ds_tst)

            for j in range(nbp + (1 if has_s else 0)):
                is_s = j >= nbp
                gblk = b0 + j
                if (not is_s) and gblk < 2:
                    continue
                pending.append(make_store(j, is_s, gblk))
        while pending:
            pending.pop(0)()
    except _Stop:
        pass
    S.emit(nc)
    es.close()
    return nc


_CACHE = {}


def _prep(x_prompt, x_sample, c_prompt, c_sample, state_conv, cache_k, cache_v,
          w_ada, b_ada, g_pre1, w_in, conv_w, w_br_conv, w_br_attn, w_o, sinks,
          g_post1, g_pre2, w_ff1, w_ff2, g_post2, rel_table):
    f = lambda a: np.ascontiguousarray(np.asarray(a, dtype=np.float32))
    x_prompt, x_sample, c_prompt, c_sample = f(x_prompt), f(x_sample), f(c_prompt), f(c_sample)
    state_conv, cache_k, cache_v = f(state_conv), f(cache_k), f(cache_v)
    w_ada, b_ada, g_pre1, w_in, conv_w = f(w_ada), f(b_ada), f(g_pre1), f(w_in), f(conv_w)
    w_br_conv, w_br_attn, w_o, sinks = f(w_br_conv), f(w_br_attn), f(w_o), f(sinks)
    g_post1, g_pre2, w_ff1, w_ff2, g_post2, rel_table = f(g_post1), f(g_pre2), f(w_ff1), f(w_ff2), f(g_post2), f(rel_table)

    def wtile(w):
        L, K, N = w.shape
        return np.ascontiguousarray(w.reshape(L, K // 128, 128, N // WCOLS, WCOLS).transpose(0, 3, 2, 1, 4)).reshape(
            L, N // WCOLS, 128, (K // 128) * WCOLS)

    w_in_p = wtile(w_in[:, :, _win_perm()])
    w_bra_p = wtile(w_br_attn[:, _attn_row_perm(), :])
    w_ada_t, w_brc_t, w_o_t, w_ff1_t = wtile(w_ada), wtile(w_br_conv), wtile(w_o), wtile(w_ff1)
    w_ff2_t = np.ascontiguousarray(w_ff2.reshape(DEPTH, 2, 16, 128, 8, 128).transpose(0, 4, 1, 3, 2, 5)).reshape(DEPTH, 16, 128, 2048)
    vecs = np.zeros((208, 128), np.float32)
    for l in range(DEPTH):
        b = l * 104
        vecs[b + 0:b + 8] = g_pre1[l].reshape(8, 128)
        vecs[b + 8:b + 16] = g_post1[l].reshape(8, 128)
        vecs[b + 16:b + 24] = g_pre2[l].reshape(8, 128)
        vecs[b + 24:b + 32] = g_post2[l].reshape(8, 128)
        vecs[b + 32:b + 56] = conv_w[l].reshape(24, 128)
        vecs[b + 56:b + 104] = b_ada[l].reshape(48, 128)
    sinkrep = np.zeros((128, 16), np.float32)
    for l in range(DEPTH):
        for c in range(8):
            sinkrep[0:64, l * 8 + c] = sinks[l, c]
            sinkrep[64:128, l * 8 + c] = sinks[l, 8 + c]
    frev = _frev()
    jj = np.arange(128)
    bdmask = (jj[:, None] // 8 == jj[None, :] // 8).astype(np.float32)
    identf = np.eye(128, dtype=np.float32)

    in_maps = []
    for core in range(NCORES):
        b, half = core // 2, core % 2
        xpc = np.zeros((NPB * 128, D), np.float32)
        if half == 0:
            xpc[256:] = x_prompt[b, 0:2048]
        else:
            xpc[:] = x_prompt[b, 2048 - 256:4096]
        ss = slice(core * NSEQ, (core + 1) * NSEQ)
        cinp = np.concatenate([c_prompt[b:b + 1], c_sample[ss]], axis=0)
        in_maps.append({
            "xp": xpc,
            "xs": np.ascontiguousarray(x_sample[ss].reshape(128, D)),
            "cin": np.ascontiguousarray(cinp),
            "vecs": vecs,
            "sinkrep": sinkrep,
            "hmask": np.full((128, 1), float(half), np.float32),
            "relt": rel_table,
            "frev": frev,
            "bdmask": bdmask,
            "identf": identf,
            "w_ada": w_ada_t, "w_in": w_in_p, "w_brc": w_brc_t, "w_bra": w_bra_p, "w_o": w_o_t,
            "w_ff1": w_ff1_t, "w_ff2": w_ff2_t,
            "sconv": np.ascontiguousarray(state_conv[:, ss].reshape(DEPTH, 32, D)),
            "ck": np.ascontiguousarray(cache_k[:, ss].reshape(DEPTH, NSEQ, 128, 128)),
            "cv": np.ascontiguousarray(cache_v[:, ss].reshape(DEPTH, NSEQ, 128, 128)),
        })
    return in_maps


def _assemble(R):

    y_prompt = np.zeros((4, 4096, D), np.float32)
    y_sample = np.zeros((128, 8, D), np.float32)
    conv_prompt = np.zeros((DEPTH, 4, 2, D), np.float32)
    k_prompt = np.zeros((DEPTH, 4, 128, 2, 64), np.float32)
    v_prompt = np.zeros((DEPTH, 4, 128, 2, 64), np.float32)
    conv_sample = np.zeros((DEPTH, 128, 2, D), np.float32)
    k_sample = np.zeros((DEPTH, 128, 128, 2, 64), np.float32)
    v_sample = np.zeros((DEPTH, 128, 128, 2, 64), np.float32)
    for core in range(NCORES):
        b, half = core // 2, core % 2
        r = R[core]
        y_prompt[b, half * 2048:(half + 1) * 2048] = r["yp"]
        ss = slice(core * NSEQ, (core + 1) * NSEQ)
        y_sample[ss] = r["ys"].reshape(NSEQ, 8, D)
        conv_sample[:, ss] = r["convs"].reshape(DEPTH, NSEQ, 2, D)
        k_sample[:, ss, 0:120] = r["ks"].reshape(DEPTH, NSEQ, 120, 2, 64)
        v_sample[:, ss, 0:120] = r["vs"].reshape(DEPTH, NSEQ, 120, 2, 64)
        k_sample[:, ss, 120:128] = r["ksn"].reshape(DEPTH, NSEQ, 8, 2, 64)
        v_sample[:, ss, 120:128] = r["vsn"].reshape(DEPTH, NSEQ, 8, 2, 64)
        if half == 1:
            conv_prompt[:, b] = r["convp"]
            k_prompt[:, b] = r["kp"].reshape(DEPTH, 128, 2, 64)
            v_prompt[:, b] = r["vp"].reshape(DEPTH, 128, 2, 64)
    return (y_prompt, y_sample, conv_prompt, k_prompt, v_prompt, conv_sample, k_sample, v_sample)


def kernel(**inputs):
    in_maps = _prep(**inputs)
    if "nc" not in _CACHE:
        _CACHE["nc"] = build_program()
    res = run_bass_kernel_spmd(_CACHE["nc"], in_maps, core_ids=list(range(NCORES)))
    return _assemble(res.results)
```

```python
import math
from contextlib import ExitStack

import numpy as np
import concourse.bass as bass
import concourse.mybir as mybir
from concourse.bass_utils import run_bass_kernel_spmd

F32 = mybir.dt.float32
BF16 = mybir.dt.bfloat16
AF = mybir.ActivationFunctionType
ALU = mybir.AluOpType

NCORES = 8
D = 1024
NCH = 8
DEPTH = 2
NPB = 18
NSEQ = 16
PROJ = 6400
DFF = 4096
EPS = 1e-6
NWSLOT = 7
WCOLS = 256
ENGS = ("pe", "act", "dve", "pool", "sp")


class Buf:
    __slots__ = ("name", "last_w", "readers", "excl")

    def __init__(self, name, excl=False):
        self.name = name
        self.last_w = None
        self.readers = []
        self.excl = excl


class Op:
    __slots__ = ("eng", "fn", "deps", "is_dma", "dsem", "needed", "inc_val")

    def __init__(self, eng, fn, deps, is_dma, dsem):
        self.eng = eng
        self.fn = fn
        self.deps = deps
        self.is_dma = is_dma
        self.dsem = dsem
        self.needed = False
        self.inc_val = None


class Sched:
    def __init__(self):
        self.ops = []
        self.n_dsem = 0

    def new_dsem(self):
        self.n_dsem += 1
        return self.n_dsem - 1

    def op(self, eng, fn, reads=(), writes=(), dma=False, dsem=None):
        ex = [b for b in reads if b.excl]
        if ex:
            writes = list(writes) + [b for b in ex if b not in writes]
            reads = [b for b in reads if not b.excl]
        deps = set()
        for b in reads:
            if b.last_w is not None:
                deps.add(b.last_w)
        for b in writes:
            if b.last_w is not None:
                deps.add(b.last_w)
            deps.update(b.readers)
        oid = len(self.ops)
        if eng == "pe":
            deps = {d for d in deps if self.ops[d].eng != "pe" or self.ops[d].is_dma}
        self.ops.append(Op(eng, fn, deps, dma, dsem))
        for b in writes:
            b.last_w = oid
            b.readers = []
        for b in reads:
            b.readers.append(oid)
        return oid

    def emit(self, nc, final_wait_eng="sp"):
        ops = self.ops
        for o in ops:
            latest = {}
            keep = set()
            for d in o.deps:
                p = ops[d]
                if p.is_dma:
                    keep.add(d)
                elif d > latest.get(p.eng, -1):
                    latest[p.eng] = d
            keep.update(latest.values())
            o.deps = keep
            for d in keep:
                ops[d].needed = True
        eng_cnt = {e: 0 for e in ENGS}
        dsem_cnt = {}
        for o in ops:
            if o.is_dma:
                dsem_cnt[o.dsem] = dsem_cnt.get(o.dsem, 0) + 16
                o.inc_val = dsem_cnt[o.dsem]
            elif o.needed:
                eng_cnt[o.eng] += 1
                o.inc_val = eng_cnt[o.eng]
        streams = {e: [] for e in ENGS}
        waited_e = {e: {f: 0 for f in ENGS} for e in ENGS}
        waited_d = {e: {} for e in ENGS}
        for o in ops:
            waits = []
            need_e = {}
            need_d = {}
            for d in o.deps:
                p = ops[d]
                if p.is_dma:
                    need_d[p.dsem] = max(need_d.get(p.dsem, 0), p.inc_val)
                else:
                    need_e[p.eng] = max(need_e.get(p.eng, 0), p.inc_val)
            for f, v in need_e.items():
                if v > waited_e[o.eng][f]:
                    waited_e[o.eng][f] = v
                    waits.append(("e", f, v))
            for s, v in need_d.items():
                if v > waited_d[o.eng].get(s, 0):
                    waited_d[o.eng][s] = v
                    waits.append(("d", s, v))
            streams[o.eng].append((waits, o))
        final_waits = []
        for s, v in dsem_cnt.items():
            if v > waited_d[final_wait_eng].get(s, 0):
                final_waits.append(("d", s, v))
        for f in ENGS:
            if eng_cnt[f] > waited_e[final_wait_eng][f]:
                final_waits.append(("e", f, eng_cnt[f]))

        with ExitStack() as es:
            esem = {e: es.enter_context(nc.semaphore("s_" + e)) for e in ENGS}
            dsems = [es.enter_context(nc.semaphore("d%d" % i)) for i in range(self.n_dsem)]
            block = es.enter_context(nc.Block())

            def run_stream(engname, eng):
                for waits, o in streams[engname]:
                    for kind, key, v in waits:
                        eng.wait_ge(esem[key] if kind == "e" else dsems[key], v)
                    ins = o.fn(eng)
                    if o.is_dma:
                        ins.then_inc(dsems[o.dsem], 16)
                    elif o.needed:
                        ins.then_inc(esem[o.eng], 1)
                if engname == final_wait_eng:
                    for kind, key, v in final_waits:
                        eng.wait_ge(esem[key] if kind == "e" else dsems[key], v)

            @block.tensor
            def _(e):
                run_stream("pe", e)

            @block.scalar
            def _(e):
                run_stream("act", e)

            @block.vector
            def _(e):
                run_stream("dve", e)

            @block.gpsimd
            def _(e):
                run_stream("pool", e)

            @block.sync
            def _(e):
                run_stream("sp", e)
        return eng_cnt


def _rel_bucket_np(dist):
    n = np.maximum(dist, 0)
    max_exact = 16
    nf = np.maximum(n, 1).astype(np.float32)
    large = max_exact + (np.log(nf / np.float32(max_exact)) / np.float32(math.log(128 / max_exact))
                         * np.float32(32 - max_exact)).astype(np.int32)
    large = np.minimum(large, 31)
    return np.where(n < max_exact, n, large)


def _frev():
    f = np.zeros((33, 383), np.float32)
    for y in range(383):
        dist = 255 - y
        if 0 <= dist < 128:
            f[int(_rel_bucket_np(np.array(dist))), y] = 1.0
        else:
            f[32, y] = 1.0
    return f


def _win_perm():
    cg0, xc0, bg0, q0, k0, v0, gt0 = 1024, 2048, 0, 3072, 4096, 4224, 4352
    cols = []
    for m in range(8):
        cols += list(range(cg0 + m * 128, cg0 + (m + 1) * 128))
        cols += list(range(xc0 + m * 128, xc0 + (m + 1) * 128))
    cols += list(range(bg0, bg0 + 1024))
    for c in range(8):
        cols += list(range(q0 + c * 64, q0 + (c + 1) * 64))
        cols += list(range(q0 + (8 + c) * 64, q0 + (9 + c) * 64))
    cols += list(range(gt0, gt0 + 2048))
    cols += list(range(k0, k0 + 128))
    cols += list(range(v0, v0 + 128))
    assert len(cols) == PROJ
    return np.array(cols)


def _attn_row_perm():
    rows = []
    for c in range(8):
        rows += list(range(c * 64, (c + 1) * 64))
        rows += list(range((8 + c) * 64, (9 + c) * 64))
    return np.array(rows)


def _proj_tiles():
    kinds = []
    for m in range(8):
        kinds += [("cg", m), ("xc", m)]
    kinds += [("bg", m) for m in range(8)]
    kinds += [("q", m) for m in range(8)]
    kinds += [("gate", m) for m in range(16)]
    return kinds


class _Stop(Exception):
    pass


def build_program(stop=None):
    nc = bass.Bass("TRN2", target_bir_lowering=False)

    ckstate = {"n": 0}

    def ckpt(n):
        ckstate["n"] += 1
        if stop is not None and ckstate["n"] == stop:
            print("STOP at checkpoint #%d (label %d)" % (stop, n))
            raise _Stop()

    def din(name, shape):
        return nc.dram_tensor(name, list(shape), F32, kind="ExternalInput").ap()

    def dout(name, shape):
        return nc.dram_tensor(name, list(shape), F32, kind="ExternalOutput").ap()

    xp = din("xp", [NPB * 128, D])
    xs = din("xs", [128, D])
    cin = din("cin", [17, D])
    vecs = din("vecs", [208, 128])
    sinkrep = din("sinkrep", [128, 16])
    hmask_d = din("hmask", [128, 1])
    relt = din("relt", [32, 16])
    frev_d = din("frev", [33, 383])
    bdmask_d = din("bdmask", [128, 128])
    ident_d = din("identf", [128, 128])
    w_ada = din("w_ada", [DEPTH, 24, 128, 8 * WCOLS])
    w_in = din("w_in", [DEPTH, 25, 128, 8 * WCOLS])
    w_brc = din("w_brc", [DEPTH, 4, 128, 8 * WCOLS])
    w_bra = din("w_bra", [DEPTH, 4, 128, 8 * WCOLS])
    w_o = din("w_o", [DEPTH, 4, 128, 8 * WCOLS])
    w_ff1 = din("w_ff1", [DEPTH, 16, 128, 8 * WCOLS])
    w_ff2 = din("w_ff2", [DEPTH, 16, 128, 8 * WCOLS])
    sconv = din("sconv", [DEPTH, 32, D])
    ck = din("ck", [DEPTH, NSEQ, 128, 128])
    cv = din("cv", [DEPTH, NSEQ, 128, 128])

    yp = dout("yp", [16 * 128, D])
    ys = dout("ys", [128, D])
    convp = dout("convp", [DEPTH, 2, D])
    kp = dout("kp", [DEPTH, 128, 128])
    vp = dout("vp", [DEPTH, 128, 128])
    convs = dout("convs", [DEPTH, 32, D])
    ks = dout("ks", [DEPTH, NSEQ, 120, 128])
    vs = dout("vs", [DEPTH, NSEQ, 120, 128])
    ksn = dout("ksn", [DEPTH, 128, 128])
    vsn = dout("vsn", [DEPTH, 128, 128])

    S = Sched()
    es = ExitStack()

    def sb(name, shape, dt):
        return es.enter_context(nc.sbuf_tensor(name, list(shape), dt))

    TM = 512
    ident = sb("ident", [128, 128], F32)
    ones_m = sb("ones_m", [128, 128], BF16)
    ones64 = sb("ones64", [128, 64], BF16)
    identb = sb("identb", [128, 128], BF16)
    vecsT = sb("vecsT", [128, 208], F32)
    der = sb("der", [128, DEPTH, 6, 8, 17], F32)
    es_t = sb("es_t", [128, 16], F32)
    hmask = sb("hmask_t", [128, 1], F32)
    epsc = sb("epsc", [128, 1], F32)
    Etab = sb("Etab", [128, 2, 16, 128], BF16)
    Esn = sb("Esn", [128, 16, 128], BF16)
    diag = sb("diag", [128, 24, 128], BF16)
    xT = sb("xT", [128, NCH, TM], F32)
    xtm = sb("xtm", [128, D], F32)
    xo = sb("xo", [128, D], F32)
    s8 = sb("s8", [128, NCH, TM], BF16)
    rstd = sb("rstd", [128, TM], F32)
    rt = sb("rt", [128, TM], F32)
    tmpr = sb("tmpr", [128, 2, TM], F32)
    hb = sb("hb", [128, NCH, TM], BF16)
    big = sb("big", [128, 32, TM], BF16)
    u_p = sb("u_p", [128, NCH, 2 + TM], BF16)
    u_s = sb("u_s", [128, NCH, NSEQ, 10], BF16)
    us_f = sb("us_f", [128, NCH, 128], F32)
    utail = sb("utail", [128, NCH, 2], F32)
    ustl = sb("ustl", [128, NCH, 32], F32)
    ucarry = sb("ucarry", [128, DEPTH, NCH, 2], BF16)
    kT = sb("kT", [128, DEPTH, 128 + TM], BF16)
    kTs = sb("kTs", [128, 128], BF16)
    vtm = sb("vtm", [128, DEPTH, 5, 128], BF16)
    v_s = sb("v_s", [128, 128], BF16)
    kvout = sb("kvout", [128, 256], F32)
    expS = sb("expS", [128, 3, 512], F32)
    PT = sb("PT", [128, 2, 2, 16, 128], BF16)
    rD = sb("rD", [128, NCH, 128], F32)
    cstage = rD[:, 0:4, :]
    mixed = sb("mixed", [128, NCH, TM], F32)
    frev = mixed[0:33, 0, 0:383]
    relx = mixed[0:33, 1, 0:16]
    bdm = mixed[:, 2, 0:128]
    bada_tmp = mixed[:, 3:5, :].rearrange("p a b -> p (a b)")[:, 0:816].rearrange("p (m n) -> p m n", n=17)
    tstage = xo[0:32, :]
    wslot = sb("wslot", [128, NWSLOT, 8 * WCOLS], BF16)
    kcT = sb("kcT", [128, NSEQ, 128], BF16)
    vc = sb("vc", [128, NSEQ, 128], BF16)
    scT = sb("scT", [128, NCH, 17], BF16)
    pall = es.enter_context(nc.psum_tensor("pall", [128, 4096], F32))

    def bank(i, n=1):
        return pall[:, i * 512:(i + n) * 512]

    B_bank = [Buf("bank%d" % i, excl=True) for i in range(8)]
    B_xT = [Buf("xT%d" % c) for c in range(NCH)]
    B_s8 = [Buf("s8_%d" % c) for c in range(NCH)]
    B_hb = [Buf("hb%d" % c) for c in range(NCH)]
    B_big = [Buf("big%d" % c) for c in range(32)]
    B_mixed = [Buf("mixed%d" % c) for c in range(NCH)]
    B_up = [Buf("up%d" % c) for c in range(NCH)]
    B_us = [Buf("us%d" % c) for c in range(NCH)]
    B_usf = [Buf("usf%d" % c) for c in range(NCH)]
    B_utail = Buf("utail")
    B_ucarry = [Buf("ucarry%d" % l) for l in range(DEPTH)]
    B_kT = [Buf("kT%d" % l) for l in range(DEPTH)]
    B_kTs = Buf("kTs")
    B_vtm = [Buf("vtm%d" % l) for l in range(DEPTH)]
    B_vs = Buf("vs")
    B_kvout = Buf("kvout")
    B_exp = [Buf("exp0"), Buf("exp1"), Buf("exp2")]
    B_PT = [[Buf("PT00"), Buf("PT01")], [Buf("PT10"), Buf("PT11")]]
    B_rD = Buf("rD")
    B_cstage = B_rD
    B_rstd = Buf("rstd")
    B_rt = Buf("rt")
    B_tmp = [Buf("tmp0"), Buf("tmp1")]
    B_w = [Buf("w%d" % i) for i in range(NWSLOT)]
    B_xtm_h = [Buf("xtm0"), Buf("xtm1")]
    B_xo = Buf("xo")
    B_const = Buf("const")
    B_vecsT = Buf("vecsT")
    B_der = Buf("der")
    B_E = Buf("E")
    B_diag = Buf("diag")
    B_kcT = Buf("kcT")
    B_vc = Buf("vc")
    B_tstage = B_xo
    B_misc = Buf("misc")
    B_scT = Buf("scT")
    B_dram_out = Buf("dram_out")

    ds_w = [S.new_dsem() for _ in range(NWSLOT)]
    ds_xtm_h = [S.new_dsem(), S.new_dsem()]
    ds_xo = S.new_dsem()
    ds_const = S.new_dsem()
    ds_kvout = S.new_dsem()
    ds_cst = S.new_dsem()
    ds_tst = S.new_dsem()
    ds_vc = S.new_dsem()
    ds_d2d = S.new_dsem()

    state = {"w": 0, "ring": 0, "exp": 0, "tmp": 0, "lb": 0, "sb": 0}

    def ACT(out, in_, func, reads, writes, **kw):
        S.op("act", lambda e: e.activation(out=out, in_=in_, func=func, **kw), reads, writes)

    def TT(out, in0, in1, op, reads, writes, eng="dve"):
        S.op(eng, lambda e: e.tensor_tensor(out=out, in0=in0, in1=in1, op=op), reads, writes)

    def TS(out, in0, s1, s2, op0, op1, reads, writes, eng="dve"):
        if s2 is None:
            S.op(eng, lambda e: e.tensor_scalar(out=out, in0=in0, scalar1=s1, scalar2=None, op0=op0), reads, writes)
        else:
            S.op(eng, lambda e: e.tensor_scalar(out=out, in0=in0, scalar1=s1, scalar2=s2, op0=op0, op1=op1), reads, writes)

    def CP(out, in_, reads, writes, eng="dve"):
        S.op(eng, lambda e: e.tensor_copy(out=out, in_=in_), reads, writes)

    def MM(out, lhsT, rhs, start, stop, reads, writes, skip=False):
        if skip:
            S.op("pe", lambda e: e.matmul(out, lhsT=lhsT, rhs=rhs, start=start, stop=stop, skip_group_check=True), reads, writes)
        else:
            S.op("pe", lambda e: e.matmul(out, lhsT=lhsT, rhs=rhs, start=start, stop=stop), reads, writes)

    def TR(out, in_, idt, reads, writes):
        S.op("pe", lambda e: e.transpose(out, in_, idt), reads, writes)

    def DMA(eng, out, in_, reads, writes, dsem, **kw):
        S.op(eng, lambda e: e.dma_start(out=out, in_=in_, **kw), reads, writes, dma=True, dsem=dsem)

    def next_ring():
        r = state["ring"] % 3
        state["ring"] += 1
        return r

    WSEQ = []
    for l_ in range(DEPTH):
        for sl_ in range(24):
            WSEQ.append((w_ada[l_, sl_], 8, WCOLS))
    for _t in range(5):
        for l_ in range(DEPTH):
            for sl_ in range(25):
                WSEQ.append((w_in[l_, sl_], 8, WCOLS))
            for wsrc in (w_brc, w_bra, w_o):
                for sl_ in range(4):
                    WSEQ.append((wsrc[l_, sl_], 8, WCOLS))
            for sl_ in range(16):
                WSEQ.append((w_ff1[l_, sl_], 8, WCOLS))
            for m_ in range(NCH):
                for kh_ in range(2):
                    WSEQ.append((w_ff2[l_, m_ * 2 + kh_], 16, 128))
    LOOKAHEAD = NWSLOT - 2
    state["wi"] = 0

    def _issue_w(k):
        src_ap, kch, ncols = WSEQ[k]
        sl = k % NWSLOT
        DMA("pool", wslot[:, sl, 0:kch * ncols], src_ap, [], [B_w[sl]], ds_w[sl])

    def load_w(src_ap, kch, ncols):
        k = state["w"]
        state["w"] += 1
        assert k < len(WSEQ) and str(WSEQ[k][0]) == str(src_ap) and WSEQ[k][1] == kch, ("weight order mismatch", k)
        while state["wi"] < len(WSEQ) and state["wi"] <= k + LOOKAHEAD:
            _issue_w(state["wi"])
            state["wi"] += 1
        sl = k % NWSLOT
        return sl, wslot[:, sl, 0:kch * ncols].rearrange("p (c e) -> p c e", c=kch)

    try:
        DMA("sp", ident[:], ident_d, [], [B_const], ds_const)
        DMA("sp", hmask[:], hmask_d, [], [B_const], ds_const)
        DMA("sp", es_t[:], sinkrep, [], [B_const], ds_const)
        DMA("sp", frev[:], frev_d, [], [B_const, B_mixed[0]], ds_const)
        DMA("sp", bdm[:], bdmask_d, [], [B_const, B_mixed[2]], ds_const)
        S.op("dve", lambda e: e.memset(ones_m[:], 1.0 / 1024.0), [], [B_const])
        S.op("dve", lambda e: e.memset(ones64[:], 1.0), [], [B_const])
        S.op("dve", lambda e: e.memset(epsc[:], EPS), [], [B_const])
        S.op("dve", lambda e: e.memset(relx[:], -1e30), [], [B_misc, B_mixed[1]])
        DMA("sp", relx[0:32, :], relt, [], [B_misc, B_mixed[1]], ds_const)
        CP(identb[:], ident[:], [B_const], [B_const])
        ACT(es_t[:], es_t[:], AF.Exp, [B_const], [B_const])

        for hlf in range(2):
            DMA("sp", xtm[0:104, 0:128], vecs[hlf * 104:(hlf + 1) * 104, :], [], B_xtm_h, ds_xtm_h[0])
            TR(bank(3)[:, 0:104], xtm[0:104, 0:128], ident[0:104, 0:104], B_xtm_h + [B_const], [B_bank[3]])
            CP(vecsT[:, hlf * 104:(hlf + 1) * 104], bank(3)[:, 0:104], [B_bank[3]], [B_vecsT])

        DMA("sp", xtm[0:17, :], cin, [], B_xtm_h, ds_xtm_h[0])
        for c in range(NCH):
            TR(bank(3)[:, c * 17:(c + 1) * 17], xtm[0:17, c * 128:(c + 1) * 128], ident[0:17, 0:17], B_xtm_h + [B_const], [B_bank[3]])
        ACT(scT[:].rearrange("p c n -> p (c n)"), bank(3)[:, 0:NCH * 17], AF.Silu, [B_bank[3]], [B_scT])

        ckpt(0)
        tiles = [(0, 4, False), (4, 4, False), (8, 4, False), (12, 4, False), (16, 2, True)]
        def tile_srcs(ti_):
            b0_, nbp_, has_s_ = tiles[ti_]
            return [xp[(b0_ + j) * 128:(b0_ + j + 1) * 128, :] for j in range(nbp_)] + ([xs] if has_s_ else [])

        xpre = set()

        def sample_prep(l):
            for g4 in range(4):
                DMA("sp", cstage[:], ck[l, g4 * 4:(g4 + 1) * 4].rearrange("s p d -> p s d"), [], [B_cstage], ds_cst)
                for sl in range(4):
                    TR(bank(3)[:, sl * 128:(sl + 1) * 128], cstage[:, sl, :], ident[:], [B_cstage, B_const], [B_bank[3]])
                CP(kcT[:, g4 * 4:(g4 + 1) * 4, :], bank(3).rearrange("p (s t) -> p s t", s=4), [B_bank[3]], [B_kcT])
            DMA("pool", vc[:], cv[l].rearrange("s p d -> p s d"), [], [B_vc], ds_vc)
            DMA("sp", tstage[:], sconv[l], [], [B_tstage], ds_tst)
            for c in range(NCH):
                TR(bank(3)[:, c * 32:(c + 1) * 32], tstage[:, c * 128:(c + 1) * 128], ident[0:32, 0:32], [B_tstage, B_const], [B_bank[3]])
            CP(u_s[:, :, :, 0:2], bank(3)[:, 0:256].rearrange("p (c s r) -> p c s r", c=NCH, r=2), [B_bank[3]], B_us)

        def stage_of(j, hf):
            if j % 2 == 0:
                return xtm, [B_xtm_h[hf]], ds_xtm_h[hf]
            return xo, [B_xo], ds_xo

        def xload(tix_):
            srcs = tile_srcs(tix_)
            for j, src in enumerate(srcs):
                for hf in range(2):
                    st, sbufs, sds = stage_of(j, hf)
                    if (tix_, j) not in xpre:
                        DMA("sp", st[:, hf * 512:(hf + 1) * 512], src[:, hf * 512:(hf + 1) * 512], [], sbufs, sds)
                    rb = state["lb"] % 4
                    state["lb"] += 1
                    for cc in range(4):
                        c = hf * 4 + cc
                        TR(bank(rb)[:, cc * 128:(cc + 1) * 128], st[:, c * 128:(c + 1) * 128], ident[:], sbufs + [B_const], [B_bank[rb]])
                    dstv = xT[:, hf * 4:(hf + 1) * 4, j * 128:(j + 1) * 128]
                    if hf == 0:
                        CP(dstv, bank(rb).rearrange("p (c t) -> p c t", c=4), [B_bank[rb]], B_xT[hf * 4:(hf + 1) * 4])
                    else:
                        ACT(dstv, bank(rb).rearrange("p (c t) -> p c t", c=4), AF.Copy, [B_bank[rb]], B_xT[hf * 4:(hf + 1) * 4])

        xload(0)
        sample_prep(0)

        pmod = pall[:, 4 * 512:7 * 512].rearrange("p (m n) -> p m n", n=32)

        def mod_slot(l, sl):
            s, wv = load_w(w_ada[l, sl], 8, WCOLS)
            for mt in range(2):
                m = sl * 2 + mt
                for c in range(NCH):
                    MM(pmod[:, m, 0:17], wv[:, c, mt * 128:(mt + 1) * 128], scT[:, c, :], c == 0, c == NCH - 1,
                       [B_w[s], B_scT], [B_bank[4 + m // 16]])

        def mod_finish(l):
            base = l * 104
            bada = vecsT[:, base + 56:base + 104]
            TT(bada_tmp[:], pmod[:, :, 0:17], bada.unsqueeze(2).to_broadcast([128, 48, 17]), ALU.add,
               [B_bank[4], B_bank[5], B_bank[6], B_vecsT], [B_misc, B_mixed[3], B_mixed[4]])
            for k, (gofs, scofs, mode) in enumerate([(0, 8, "a"), (None, 0, "b"), (8, 16, "g"), (16, 32, "a"), (None, 24, "b"), (24, 40, "g")]):
                dst = der[:, l, k]
                src = bada_tmp[:, scofs:scofs + 8, :]
                if mode == "b":
                    CP(dst, src, [B_misc, B_mixed[3], B_mixed[4]], [B_der])
                else:
                    gvec = vecsT[:, base + gofs:base + gofs + 8].unsqueeze(2).to_broadcast([128, 8, 17])
                    if mode == "a":
                        TS(dst, src, 1.0, None, ALU.add, None, [B_misc, B_mixed[3], B_mixed[4]], [B_der])
                        TT(dst, dst, gvec, ALU.mult, [B_der, B_vecsT], [B_der])
                    else:
                        TT(dst, src, gvec, ALU.mult, [B_misc, B_mixed[3], B_mixed[4], B_vecsT], [B_der])

        frevb0 = s8[0:33, 0, 0:382]
        frevb1 = s8[0:33, 1, 0:382]
        relh = s8[0:33, 2, 0:16]
        rell = s8[0:33, 2, 16:32]
        CP(frevb0, frev[:, 0:382], [B_const, B_mixed[0]], [B_s8[0]])
        CP(frevb1, frev[:, 1:383], [B_const, B_mixed[0]], [B_s8[1]])
        CP(relh, relx[:], [B_misc, B_mixed[1]], [B_s8[2]])
        TT(rell, relx[:], relh, ALU.subtract, [B_misc, B_mixed[1], B_s8[2]], [B_s8[2]])
        e_jobs = []
        for kb, off in ((0, 127), (1, 255)):
            for i in range(128):
                e_jobs.append((kb, off, i))

        def e_build(n):
            for _ in range(n):
                if not e_jobs:
                    return
                kb, off, i = e_jobs.pop(0)
                pE = pall[:, 0:2048].rearrange("p (i h) -> p i h", h=16)
                o = off - i
                lw = frevb0[:, o:o + 128] if o % 2 == 0 else frevb1[:, o - 1:o - 1 + 128]
                rds = [B_s8[0], B_s8[1], B_s8[2]]
                MM(pE[:, i, :], lw, relh, True, False, rds, [B_bank[i // 32]])
                MM(pE[:, i, :], lw, rell, False, True, rds, [B_bank[i // 32]])
                if i == 127:
                    ACT(Etab[:, kb].rearrange("p h i -> p i h"), pE, AF.Exp, [B_bank[j] for j in range(4)], [B_E])

        for sl in range(24):
            mod_slot(0, sl)
            e_build(6)
        mod_finish(0)
        for sl in range(24):
            mod_slot(1, sl)
            e_build(6)
        e_build(256)
        mod_finish(1)
        TT(Esn[:], Etab[:, 1], bdm[:].unsqueeze(1).to_broadcast([128, 16, 128]), ALU.mult, [B_E, B_const, B_mixed[2]], [B_E])

        ckpt(2)
        S.op("dve", lambda e: e.memset(ucarry[:], 0.0), [], B_ucarry)
        S.op("dve", lambda e: e.memset(kT[:, :, 0:128], 0.0), [], B_kT)
        S.op("dve", lambda e: e.memset(vtm[:, :, 0, :], 0.0), [], B_vtm)

        LO = {"v": 0}
        def norm_stats(T):
            for c in range(NCH):
                MM(bank(3)[:, LO["v"]:T], ones_m[:], s8[:, c, LO["v"]:T], c == 0, c == NCH - 1, [B_const, B_s8[c]], [B_bank[3]])
            ACT(rt[:, LO["v"]:T], bank(3)[:, LO["v"]:T], AF.Ln, [B_bank[3], B_const], [B_rt], bias=epsc[:, 0:1])
            ACT(rstd[:, LO["v"]:T], rt[:, LO["v"]:T], AF.Exp, [B_rt], [B_rstd], scale=-0.5)

        def pre_norm(l, T, Tp, ka, kbb, squares_done=False):
            if not squares_done:
                for c in range(NCH):
                    ACT(s8[:, c, LO["v"]:T], xT[:, c, LO["v"]:T], AF.Square, [B_xT[c]], [B_s8[c]])
            norm_stats(T)
            for c in range(NCH):
                ti = state["tmp"] % 2
                state["tmp"] += 1
                tv = tmpr[:, ti, LO["v"]:T]
                eng = "pool" if c in (1, 4, 6) else "dve"
                TT(tv, xT[:, c, LO["v"]:T], rstd[:, LO["v"]:T], ALU.mult, [B_xT[c], B_rstd], [B_tmp[ti]], eng=eng)
                if Tp > 0:
                    ACT(hb[:, c, LO["v"]:Tp], tmpr[:, ti, LO["v"]:Tp], AF.Identity, [B_tmp[ti], B_der], [B_hb[c]],
                        scale=der[:, l, ka, c, 0:1], bias=der[:, l, kbb, c, 0:1])
                if T > Tp:
                    sv = tmpr[:, ti, Tp:T].rearrange("p (s i) -> p s i", i=8)
                    TT(sv, sv, der[:, l, ka, c, 1:17].unsqueeze(2).to_broadcast([128, 16, 8]), ALU.mult,
                       [B_tmp[ti], B_der], [B_tmp[ti]])
                    TT(hb[:, c, Tp:T].rearrange("p (s i) -> p s i", i=8), sv,
                       der[:, l, kbb, c, 1:17].unsqueeze(2).to_broadcast([128, 16, 8]), ALU.add,
                       [B_tmp[ti], B_der], [B_hb[c]])

        def post_norm_residual(l, T, Tp, kg, squares_after=True, into_mixed=False):
            norm_stats(T)
            for c in range(NCH):
                eng = "pool" if c in (1, 4, 6) else "dve"
                TT(mixed[:, c, LO["v"]:T], mixed[:, c, LO["v"]:T], rstd[:, LO["v"]:T], ALU.mult, [B_mixed[c], B_rstd], [B_mixed[c]], eng=eng)
                if into_mixed:
                    TT(mixed[:, c, LO["v"]:T], mixed[:, c, LO["v"]:T], xT[:, c, LO["v"]:T], ALU.add, [B_xT[c], B_mixed[c]], [B_mixed[c]], eng=eng)
                    continue
                TT(xT[:, c, LO["v"]:T], xT[:, c, LO["v"]:T], mixed[:, c, LO["v"]:T], ALU.add, [B_xT[c], B_mixed[c]], [B_xT[c]], eng=eng)
                if squares_after:
                    ACT(s8[:, c, LO["v"]:T], xT[:, c, LO["v"]:T], AF.Square, [B_xT[c]], [B_s8[c]])

        def evac_scaled(l, m, rb, T, Tp, kg):
            pv = bank(rb)
            ACT(s8[:, m, LO["v"]:T], pv[:, LO["v"]:T], AF.Square, [B_bank[rb]], [B_s8[m]])
            if Tp > 0:
                ACT(mixed[:, m, LO["v"]:Tp], pv[:, LO["v"]:Tp], AF.Identity, [B_bank[rb], B_der], [B_mixed[m]],
                    scale=der[:, l, kg, m, 0:1])
            if T > Tp:
                TT(mixed[:, m, Tp:T].rearrange("p (s i) -> p s i", i=8), pv[:, Tp:T].rearrange("p (s i) -> p s i", i=8),
                   der[:, l, kg, m, 1:17].unsqueeze(2).to_broadcast([128, 16, 8]), ALU.mult,
                   [B_bank[rb], B_der], [B_mixed[m]])

        def fm_tile(wv, mt, s, rhs_fn, rhs_bufs, T, nk=NCH, first=True, last=True, rb=None, kofs=0):
            if rb is None:
                rb = next_ring()
            for c in range(nk):
                MM(bank(rb)[:, LO["v"]:T], wv[:, c, mt * 128:(mt + 1) * 128], rhs_fn(kofs + c), first and c == 0, last and c == nk - 1,
                   [B_w[s]] + rhs_bufs(kofs + c), [B_bank[rb]])
            return rb

        def fm_group(wvs, ss, mts, rhs_fn, rhs_bufs, T, banks):
            for c in range(NCH):
                for g in range(len(mts)):
                    MM(bank(banks[g])[:, LO["v"]:T], wvs[g][:, c, mts[g] * 128:(mts[g] + 1) * 128], rhs_fn(c), c == 0, c == NCH - 1,
                       [B_w[ss[g]]] + rhs_bufs(c), [B_bank[banks[g]]])

        pending = []

        def make_store(j, is_s, gblk):
            def fn():
                k = state.setdefault("st", 0)
                state["st"] = k + 1
                if k % 2 == 0:
                    stg, sb_, sds = xo, [B_xo], ds_xo
                else:
                    stg, sb_, sds = xtm, list(B_xtm_h), ds_xtm_h[0]
                for hf in range(2):
                    rb = 4 + state["sb"] % 4
                    state["sb"] += 1
                    for cc in range(4):
                        c = hf * 4 + cc
                        TR(bank(rb)[:, cc * 128:(cc + 1) * 128], mixed[:, c, j * 128:(j + 1) * 128], ident[:], [B_mixed[c], B_const],
                           [B_bank[rb]])
                    if hf == 0:
                        CP(stg[:, 0:512], bank(rb), [B_bank[rb]], sb_)
                    else:
                        ACT(stg[:, 512:1024], bank(rb), AF.Copy, [B_bank[rb]], sb_)
                dst = ys if is_s else yp[(gblk - 2) * 128:(gblk - 1) * 128, :]
                DMA("sp", dst, stg[:], sb_, [B_dram_out], sds)
            return fn

        for tix, (b0, nbp, has_s) in enumerate(tiles):
            Tp = nbp * 128
            T = Tp + (128 if has_s else 0)
            if tix > 0:
                xload(tix)
            ckpt(3)
            for l in range(DEPTH):
                base = l * 104
                if b0 == 0:
                    lo_u, lo_m = (0, 128) if l == 0 else (128, 256)
                else:
                    lo_u, lo_m = 0, 0
                LO["v"] = lo_u
                if l == 1 and tix + 1 < len(tiles):
                    for jn in range(2):
                        nsrc = tile_srcs(tix + 1)[jn]
                        for hf in range(2):
                            st, sbufs, sds = stage_of(jn, hf)
                            DMA("sp", st[:, hf * 512:(hf + 1) * 512], nsrc[:, hf * 512:(hf + 1) * 512], [], sbufs, sds)
                        xpre.add((tix + 1, jn))
                if tix == 2:
                    DMA("sp", ks[l], ck[l][:, 8:128, :], [], [B_dram_out], ds_d2d)
                    DMA("sp", vs[l], cv[l][:, 8:128, :], [], [B_dram_out], ds_d2d)
                CP(u_p[:, :, 0:2], ucarry[:, l], [B_ucarry[l]], B_up)
                for j in range(3):
                    for c in range(NCH):
                        TS(diag[:, j * 8 + c, :], identb[:], vecsT[:, base + 32 + j * 8 + c:base + 33 + j * 8 + c], None, ALU.mult, None,
                           [B_const, B_vecsT], [B_diag])

                pre_norm(l, T, Tp, 0, 1, squares_done=(l > 0))

                ckpt(4)
                hfn = lambda c: hb[:, c, LO["v"]:T]
                hbufs = lambda c: [B_hb[c]]
                kinds = _proj_tiles()
                cgst = {}

                def proj_evac(kind, m, rb):
                    pv = bank(rb)
                    if kind == "cg":
                        ti = state["tmp"] % 2
                        state["tmp"] += 1
                        cgst[m] = ti
                        ACT(tmpr[:, ti, LO["v"]:T], pv[:, LO["v"]:T], AF.Copy, [B_bank[rb]], [B_tmp[ti]])
                    elif kind == "xc":
                        ti = cgst[m]
                        TT(u_p[:, m, 2 + LO["v"]:2 + Tp], pv[:, LO["v"]:Tp], tmpr[:, ti, LO["v"]:Tp], ALU.mult, [B_bank[rb], B_tmp[ti]], [B_up[m]])
                        if b0 + nbp == NPB:
                            TT(utail[:, m, :], pv[:, Tp - 2:Tp], tmpr[:, ti, Tp - 2:Tp], ALU.mult, [B_bank[rb], B_tmp[ti]], [B_utail])
                        if has_s:
                            TT(us_f[:, m, :], pv[:, Tp:T], tmpr[:, ti, Tp:T], ALU.mult, [B_bank[rb], B_tmp[ti]], [B_usf[m]])
                            CP(u_s[:, m, :, 2:10], us_f[:, m, :].rearrange("p (s i) -> p s i", i=8), [B_usf[m]], [B_us[m]])
                            CP(ustl[:, m, :].rearrange("p (s r) -> p s r", r=2),
                               us_f[:, m, :].rearrange("p (s i) -> p s i", i=8)[:, :, 6:8], [B_usf[m]], [B_utail])
                    elif kind == "bg":
                        ACT(big[:, 16 + m, LO["v"]:T], pv[:, LO["v"]:T], AF.Copy, [B_bank[rb]], [B_big[16 + m]])
                    elif kind == "q":
                        ACT(big[:, 24 + m, LO["v"]:T], pv[:, LO["v"]:T], AF.Copy, [B_bank[rb]], [B_big[24 + m]], scale=0.125)
                    else:
                        ACT(big[:, m, LO["v"]:T], pv[:, LO["v"]:T], AF.Sigmoid, [B_bank[rb]], [B_big[m]])

                LO["v"] = lo_u
                s0, wv0 = load_w(w_in[l, 0], 8, WCOLS)
                s1, wv1 = load_w(w_in[l, 1], 8, WCOLS)
                hbanks = [next_ring(), next_ring(), next_ring(), 4]
                fm_group([wv0, wv0, wv1, wv1], [s0, s0, s1, s1], [0, 1, 0, 1], hfn, hbufs, T, hbanks)
                for g in range(4):
                    proj_evac(kinds[g][0], kinds[g][1], hbanks[g])
                for sl in range(2, 24):
                    s, wv = load_w(w_in[l, sl], 8, WCOLS)
                    if pending and sl in (3, 8, 13, 18):
                        pending.pop(0)()
                    for mt in range(2):
                        kind, m = kinds[sl * 2 + mt]
                        LO["v"] = lo_u if kind in ("cg", "xc") else lo_m
                        rb = fm_tile(wv, mt, s, hfn, hbufs, T)
                        proj_evac(kind, m, rb)
                ckpt(41)
                while pending:
                    pending.pop(0)()
                s, wv = load_w(w_in[l, 24], 8, WCOLS)
                LO["v"] = lo_u
                rb = fm_tile(wv, 0, s, hfn, hbufs, T)
                ACT(kT[:, l, 128 + LO["v"]:128 + Tp], bank(rb)[:, LO["v"]:Tp], AF.Copy, [B_bank[rb]], [B_kT[l]])
                if has_s:
                    ACT(kTs[:], bank(rb)[:, Tp:T], AF.Copy, [B_bank[rb]], [B_kTs])
                ckpt(42)
                nblk = nbp + (1 if has_s else 0)
                for j in range(nblk):
                    is_s = j >= nbp
                    if j * 128 < lo_u:
                        continue
                    need_k = is_s or (b0 + j == NPB - 1)
                    c0 = 0 if need_k else 128
                    for c in range(NCH):
                        MM(bank(3)[:, c0:256], hb[:, c, j * 128:(j + 1) * 128], wv[:, c, c0:256], c == 0, c == NCH - 1,
                           [B_hb[c], B_w[s]], [B_bank[3]])
                    if is_s:
                        CP(v_s[:], bank(3)[:, 128:256], [B_bank[3]], [B_vs])
                    else:
                        CP(vtm[:, l, 1 + j, :], bank(3)[:, 128:256], [B_bank[3]], [B_vtm[l]])
                    if is_s or (b0 + j == NPB - 1):
                        ACT(kvout[:], bank(3)[:, 0:256], AF.Copy, [B_bank[3]], [B_kvout])
                        if is_s:
                            DMA("sp", ksn[l], kvout[:, 0:128], [B_kvout], [B_dram_out], ds_kvout)
                            DMA("sp", vsn[l], kvout[:, 128:256], [B_kvout], [B_dram_out], ds_kvout)
                        else:
                            DMA("sp", kp[l], kvout[:, 0:128], [B_kvout], [B_dram_out], ds_kvout)
                            DMA("sp", vp[l], kvout[:, 128:256], [B_kvout], [B_dram_out], ds_kvout)

                ckpt(5)
                LO["v"] = lo_m

                def conv_stage():
                  if True:
                    if b0 == 0:
                        TS(u_p[:, :, 2 + 254:2 + 256], u_p[:, :, 2 + 254:2 + 256], hmask[:, 0:1], None, ALU.mult, None,
                           B_up + [B_const], B_up)
                    for m in range(NCH):
                        rb = next_ring()
                        for j in range(3):
                            MM(bank(rb)[:, LO["v"]:Tp], diag[:, j * 8 + m, :], u_p[:, m, LO["v"] + j:j + Tp], j == 0, j == 2, [B_diag, B_up[m]], [B_bank[rb]])
                        if has_s:
                            for j in range(3):
                                MM(bank(rb)[:, Tp:T].rearrange("p (s i) -> p s i", i=8), diag[:, j * 8 + m, :], u_s[:, m, :, j:j + 8],
                                   j == 0, j == 2, [B_diag, B_us[m]], [B_bank[rb]])
                        TT(s8[:, m, LO["v"]:T], bank(rb)[:, LO["v"]:T], big[:, 16 + m, LO["v"]:T], ALU.mult, [B_bank[rb], B_big[16 + m]], [B_s8[m]])
                    CP(ucarry[:, l], u_p[:, :, Tp:Tp + 2], B_up, [B_ucarry[l]])
                ckpt(6)
                pD = pall[:, 6 * 512:8 * 512].rearrange("p (c q) -> p c q", c=8)
                pO = pall[:, 4 * 512:6 * 512].rearrange("p (c q) -> p c q", c=8)

                def att_A(j):
                    is_s = j >= nbp
                    q0 = j * 128
                    pb = j % 2
                    for kb in range(2):
                        for g in range(2):
                            for quad in range(2):
                                rb = next_ring()
                                hs = 8 * g + 4 * quad
                                qv = big[g * 64:(g + 1) * 64, 24 + 4 * quad:24 + 4 * quad + 4, q0:q0 + 128]
                                qb = B_big[24 + 4 * quad:24 + 4 * quad + 4]
                                pv = bank(rb)
                                if not is_s:
                                    kcols = slice(j * 128 + kb * 128, j * 128 + kb * 128 + 128)
                                    MM(pv, kT[g * 64:(g + 1) * 64, l, kcols], qv, True, True, [B_kT[l]] + qb, [B_bank[rb]])
                                    ein = Etab[:, kb, hs:hs + 4, :]
                                elif kb == 1:
                                    MM(pv, kTs[g * 64:(g + 1) * 64, :], qv, True, True, [B_kTs] + qb, [B_bank[rb]])
                                    ein = Esn[:, hs:hs + 4, :]
                                else:
                                    for sq in range(NSEQ):
                                        MM(pv[:, sq * 32:(sq + 1) * 32], kcT[g * 64:(g + 1) * 64, sq, :],
                                           big[g * 64:(g + 1) * 64, 24 + 4 * quad:24 + 4 * quad + 4, q0 + sq * 8:q0 + sq * 8 + 8],
                                           True, True, [B_kcT] + qb, [B_bank[rb]], skip=True)
                                    ein = Etab[:, 0, hs:hs + 4, 0:8].unsqueeze(1).to_broadcast([128, 16, 4, 8])
                                xi = state["exp"] % 3
                                state["exp"] += 1
                                ACT(expS[:, xi, :], pv, AF.Exp, [B_bank[rb]], [B_exp[xi]])
                                meng = "pool" if quad == 1 else "dve"
                                if is_s and kb == 0:
                                    TT(PT[:, pb, kb, hs:hs + 4, :].rearrange("p h (s i) -> p s h i", i=8),
                                       expS[:, xi, :].rearrange("p (s h i) -> p s h i", h=4, i=8), ein, ALU.mult,
                                       [B_exp[xi], B_E], [B_PT[pb][kb]], eng=meng)
                                else:
                                    TT(PT[:, pb, kb, hs:hs + 4, :], expS[:, xi, :].rearrange("p (h q) -> p h q", h=4), ein, ALU.mult,
                                       [B_exp[xi], B_E], [B_PT[pb][kb]], eng=meng)
                    if (not is_s) and (b0 + j) == 2:
                        TS(PT[:, pb, 0], PT[:, pb, 0], hmask[:, 0:1], None, ALU.mult, None, [B_PT[pb][0], B_const], [B_PT[pb][0]])

                def att_B(j):
                    is_s = j >= nbp
                    q0 = j * 128
                    pb = j % 2
                    for g in range(2):
                        for quad in range(2):
                            hs = 8 * g + 4 * quad
                            for kb in range(2):
                                MM(pD[g * 64:(g + 1) * 64, 4 * quad:4 * quad + 4, :], ones64[:], PT[:, pb, kb, hs:hs + 4, :], kb == 0, kb == 1,
                                   [B_const, B_PT[pb][kb]], [B_bank[6 + quad]])
                    if not is_s:
                        for g in range(2):
                            for c in range(8):
                                for kb in range(2):
                                    MM(pO[g * 64:(g + 1) * 64, c, :], vtm[:, l, j + kb, g * 64:(g + 1) * 64], PT[:, pb, kb, 8 * g + c, :],
                                       kb == 0, kb == 1, [B_vtm[l], B_PT[pb][kb]], [B_bank[4 + c // 4]])
                    else:
                        for g in range(2):
                            for c in range(8):
                                MM(pO[g * 64:(g + 1) * 64, c, :], v_s[:, g * 64:(g + 1) * 64], PT[:, pb, 1, 8 * g + c, :],
                                   True, True, [B_vs, B_PT[pb][1]], [B_bank[4 + c // 4]])
                        for quad in range(2):
                            for g in range(2):
                                for sq in range(NSEQ):
                                    MM(bank(quad)[g * 64:(g + 1) * 64, sq * 32:(sq + 1) * 32],
                                       vc[:, sq, g * 64:(g + 1) * 64],
                                       PT[:, pb, 0, 8 * g + 4 * quad:8 * g + 4 * quad + 4, sq * 8:(sq + 1) * 8],
                                       True, True, [B_vc, B_PT[pb][0]], [B_bank[quad]], skip=True)
                            ACT(mixed[:, 4 * quad:4 * quad + 4, 0:128].rearrange("p c (s i) -> p s c i", i=8),
                                bank(quad).rearrange("p (s c i) -> p s c i", c=4, i=8), AF.Copy, [B_bank[quad]],
                                B_mixed[4 * quad:4 * quad + 4])
                    TT(rD[:], pD, es_t[:, l * 8:(l + 1) * 8].unsqueeze(2).to_broadcast([128, 8, 128]), ALU.add,
                       [B_bank[6], B_bank[7], B_const], [B_rD])
                    ACT(rD[:], rD[:], AF.Ln, [B_rD], [B_rD])
                    ACT(rD[:], rD[:], AF.Exp, [B_rD], [B_rD], scale=-1.0)
                    if not is_s:
                        TT(hb[:, :, q0:q0 + 128], pO, rD[:], ALU.mult, [B_bank[4], B_bank[5], B_rD], B_hb)
                    else:
                        TT(mixed[:, :, 0:128], mixed[:, :, 0:128], pO, ALU.add, B_mixed + [B_bank[4], B_bank[5]], B_mixed)
                        TT(hb[:, :, q0:q0 + 128], mixed[:, :, 0:128], rD[:], ALU.mult, B_mixed + [B_rD], B_hb)

                brc_state = {"m": 0, "wv": None, "s": None}

                def brc_tiles(n):
                    for _ in range(n):
                        m = brc_state["m"]
                        if m >= NCH:
                            return
                        if m % 2 == 0:
                            brc_state["s"], brc_state["wv"] = load_w(w_brc[l, m // 2], 8, WCOLS)
                        rb = fm_tile(brc_state["wv"], m % 2, brc_state["s"], lambda c: s8[:, c, LO["v"]:T], lambda c: [B_s8[c]], T)
                        TT(big[:, 16 + m, LO["v"]:T], bank(rb)[:, LO["v"]:T], big[:, m, LO["v"]:T], ALU.mult, [B_bank[rb], B_big[m]], [B_big[16 + m]])
                        brc_state["m"] = m + 1

                ablk = [j for j in range(nblk) if j * 128 >= lo_m]
                per = NCH // (len(ablk) + 1)
                att_A(ablk[0])
                conv_stage()
                for ii, j in enumerate(ablk):
                    if ii + 1 < len(ablk):
                        att_A(ablk[ii + 1])
                    brc_tiles(per)
                    att_B(j)
                brc_tiles(NCH)
                if not has_s:
                    CP(kT[:, l, 0:128], kT[:, l, Tp:Tp + 128], [B_kT[l]], [B_kT[l]])
                    CP(vtm[:, l, 0, :], vtm[:, l, nbp, :], [B_vtm[l]], [B_vtm[l]])

                ckpt(7)
                if has_s and l == 0:
                    sample_prep(1)
                for half in range(4):
                    s, wv = load_w(w_bra[l, half], 8, WCOLS)
                    for mt in range(2):
                        m = half * 2 + mt
                        rb = fm_tile(wv, mt, s, lambda c: hb[:, c, LO["v"]:T], lambda c: [B_hb[c]], T)
                        TT(big[:, 24 + m, LO["v"]:T], bank(rb)[:, LO["v"]:T], big[:, 8 + m, LO["v"]:T], ALU.mult, [B_bank[rb], B_big[8 + m]], [B_big[24 + m]])
                        TT(big[:, 16 + m, LO["v"]:T], big[:, 16 + m, LO["v"]:T], big[:, 24 + m, LO["v"]:T], ALU.add, [B_big[16 + m], B_big[24 + m]],
                           [B_big[16 + m]], eng="pool")
                for half in range(4):
                    s, wv = load_w(w_o[l, half], 8, WCOLS)
                    for mt in range(2):
                        m = half * 2 + mt
                        rb = fm_tile(wv, mt, s, lambda c: big[:, 16 + c, LO["v"]:T], lambda c: [B_big[16 + c]], T)
                        evac_scaled(l, m, rb, T, Tp, 2)
                post_norm_residual(l, T, Tp, 2, squares_after=True)

                ckpt(8)
                pre_norm(l, T, Tp, 3, 4, squares_done=True)
                def ff1_evac(jx, rb):
                    ACT(big[:, jx, LO["v"]:T], bank(rb)[:, LO["v"]:T], AF.Relu, [B_bank[rb]], [B_big[jx]])
                    TT(big[:, jx, LO["v"]:T], big[:, jx, LO["v"]:T], big[:, jx, LO["v"]:T], ALU.mult, [B_big[jx]], [B_big[jx]],
                       eng=("pool" if jx % 2 else "dve"))

                s0, wv0 = load_w(w_ff1[l, 0], 8, WCOLS)
                s1, wv1 = load_w(w_ff1[l, 1], 8, WCOLS)
                hbanks = [next_ring(), next_ring(), next_ring(), 4]
                fm_group([wv0, wv0, wv1, wv1], [s0, s0, s1, s1], [0, 1, 0, 1], lambda c: hb[:, c, LO["v"]:T], lambda c: [B_hb[c]], T, hbanks)
                for g in range(4):
                    ff1_evac(g, hbanks[g])
                for sl in range(2, 16):
                    s, wv = load_w(w_ff1[l, sl], 8, WCOLS)
                    for mt in range(2):
                        jx = sl * 2 + mt
                        rb = fm_tile(wv, mt, s, lambda c: hb[:, c, LO["v"]:T], lambda c: [B_hb[c]], T)
                        ff1_evac(jx, rb)
                for m in range(NCH):
                    rb = next_ring()
                    for kh in range(2):
                        s, wv = load_w(w_ff2[l, m * 2 + kh], 16, 128)
                        fm_tile(wv, 0, s, lambda c: big[:, c, LO["v"]:T], lambda c: [B_big[c]], T, nk=16, first=(kh == 0), last=(kh == 1),
                                rb=rb, kofs=kh * 16)
                    evac_scaled(l, m, rb, T, Tp, 5)
                post_norm_residual(l, T, Tp, 5, squares_after=(l + 1 < DEPTH), into_mixed=(l + 1 == DEPTH))

                ckpt(9)
                if b0 + nbp == NPB:
                    for hf in range(2):
                        for cc in range(4):
                            c = hf * 4 + cc
                            TR(bank(3)[0:2, cc * 128:(cc + 1) * 128], utail[:, c, :], ident[:], [B_utail, B_const], [B_bank[3]])
                        CP(tstage[0:2, hf * 512:(hf + 1) * 512], bank(3)[0:2, :], [B_bank[3]], [B_tstage])
                    DMA("sp", convp[l], tstage[0:2, :], [B_tstage], [B_dram_out], ds_tst)
                if has_s:
                    for hf in range(2):
                        for cc in range(4):
                            c = hf * 4 + cc
                            TR(bank(3)[0:32, cc * 128:(cc + 1) * 128],
                               ustl[:, c, :], ident[:], [B_utail, B_const], [B_bank[3]])
                        CP(tstage[:, hf * 512:(hf + 1) * 512], bank(3)[0:32, :], [B_bank[3]], [B_tstage])
                    DMA("sp", convs[l], tstage[:], [B_tstage], [B_dram_out], ds_tst)

            for j in range(nbp + (1 if has_s else 0)):
                is_s = j >= nbp
                gblk = b0 + j
                if (not is_s) and gblk < 2:
                    continue
                pending.append(make_store(j, is_s, gblk))
        while pending:
            pending.pop(0)()
    except _Stop:
        pass
    S.emit(nc)
    es.close()
    return nc


_CACHE = {}


def _prep(x_prompt, x_sample, c_prompt, c_sample, state_conv, cache_k, cache_v,
          w_ada, b_ada, g_pre1, w_in, conv_w, w_br_conv, w_br_attn, w_o, sinks,
          g_post1, g_pre2, w_ff1, w_ff2, g_post2, rel_table):
    f = lambda a: np.ascontiguousarray(np.asarray(a, dtype=np.float32))
    x_prompt, x_sample, c_prompt, c_sample = f(x_prompt), f(x_sample), f(c_prompt), f(c_sample)
    state_conv, cache_k, cache_v = f(state_conv), f(cache_k), f(cache_v)
    w_ada, b_ada, g_pre1, w_in, conv_w = f(w_ada), f(b_ada), f(g_pre1), f(w_in), f(conv_w)
    w_br_conv, w_br_attn, w_o, sinks = f(w_br_conv), f(w_br_attn), f(w_o), f(sinks)
    g_post1, g_pre2, w_ff1, w_ff2, g_post2, rel_table = f(g_post1), f(g_pre2), f(w_ff1), f(w_ff2), f(g_post2), f(rel_table)

    def wtile(w):
        L, K, N = w.shape
        return np.ascontiguousarray(w.reshape(L, K // 128, 128, N // WCOLS, WCOLS).transpose(0, 3, 2, 1, 4)).reshape(
            L, N // WCOLS, 128, (K // 128) * WCOLS)

    w_in_p = wtile(w_in[:, :, _win_perm()])
    w_bra_p = wtile(w_br_attn[:, _attn_row_perm(), :])
    w_ada_t, w_brc_t, w_o_t, w_ff1_t = wtile(w_ada), wtile(w_br_conv), wtile(w_o), wtile(w_ff1)
    w_ff2_t = np.ascontiguousarray(w_ff2.reshape(DEPTH, 2, 16, 128, 8, 128).transpose(0, 4, 1, 3, 2, 5)).reshape(DEPTH, 16, 128, 2048)
    vecs = np.zeros((208, 128), np.float32)
    for l in range(DEPTH):
        b = l * 104
        vecs[b + 0:b + 8] = g_pre1[l].reshape(8, 128)
        vecs[b + 8:b + 16] = g_post1[l].reshape(8, 128)
        vecs[b + 16:b + 24] = g_pre2[l].reshape(8, 128)
        vecs[b + 24:b + 32] = g_post2[l].reshape(8, 128)
        vecs[b + 32:b + 56] = conv_w[l].reshape(24, 128)
        vecs[b + 56:b + 104] = b_ada[l].reshape(48, 128)
    sinkrep = np.zeros((128, 16), np.float32)
    for l in range(DEPTH):
        for c in range(8):
            sinkrep[0:64, l * 8 + c] = sinks[l, c]
            sinkrep[64:128, l * 8 + c] = sinks[l, 8 + c]
    frev = _frev()
    jj = np.arange(128)
    bdmask = (jj[:, None] // 8 == jj[None, :] // 8).astype(np.float32)
    identf = np.eye(128, dtype=np.float32)

    in_maps = []
    for core in range(NCORES):
        b, half = core // 2, core % 2
        xpc = np.zeros((NPB * 128, D), np.float32)
        if half == 0:
            xpc[256:] = x_prompt[b, 0:2048]
        else:
            xpc[:] = x_prompt[b, 2048 - 256:4096]
        ss = slice(core * NSEQ, (core + 1) * NSEQ)
        cinp = np.concatenate([c_prompt[b:b + 1], c_sample[ss]], axis=0)
        in_maps.append({
            "xp": xpc,
            "xs": np.ascontiguousarray(x_sample[ss].reshape(128, D)),
            "cin": np.ascontiguousarray(cinp),
            "vecs": vecs,
            "sinkrep": sinkrep,
            "hmask": np.full((128, 1), float(half), np.float32),
            "relt": rel_table,
            "frev": frev,
            "bdmask": bdmask,
            "identf": identf,
            "w_ada": w_ada_t, "w_in": w_in_p, "w_brc": w_brc_t, "w_bra": w_bra_p, "w_o": w_o_t,
            "w_ff1": w_ff1_t, "w_ff2": w_ff2_t,
            "sconv": np.ascontiguousarray(state_conv[:, ss].reshape(DEPTH, 32, D)),
            "ck": np.ascontiguousarray(cache_k[:, ss].reshape(DEPTH, NSEQ, 128, 128)),
            "cv": np.ascontiguousarray(cache_v[:, ss].reshape(DEPTH, NSEQ, 128, 128)),
        })
    return in_maps


def _assemble(R):

    y_prompt = np.zeros((4, 4096, D), np.float32)
    y_sample = np.zeros((128, 8, D), np.float32)
    conv_prompt = np.zeros((DEPTH, 4, 2, D), np.float32)
    k_prompt = np.zeros((DEPTH, 4, 128, 2, 64), np.float32)
    v_prompt = np.zeros((DEPTH, 4, 128, 2, 64), np.float32)
    conv_sample = np.zeros((DEPTH, 128, 2, D), np.float32)
    k_sample = np.zeros((DEPTH, 128, 128, 2, 64), np.float32)
    v_sample = np.zeros((DEPTH, 128, 128, 2, 64), np.float32)
    for core in range(NCORES):
        b, half = core // 2, core % 2
        r = R[core]
        y_prompt[b, half * 2048:(half + 1) * 2048] = r["yp"]
        ss = slice(core * NSEQ, (core + 1) * NSEQ)
        y_sample[ss] = r["ys"].reshape(NSEQ, 8, D)
        conv_sample[:, ss] = r["convs"].reshape(DEPTH, NSEQ, 2, D)
        k_sample[:, ss, 0:120] = r["ks"].reshape(DEPTH, NSEQ, 120, 2, 64)
        v_sample[:, ss, 0:120] = r["vs"].reshape(DEPTH, NSEQ, 120, 2, 64)
        k_sample[:, ss, 120:128] = r["ksn"].reshape(DEPTH, NSEQ, 8, 2, 64)
        v_sample[:, ss, 120:128] = r["vsn"].reshape(DEPTH, NSEQ, 8, 2, 64)
        if half == 1:
            conv_prompt[:, b] = r["convp"]
            k_prompt[:, b] = r["kp"].reshape(DEPTH, 128, 2, 64)
            v_prompt[:, b] = r["vp"].reshape(DEPTH, 128, 2, 64)
    return (y_prompt, y_sample, conv_prompt, k_prompt, v_prompt, conv_sample, k_sample, v_sample)


def kernel(**inputs):
    in_maps = _prep(**inputs)
    if "nc" not in _CACHE:
        _CACHE["nc"] = build_program()
    res = run_bass_kernel_spmd(_CACHE["nc"], in_maps, core_ids=list(range(NCORES)))
    return _assemble(res.results)
```

```python
import math
from contextlib import ExitStack

import numpy as np
import concourse.bass as bass
import concourse.mybir as mybir
from concourse.bass_utils import run_bass_kernel_spmd

F32 = mybir.dt.float32
BF16 = mybir.dt.bfloat16
AF = mybir.ActivationFunctionType
ALU = mybir.AluOpType

NCORES = 8
D = 1024
NCH = 8
DEPTH = 2
NPB = 18
NSEQ = 16
PROJ = 6400
DFF = 4096
EPS = 1e-6
NWSLOT = 7
WCOLS = 256
ENGS = ("pe", "act", "dve", "pool", "sp")


class Buf:
    __slots__ = ("name", "last_w", "readers", "excl")

    def __init__(self, name, excl=False):
        self.name = name
        self.last_w = None
        self.readers = []
        self.excl = excl


class Op:
    __slots__ = ("eng", "fn", "deps", "is_dma", "dsem", "needed", "inc_val")

    def __init__(self, eng, fn, deps, is_dma, dsem):
        self.eng = eng
        self.fn = fn
        self.deps = deps
        self.is_dma = is_dma
        self.dsem = dsem
        self.needed = False
        self.inc_val = None


class Sched:
    def __init__(self):
        self.ops = []
        self.n_dsem = 0

    def new_dsem(self):
        self.n_dsem += 1
        return self.n_dsem - 1

    def op(self, eng, fn, reads=(), writes=(), dma=False, dsem=None):
        ex = [b for b in reads if b.excl]
        if ex:
            writes = list(writes) + [b for b in ex if b not in writes]
            reads = [b for b in reads if not b.excl]
        deps = set()
        for b in reads:
            if b.last_w is not None:
                deps.add(b.last_w)
        for b in writes:
            if b.last_w is not None:
                deps.add(b.last_w)
            deps.update(b.readers)
        oid = len(self.ops)
        if eng == "pe":
            deps = {d for d in deps if self.ops[d].eng != "pe" or self.ops[d].is_dma}
        self.ops.append(Op(eng, fn, deps, dma, dsem))
        for b in writes:
            b.last_w = oid
            b.readers = []
        for b in reads:
            b.readers.append(oid)
        return oid

    def emit(self, nc, final_wait_eng="sp"):
        ops = self.ops
        for o in ops:
            latest = {}
            keep = set()
            for d in o.deps:
                p = ops[d]
                if p.is_dma:
                    keep.add(d)
                elif d > latest.get(p.eng, -1):
                    latest[p.eng] = d
            keep.update(latest.values())
            o.deps = keep
            for d in keep:
                ops[d].needed = True
        eng_cnt = {e: 0 for e in ENGS}
        dsem_cnt = {}
        for o in ops:
            if o.is_dma:
                dsem_cnt[o.dsem] = dsem_cnt.get(o.dsem, 0) + 16
                o.inc_val = dsem_cnt[o.dsem]
            elif o.needed:
                eng_cnt[o.eng] += 1
                o.inc_val = eng_cnt[o.eng]
        streams = {e: [] for e in ENGS}
        waited_e = {e: {f: 0 for f in ENGS} for e in ENGS}
        waited_d = {e: {} for e in ENGS}
        for o in ops:
            waits = []
            need_e = {}
            need_d = {}
            for d in o.deps:
                p = ops[d]
                if p.is_dma:
                    need_d[p.dsem] = max(need_d.get(p.dsem, 0), p.inc_val)
                else:
                    need_e[p.eng] = max(need_e.get(p.eng, 0), p.inc_val)
            for f, v in need_e.items():
                if v > waited_e[o.eng][f]:
                    waited_e[o.eng][f] = v
                    waits.append(("e", f, v))
            for s, v in need_d.items():
                if v > waited_d[o.eng].get(s, 0):
                    waited_d[o.eng][s] = v
                    waits.append(("d", s, v))
            streams[o.eng].append((waits, o))
        final_waits = []
        for s, v in dsem_cnt.items():
            if v > waited_d[final_wait_eng].get(s, 0):
                final_waits.append(("d", s, v))
        for f in ENGS:
            if eng_cnt[f] > waited_e[final_wait_eng][f]:
                final_waits.append(("e", f, eng_cnt[f]))

        with ExitStack() as es:
            esem = {e: es.enter_context(nc.semaphore("s_" + e)) for e in ENGS}
            dsems = [es.enter_context(nc.semaphore("d%d" % i)) for i in range(self.n_dsem)]
            block = es.enter_context(nc.Block())

            def run_stream(engname, eng):
                for waits, o in streams[engname]:
                    for kind, key, v in waits:
                        eng.wait_ge(esem[key] if kind == "e" else dsems[key], v)
                    ins = o.fn(eng)
                    if o.is_dma:
                        ins.then_inc(dsems[o.dsem], 16)
                    elif o.needed:
                        ins.then_inc(esem[o.eng], 1)
                if engname == final_wait_eng:
                    for kind, key, v in final_waits:
                        eng.wait_ge(esem[key] if kind == "e" else dsems[key], v)

            @block.tensor
            def _(e):
                run_stream("pe", e)

            @block.scalar
            def _(e):
                run_stream("act", e)

            @block.vector
            def _(e):
                run_stream("dve", e)

            @block.gpsimd
            def _(e):
                run_stream("pool", e)

            @block.sync
            def _(e):
                run_stream("sp", e)
        return eng_cnt


def _rel_bucket_np(dist):
    n = np.maximum(dist, 0)
    max_exact = 16
    nf = np.maximum(n, 1).astype(np.float32)
    large = max_exact + (np.log(nf / np.float32(max_exact)) / np.float32(math.log(128 / max_exact))
                         * np.float32(32 - max_exact)).astype(np.int32)
    large = np.minimum(large, 31)
    return np.where(n < max_exact, n, large)


def _frev():
    f = np.zeros((33, 383), np.float32)
    for y in range(383):
        dist = 255 - y
        if 0 <= dist < 128:
            f[int(_rel_bucket_np(np.array(dist))), y] = 1.0
        else:
            f[32, y] = 1.0
    return f


def _win_perm():
    cg0, xc0, bg0, q0, k0, v0, gt0 = 1024, 2048, 0, 3072, 4096, 4224, 4352
    cols = []
    for m in range(8):
        cols += list(range(cg0 + m * 128, cg0 + (m + 1) * 128))
        cols += list(range(xc0 + m * 128, xc0 + (m + 1) * 128))
    cols += list(range(bg0, bg0 + 1024))
    for c in range(8):
        cols += list(range(q0 + c * 64, q0 + (c + 1) * 64))
        cols += list(range(q0 + (8 + c) * 64, q0 + (9 + c) * 64))
    cols += list(range(gt0, gt0 + 2048))
    cols += list(range(k0, k0 + 128))
    cols += list(range(v0, v0 + 128))
    assert len(cols) == PROJ
    return np.array(cols)


def _attn_row_perm():
    rows = []
    for c in range(8):
        rows += list(range(c * 64, (c + 1) * 64))
        rows += list(range((8 + c) * 64, (9 + c) * 64))
    return np.array(rows)


def _proj_tiles():
    kinds = []
    for m in range(8):
        kinds += [("cg", m), ("xc", m)]
    kinds += [("bg", m) for m in range(8)]
    kinds += [("q", m) for m in range(8)]
    kinds += [("gate", m) for m in range(16)]
    return kinds


class _Stop(Exception):
    pass


def build_program(stop=None):
    nc = bass.Bass("TRN2", target_bir_lowering=False)

    ckstate = {"n": 0}

    def ckpt(n):
        ckstate["n"] += 1
        if stop is not None and ckstate["n"] == stop:
            print("STOP at checkpoint #%d (label %d)" % (stop, n))
            raise _Stop()

    def din(name, shape):
        return nc.dram_tensor(name, list(shape), F32, kind="ExternalInput").ap()

    def dout(name, shape):
        return nc.dram_tensor(name, list(shape), F32, kind="ExternalOutput").ap()

    xp = din("xp", [NPB * 128, D])
    xs = din("xs", [128, D])
    cin = din("cin", [17, D])
    vecs = din("vecs", [208, 128])
    sinkrep = din("sinkrep", [128, 16])
    hmask_d = din("hmask", [128, 1])
    relt = din("relt", [32, 16])
    frev_d = din("frev", [33, 383])
    bdmask_d = din("bdmask", [128, 128])
    ident_d = din("identf", [128, 128])
    w_ada = din("w_ada", [DEPTH, 24, 128, 8 * WCOLS])
    w_in = din("w_in", [DEPTH, 25, 128, 8 * WCOLS])
    w_brc = din("w_brc", [DEPTH, 4, 128, 8 * WCOLS])
    w_bra = din("w_bra", [DEPTH, 4, 128, 8 * WCOLS])
    w_o = din("w_o", [DEPTH, 4, 128, 8 * WCOLS])
    w_ff1 = din("w_ff1", [DEPTH, 16, 128, 8 * WCOLS])
    w_ff2 = din("w_ff2", [DEPTH, 16, 128, 8 * WCOLS])
    sconv = din("sconv", [DEPTH, 32, D])
    ck = din("ck", [DEPTH, NSEQ, 128, 128])
    cv = din("cv", [DEPTH, NSEQ, 128, 128])

    yp = dout("yp", [16 * 128, D])
    ys = dout("ys", [128, D])
    convp = dout("convp", [DEPTH, 2, D])
    kp = dout("kp", [DEPTH, 128, 128])
    vp = dout("vp", [DEPTH, 128, 128])
    convs = dout("convs", [DEPTH, 32, D])
    ks = dout("ks", [DEPTH, NSEQ, 120, 128])
    vs = dout("vs", [DEPTH, NSEQ, 120, 128])
    ksn = dout("ksn", [DEPTH, 128, 128])
    vsn = dout("vsn", [DEPTH, 128, 128])

    S = Sched()
    es = ExitStack()

    def sb(name, shape, dt):
        return es.enter_context(nc.sbuf_tensor(name, list(shape), dt))

    TM = 512
    ident = sb("ident", [128, 128], F32)
    ones_m = sb("ones_m", [128, 128], BF16)
    ones64 = sb("ones64", [128, 64], BF16)
    identb = sb("identb", [128, 128], BF16)
    vecsT = sb("vecsT", [128, 208], F32)
    der = sb("der", [128, DEPTH, 6, 8, 17], F32)
    es_t = sb("es_t", [128, 16], F32)
    hmask = sb("hmask_t", [128, 1], F32)
    epsc = sb("epsc", [128, 1], F32)
    Etab = sb("Etab", [128, 2, 16, 128], BF16)
    Esn = sb("Esn", [128, 16, 128], BF16)
    diag = sb("diag", [128, 24, 128], BF16)
    xT = sb("xT", [128, NCH, TM], F32)
    xtm = sb("xtm", [128, D], F32)
    xo = sb("xo", [128, D], F32)
    s8 = sb("s8", [128, NCH, TM], BF16)
    rstd = sb("rstd", [128, TM], F32)
    rt = sb("rt", [128, TM], F32)
    tmpr = sb("tmpr", [128, 2, TM], F32)
    hb = sb("hb", [128, NCH, TM], BF16)
    big = sb("big", [128, 32, TM], BF16)
    u_p = sb("u_p", [128, NCH, 2 + TM], BF16)
    u_s = sb("u_s", [128, NCH, NSEQ, 10], BF16)
    us_f = sb("us_f", [128, NCH, 128], F32)
    utail = sb("utail", [128, NCH, 2], F32)
    ustl = sb("ustl", [128, NCH, 32], F32)
    ucarry = sb("ucarry", [128, DEPTH, NCH, 2], BF16)
    kT = sb("kT", [128, DEPTH, 128 + TM], BF16)
    kTs = sb("kTs", [128, 128], BF16)
    vtm = sb("vtm", [128, DEPTH, 5, 128], BF16)
    v_s = sb("v_s", [128, 128], BF16)
    kvout = sb("kvout", [128, 256], F32)
    expS = sb("expS", [128, 3, 512], F32)
    PT = sb("PT", [128, 2, 2, 16, 128], BF16)
    rD = sb("rD", [128, NCH, 128], F32)
    cstage = rD[:, 0:4, :]
    mixed = sb("mixed", [128, NCH, TM], F32)
    frev = mixed[0:33, 0, 0:383]
    relx = mixed[0:33, 1, 0:16]
    bdm = mixed[:, 2, 0:128]
    bada_tmp = mixed[:, 3:5, :].rearrange("p a b -> p (a b)")[:, 0:816].rearrange("p (m n) -> p m n", n=17)
    tstage = xo[0:32, :]
    wslot = sb("wslot", [128, NWSLOT, 8 * WCOLS], BF16)
    kcT = sb("kcT", [128, NSEQ, 128], BF16)
    vc = sb("vc", [128, NSEQ, 128], BF16)
    scT = sb("scT", [128, NCH, 17], BF16)
    pall = es.enter_context(nc.psum_tensor("pall", [128, 4096], F32))

    def bank(i, n=1):
        return pall[:, i * 512:(i + n) * 512]

    B_bank = [Buf("bank%d" % i, excl=True) for i in range(8)]
    B_xT = [Buf("xT%d" % c) for c in range(NCH)]
    B_s8 = [Buf("s8_%d" % c) for c in range(NCH)]
    B_hb = [Buf("hb%d" % c) for c in range(NCH)]
    B_big = [Buf("big%d" % c) for c in range(32)]
    B_mixed = [Buf("mixed%d" % c) for c in range(NCH)]
    B_up = [Buf("up%d" % c) for c in range(NCH)]
    B_us = [Buf("us%d" % c) for c in range(NCH)]
    B_usf = [Buf("usf%d" % c) for c in range(NCH)]
    B_utail = Buf("utail")
    B_ucarry = [Buf("ucarry%d" % l) for l in range(DEPTH)]
    B_kT = [Buf("kT%d" % l) for l in range(DEPTH)]
    B_kTs = Buf("kTs")
    B_vtm = [Buf("vtm%d" % l) for l in range(DEPTH)]
    B_vs = Buf("vs")
    B_kvout = Buf("kvout")
    B_exp = [Buf("exp0"), Buf("exp1"), Buf("exp2")]
    B_PT = [[Buf("PT00"), Buf("PT01")], [Buf("PT10"), Buf("PT11")]]
    B_rD = Buf("rD")
    B_cstage = B_rD
    B_rstd = Buf("rstd")
    B_rt = Buf("rt")
    B_tmp = [Buf("tmp0"), Buf("tmp1")]
    B_w = [Buf("w%d" % i) for i in range(NWSLOT)]
    B_xtm_h = [Buf("xtm0"), Buf("xtm1")]
    B_xo = Buf("xo")
    B_const = Buf("const")
    B_vecsT = Buf("vecsT")
    B_der = Buf("der")
    B_E = Buf("E")
    B_diag = Buf("diag")
    B_kcT = Buf("kcT")
    B_vc = Buf("vc")
    B_tstage = B_xo
    B_misc = Buf("misc")
    B_scT = Buf("scT")
    B_dram_out = Buf("dram_out")

    ds_w = [S.new_dsem() for _ in range(NWSLOT)]
    ds_xtm_h = [S.new_dsem(), S.new_dsem()]
    ds_xo = S.new_dsem()
    ds_const = S.new_dsem()
    ds_kvout = S.new_dsem()
    ds_cst = S.new_dsem()
    ds_tst = S.new_dsem()
    ds_vc = S.new_dsem()
    ds_d2d = S.new_dsem()

    state = {"w": 0, "ring": 0, "exp": 0, "tmp": 0, "lb": 0, "sb": 0}

    def ACT(out, in_, func, reads, writes, **kw):
        S.op("act", lambda e: e.activation(out=out, in_=in_, func=func, **kw), reads, writes)

    def TT(out, in0, in1, op, reads, writes, eng="dve"):
        S.op(eng, lambda e: e.tensor_tensor(out=out, in0=in0, in1=in1, op=op), reads, writes)

    def TS(out, in0, s1, s2, op0, op1, reads, writes, eng="dve"):
        if s2 is None:
            S.op(eng, lambda e: e.tensor_scalar(out=out, in0=in0, scalar1=s1, scalar2=None, op0=op0), reads, writes)
        else:
            S.op(eng, lambda e: e.tensor_scalar(out=out, in0=in0, scalar1=s1, scalar2=s2, op0=op0, op1=op1), reads, writes)

    def CP(out, in_, reads, writes, eng="dve"):
        S.op(eng, lambda e: e.tensor_copy(out=out, in_=in_), reads, writes)

    def MM(out, lhsT, rhs, start, stop, reads, writes, skip=False):
        if skip:
            S.op("pe", lambda e: e.matmul(out, lhsT=lhsT, rhs=rhs, start=start, stop=stop, skip_group_check=True), reads, writes)
        else:
            S.op("pe", lambda e: e.matmul(out, lhsT=lhsT, rhs=rhs, start=start, stop=stop), reads, writes)

    def TR(out, in_, idt, reads, writes):
        S.op("pe", lambda e: e.transpose(out, in_, idt), reads, writes)

    def DMA(eng, out, in_, reads, writes, dsem, **kw):
        S.op(eng, lambda e: e.dma_start(out=out, in_=in_, **kw), reads, writes, dma=True, dsem=dsem)

    def next_ring():
        r = state["ring"] % 3
        state["ring"] += 1
        return r

    WSEQ = []
    for l_ in range(DEPTH):
        for sl_ in range(24):
            WSEQ.append((w_ada[l_, sl_], 8, WCOLS))
    for _t in range(5):
        for l_ in range(DEPTH):
            for sl_ in range(25):
                WSEQ.append((w_in[l_, sl_], 8, WCOLS))
            for wsrc in (w_brc, w_bra, w_o):
                for sl_ in range(4):
                    WSEQ.append((wsrc[l_, sl_], 8, WCOLS))
            for sl_ in range(16):
                WSEQ.append((w_ff1[l_, sl_], 8, WCOLS))
            for m_ in range(NCH):
                for kh_ in range(2):
                    WSEQ.append((w_ff2[l_, m_ * 2 + kh_], 16, 128))
    LOOKAHEAD = NWSLOT - 2
    state["wi"] = 0

    def _issue_w(k):
        src_ap, kch, ncols = WSEQ[k]
        sl = k % NWSLOT
        DMA("pool", wslot[:, sl, 0:kch * ncols], src_ap, [], [B_w[sl]], ds_w[sl])

    def load_w(src_ap, kch, ncols):
        k = state["w"]
        state["w"] += 1
        assert k < len(WSEQ) and str(WSEQ[k][0]) == str(src_ap) and WSEQ[k][1] == kch, ("weight order mismatch", k)
        while state["wi"] < len(WSEQ) and state["wi"] <= k + LOOKAHEAD:
            _issue_w(state["wi"])
            state["wi"] += 1
        sl = k % NWSLOT
        return sl, wslot[:, sl, 0:kch * ncols].rearrange("p (c e) -> p c e", c=kch)

    try:
        DMA("sp", ident[:], ident_d, [], [B_const], ds_const)
        DMA("sp", hmask[:], hmask_d, [], [B_const], ds_const)
        DMA("sp", es_t[:], sinkrep, [], [B_const], ds_const)
        DMA("sp", frev[:], frev_d, [], [B_const, B_mixed[0]], ds_const)
        DMA("sp", bdm[:], bdmask_d, [], [B_const, B_mixed[2]], ds_const)
        S.op("dve", lambda e: e.memset(ones_m[:], 1.0 / 1024.0), [], [B_const])
        S.op("dve", lambda e: e.memset(ones64[:], 1.0), [], [B_const])
        S.op("dve", lambda e: e.memset(epsc[:], EPS), [], [B_const])
        S.op("dve", lambda e: e.memset(relx[:], -1e30), [], [B_misc, B_mixed[1]])
        DMA("sp", relx[0:32, :], relt, [], [B_misc, B_mixed[1]], ds_const)
        CP(identb[:], ident[:], [B_const], [B_const])
        ACT(es_t[:], es_t[:], AF.Exp, [B_const], [B_const])

        for hlf in range(2):
            DMA("sp", xtm[0:104, 0:128], vecs[hlf * 104:(hlf + 1) * 104, :], [], B_xtm_h, ds_xtm_h[0])
            TR(bank(3)[:, 0:104], xtm[0:104, 0:128], ident[0:104, 0:104], B_xtm_h + [B_const], [B_bank[3]])
            CP(vecsT[:, hlf * 104:(hlf + 1) * 104], bank(3)[:, 0:104], [B_bank[3]], [B_vecsT])

        DMA("sp", xtm[0:17, :], cin, [], B_xtm_h, ds_xtm_h[0])
        for c in range(NCH):
            TR(bank(3)[:, c * 17:(c + 1) * 17], xtm[0:17, c * 128:(c + 1) * 128], ident[0:17, 0:17], B_xtm_h + [B_const], [B_bank[3]])
        ACT(scT[:].rearrange("p c n -> p (c n)"), bank(3)[:, 0:NCH * 17], AF.Silu, [B_bank[3]], [B_scT])

        ckpt(0)
        tiles = [(0, 4, False), (4, 4, False), (8, 4, False), (12, 4, False), (16, 2, True)]
        def tile_srcs(ti_):
            b0_, nbp_, has_s_ = tiles[ti_]
            return [xp[(b0_ + j) * 128:(b0_ + j + 1) * 128, :] for j in range(nbp_)] + ([xs] if has_s_ else [])

        xpre = set()

        def sample_prep(l):
            for g4 in range(4):
                DMA("sp", cstage[:], ck[l, g4 * 4:(g4 + 1) * 4].rearrange("s p d -> p s d"), [], [B_cstage], ds_cst)
                for sl in range(4):
                    TR(bank(3)[:, sl * 128:(sl + 1) * 128], cstage[:, sl, :], ident[:], [B_cstage, B_const], [B_bank[3]])
                CP(kcT[:, g4 * 4:(g4 + 1) * 4, :], bank(3).rearrange("p (s t) -> p s t", s=4), [B_bank[3]], [B_kcT])
            DMA("pool", vc[:], cv[l].rearrange("s p d -> p s d"), [], [B_vc], ds_vc)
            DMA("sp", tstage[:], sconv[l], [], [B_tstage], ds_tst)
            for c in range(NCH):
                TR(bank(3)[:, c * 32:(c + 1) * 32], tstage[:, c * 128:(c + 1) * 128], ident[0:32, 0:32], [B_tstage, B_const], [B_bank[3]])
            CP(u_s[:, :, :, 0:2], bank(3)[:, 0:256].rearrange("p (c s r) -> p c s r", c=NCH, r=2), [B_bank[3]], B_us)

        def stage_of(j, hf):
            if j % 2 == 0:
                return xtm, [B_xtm_h[hf]], ds_xtm_h[hf]
            return xo, [B_xo], ds_xo

        def xload(tix_):
            srcs = tile_srcs(tix_)
            for j, src in enumerate(srcs):
                for hf in range(2):
                    st, sbufs, sds = stage_of(j, hf)
                    if (tix_, j) not in xpre:
                        DMA("sp", st[:, hf * 512:(hf + 1) * 512], src[:, hf * 512:(hf + 1) * 512], [], sbufs, sds)
                    rb = state["lb"] % 4
                    state["lb"] += 1
                    for cc in range(4):
                        c = hf * 4 + cc
                        TR(bank(rb)[:, cc * 128:(cc + 1) * 128], st[:, c * 128:(c + 1) * 128], ident[:], sbufs + [B_const], [B_bank[rb]])
                    dstv = xT[:, hf * 4:(hf + 1) * 4, j * 128:(j + 1) * 128]
                    if hf == 0:
                        CP(dstv, bank(rb).rearrange("p (c t) -> p c t", c=4), [B_bank[rb]], B_xT[hf * 4:(hf + 1) * 4])
                    else:
                        ACT(dstv, bank(rb).rearrange("p (c t) -> p c t", c=4), AF.Copy, [B_bank[rb]], B_xT[hf * 4:(hf + 1) * 4])

        xload(0)
        sample_prep(0)

        pmod = pall[:, 4 * 512:7 * 512].rearrange("p (m n) -> p m n", n=32)

        def mod_slot(l, sl):
            s, wv = load_w(w_ada[l, sl], 8, WCOLS)
            for mt in range(2):
                m = sl * 2 + mt
                for c in range(NCH):
                    MM(pmod[:, m, 0:17], wv[:, c, mt * 128:(mt + 1) * 128], scT[:, c, :], c == 0, c == NCH - 1,
                       [B_w[s], B_scT], [B_bank[4 + m // 16]])

        def mod_finish(l):
            base = l * 104
            bada = vecsT[:, base + 56:base + 104]
            TT(bada_tmp[:], pmod[:, :, 0:17], bada.unsqueeze(2).to_broadcast([128, 48, 17]), ALU.add,
               [B_bank[4], B_bank[5], B_bank[6], B_vecsT], [B_misc, B_mixed[3], B_mixed[4]])
            for k, (gofs, scofs, mode) in enumerate([(0, 8, "a"), (None, 0, "b"), (8, 16, "g"), (16, 32, "a"), (None, 24, "b"), (24, 40, "g")]):
                dst = der[:, l, k]
                src = bada_tmp[:, scofs:scofs + 8, :]
                if mode == "b":
                    CP(dst, src, [B_misc, B_mixed[3], B_mixed[4]], [B_der])
                else:
                    gvec = vecsT[:, base + gofs:base + gofs + 8].unsqueeze(2).to_broadcast([128, 8, 17])
                    if mode == "a":
                        TS(dst, src, 1.0, None, ALU.add, None, [B_misc, B_mixed[3], B_mixed[4]], [B_der])
                        TT(dst, dst, gvec, ALU.mult, [B_der, B_vecsT], [B_der])
                    else:
                        TT(dst, src, gvec, ALU.mult, [B_misc, B_mixed[3], B_mixed[4], B_vecsT], [B_der])

        frevb0 = s8[0:33, 0, 0:382]
        frevb1 = s8[0:33, 1, 0:382]
        relh = s8[0:33, 2, 0:16]
        rell = s8[0:33, 2, 16:32]
        CP(frevb0, frev[:, 0:382], [B_const, B_mixed[0]], [B_s8[0]])
        CP(frevb1, frev[:, 1:383], [B_const, B_mixed[0]], [B_s8[1]])
        CP(relh, relx[:], [B_misc, B_mixed[1]], [B_s8[2]])
        TT(rell, relx[:], relh, ALU.subtract, [B_misc, B_mixed[1], B_s8[2]], [B_s8[2]])
        e_jobs = []
        for kb, off in ((0, 127), (1, 255)):
            for i in range(128):
                e_jobs.append((kb, off, i))

        def e_build(n):
            for _ in range(n):
                if not e_jobs:
                    return
                kb, off, i = e_jobs.pop(0)
                pE = pall[:, 0:2048].rearrange("p (i h) -> p i h", h=16)
                o = off - i
                lw = frevb0[:, o:o + 128] if o % 2 == 0 else frevb1[:, o - 1:o - 1 + 128]
                rds = [B_s8[0], B_s8[1], B_s8[2]]
                MM(pE[:, i, :], lw, relh, True, False, rds, [B_bank[i // 32]])
                MM(pE[:, i, :], lw, rell, False, True, rds, [B_bank[i // 32]])
                if i == 127:
                    ACT(Etab[:, kb].rearrange("p h i -> p i h"), pE, AF.Exp, [B_bank[j] for j in range(4)], [B_E])

        for sl in range(24):
            mod_slot(0, sl)
            e_build(6)
        mod_finish(0)
        for sl in range(24):
            mod_slot(1, sl)
            e_build(6)
        e_build(256)
        mod_finish(1)
        TT(Esn[:], Etab[:, 1], bdm[:].unsqueeze(1).to_broadcast([128, 16, 128]), ALU.mult, [B_E, B_const, B_mixed[2]], [B_E])

        ckpt(2)
        S.op("dve", lambda e: e.memset(ucarry[:], 0.0), [], B_ucarry)
        S.op("dve", lambda e: e.memset(kT[:, :, 0:128], 0.0), [], B_kT)
        S.op("dve", lambda e: e.memset(vtm[:, :, 0, :], 0.0), [], B_vtm)

        LO = {"v": 0}
        def norm_stats(T):
            for c in range(NCH):
                MM(bank(3)[:, LO["v"]:T], ones_m[:], s8[:, c, LO["v"]:T], c == 0, c == NCH - 1, [B_const, B_s8[c]], [B_bank[3]])
            ACT(rt[:, LO["v"]:T], bank(3)[:, LO["v"]:T], AF.Ln, [B_bank[3], B_const], [B_rt], bias=epsc[:, 0:1])
            ACT(rstd[:, LO["v"]:T], rt[:, LO["v"]:T], AF.Exp, [B_rt], [B_rstd], scale=-0.5)

        def pre_norm(l, T, Tp, ka, kbb, squares_done=False):
            if not squares_done:
                for c in range(NCH):
                    ACT(s8[:, c, LO["v"]:T], xT[:, c, LO["v"]:T], AF.Square, [B_xT[c]], [B_s8[c]])
            norm_stats(T)
            for c in range(NCH):
                ti = state["tmp"] % 2
                state["tmp"] += 1
                tv = tmpr[:, ti, LO["v"]:T]
                eng = "pool" if c in (1, 4, 6) else "dve"
                TT(tv, xT[:, c, LO["v"]:T], rstd[:, LO["v"]:T], ALU.mult, [B_xT[c], B_rstd], [B_tmp[ti]], eng=eng)
                if Tp > 0:
                    ACT(hb[:, c, LO["v"]:Tp], tmpr[:, ti, LO["v"]:Tp], AF.Identity, [B_tmp[ti], B_der], [B_hb[c]],
                        scale=der[:, l, ka, c, 0:1], bias=der[:, l, kbb, c, 0:1])
                if T > Tp:
                    sv = tmpr[:, ti, Tp:T].rearrange("p (s i) -> p s i", i=8)
                    TT(sv, sv, der[:, l, ka, c, 1:17].unsqueeze(2).to_broadcast([128, 16, 8]), ALU.mult,
                       [B_tmp[ti], B_der], [B_tmp[ti]])
                    TT(hb[:, c, Tp:T].rearrange("p (s i) -> p s i", i=8), sv,
                       der[:, l, kbb, c, 1:17].unsqueeze(2).to_broadcast([128, 16, 8]), ALU.add,
                       [B_tmp[ti], B_der], [B_hb[c]])

        def post_norm_residual(l, T, Tp, kg, squares_after=True, into_mixed=False):
            norm_stats(T)
            for c in range(NCH):
                eng = "pool" if c in (1, 4, 6) else "dve"
                TT(mixed[:, c, LO["v"]:T], mixed[:, c, LO["v"]:T], rstd[:, LO["v"]:T], ALU.mult, [B_mixed[c], B_rstd], [B_mixed[c]], eng=eng)
                if into_mixed:
                    TT(mixed[:, c, LO["v"]:T], mixed[:, c, LO["v"]:T], xT[:, c, LO["v"]:T], ALU.add, [B_xT[c], B_mixed[c]], [B_mixed[c]], eng=eng)
                    continue
                TT(xT[:, c, LO["v"]:T], xT[:, c, LO["v"]:T], mixed[:, c, LO["v"]:T], ALU.add, [B_xT[c], B_mixed[c]], [B_xT[c]], eng=eng)
                if squares_after:
                    ACT(s8[:, c, LO["v"]:T], xT[:, c, LO["v"]:T], AF.Square, [B_xT[c]], [B_s8[c]])

        def evac_scaled(l, m, rb, T, Tp, kg):
            pv = bank(rb)
            ACT(s8[:, m, LO["v"]:T], pv[:, LO["v"]:T], AF.Square, [B_bank[rb]], [B_s8[m]])
            if Tp > 0:
                ACT(mixed[:, m, LO["v"]:Tp], pv[:, LO["v"]:Tp], AF.Identity, [B_bank[rb], B_der], [B_mixed[m]],
                    scale=der[:, l, kg, m, 0:1])
            if T > Tp:
                TT(mixed[:, m, Tp:T].rearrange("p (s i) -> p s i", i=8), pv[:, Tp:T].rearrange("p (s i) -> p s i", i=8),
                   der[:, l, kg, m, 1:17].unsqueeze(2).to_broadcast([128, 16, 8]), ALU.mult,
                   [B_bank[rb], B_der], [B_mixed[m]])

        def fm_tile(wv, mt, s, rhs_fn, rhs_bufs, T, nk=NCH, first=True, last=True, rb=None, kofs=0):
            if rb is None:
                rb = next_ring()
            for c in range(nk):
                MM(bank(rb)[:, LO["v"]:T], wv[:, c, mt * 128:(mt + 1) * 128], rhs_fn(kofs + c), first and c == 0, last and c == nk - 1,
                   [B_w[s]] + rhs_bufs(kofs + c), [B_bank[rb]])
            return rb

        def fm_group(wvs, ss, mts, rhs_fn, rhs_bufs, T, banks):
            for c in range(NCH):
                for g in range(len(mts)):
                    MM(bank(banks[g])[:, LO["v"]:T], wvs[g][:, c, mts[g] * 128:(mts[g] + 1) * 128], rhs_fn(c), c == 0, c == NCH - 1,
                       [B_w[ss[g]]] + rhs_bufs(c), [B_bank[banks[g]]])

        pending = []

        def make_store(j, is_s, gblk):
            def fn(alt=False):
                if alt:
                    stg, sb_, sds = xtm, list(B_xtm_h), ds_xtm_h[0]
                else:
                    stg, sb_, sds = xo, [B_xo], ds_xo
                for hf in range(2):
                    rb = 4 + state["sb"] % 4
                    state["sb"] += 1
                    for cc in range(4):
                        c = hf * 4 + cc
                        TR(bank(rb)[:, cc * 128:(cc + 1) * 128], mixed[:, c, j * 128:(j + 1) * 128], ident[:], [B_mixed[c], B_const],
                           [B_bank[rb]])
                    if hf == 0:
                        CP(stg[:, 0:512], bank(rb), [B_bank[rb]], sb_)
                    else:
                        ACT(stg[:, 512:1024], bank(rb), AF.Copy, [B_bank[rb]], sb_)
                dst = ys if is_s else yp[(gblk - 2) * 128:(gblk - 1) * 128, :]
                DMA("sp", dst, stg[:], sb_, [B_dram_out], sds)
            return fn

        for tix, (b0, nbp, has_s) in enumerate(tiles):
            Tp = nbp * 128
            T = Tp + (128 if has_s else 0)
            if tix > 0:
                xload(tix)
            ckpt(3)
            for l in range(DEPTH):
                base = l * 104
                if b0 == 0:
                    lo_u, lo_m = (0, 128) if l == 0 else (128, 256)
                else:
                    lo_u, lo_m = 0, 0
                LO["v"] = lo_u
                if l == 1 and tix + 1 < len(tiles):
                    for jn in range(2):
                        nsrc = tile_srcs(tix + 1)[jn]
                        for hf in range(2):
                            st, sbufs, sds = stage_of(jn, hf)
                            DMA("sp", st[:, hf * 512:(hf + 1) * 512], nsrc[:, hf * 512:(hf + 1) * 512], [], sbufs, sds)
                        xpre.add((tix + 1, jn))
                if tix == 2:
                    DMA("sp", ks[l], ck[l][:, 8:128, :], [], [B_dram_out], ds_d2d)
                    DMA("sp", vs[l], cv[l][:, 8:128, :], [], [B_dram_out], ds_d2d)
                CP(u_p[:, :, 0:2], ucarry[:, l], [B_ucarry[l]], B_up)
                for j in range(3):
                    for c in range(NCH):
                        TS(diag[:, j * 8 + c, :], identb[:], vecsT[:, base + 32 + j * 8 + c:base + 33 + j * 8 + c], None, ALU.mult, None,
                           [B_const, B_vecsT], [B_diag])

                pre_norm(l, T, Tp, 0, 1, squares_done=(l > 0))

                ckpt(4)
                hfn = lambda c: hb[:, c, LO["v"]:T]
                hbufs = lambda c: [B_hb[c]]
                kinds = _proj_tiles()
                cgst = {}

                def proj_evac(kind, m, rb):
                    pv = bank(rb)
                    if kind == "cg":
                        ti = state["tmp"] % 2
                        state["tmp"] += 1
                        cgst[m] = ti
                        ACT(tmpr[:, ti, LO["v"]:T], pv[:, LO["v"]:T], AF.Copy, [B_bank[rb]], [B_tmp[ti]])
                    elif kind == "xc":
                        ti = cgst[m]
                        TT(u_p[:, m, 2 + LO["v"]:2 + Tp], pv[:, LO["v"]:Tp], tmpr[:, ti, LO["v"]:Tp], ALU.mult, [B_bank[rb], B_tmp[ti]], [B_up[m]])
                        if b0 + nbp == NPB:
                            TT(utail[:, m, :], pv[:, Tp - 2:Tp], tmpr[:, ti, Tp - 2:Tp], ALU.mult, [B_bank[rb], B_tmp[ti]], [B_utail])
                        if has_s:
                            TT(us_f[:, m, :], pv[:, Tp:T], tmpr[:, ti, Tp:T], ALU.mult, [B_bank[rb], B_tmp[ti]], [B_usf[m]])
                            CP(u_s[:, m, :, 2:10], us_f[:, m, :].rearrange("p (s i) -> p s i", i=8), [B_usf[m]], [B_us[m]])
                            CP(ustl[:, m, :].rearrange("p (s r) -> p s r", r=2),
                               us_f[:, m, :].rearrange("p (s i) -> p s i", i=8)[:, :, 6:8], [B_usf[m]], [B_utail])
                    elif kind == "bg":
                        ACT(big[:, 16 + m, LO["v"]:T], pv[:, LO["v"]:T], AF.Copy, [B_bank[rb]], [B_big[16 + m]])
                    elif kind == "q":
                        ACT(big[:, 24 + m, LO["v"]:T], pv[:, LO["v"]:T], AF.Copy, [B_bank[rb]], [B_big[24 + m]], scale=0.125)
                    else:
                        ACT(big[:, m, LO["v"]:T], pv[:, LO["v"]:T], AF.Sigmoid, [B_bank[rb]], [B_big[m]])

                LO["v"] = lo_u
                s0, wv0 = load_w(w_in[l, 0], 8, WCOLS)
                s1, wv1 = load_w(w_in[l, 1], 8, WCOLS)
                hbanks = [next_ring(), next_ring(), next_ring(), 4]
                fm_group([wv0, wv0, wv1, wv1], [s0, s0, s1, s1], [0, 1, 0, 1], hfn, hbufs, T, hbanks)
                for g in range(4):
                    proj_evac(kinds[g][0], kinds[g][1], hbanks[g])
                for sl in range(2, 24):
                    s, wv = load_w(w_in[l, sl], 8, WCOLS)
                    if pending and sl in (3, 8, 13, 18):
                        pending.pop(0)()
                    for mt in range(2):
                        kind, m = kinds[sl * 2 + mt]
                        LO["v"] = lo_u if kind in ("cg", "xc") else lo_m
                        rb = fm_tile(wv, mt, s, hfn, hbufs, T)
                        proj_evac(kind, m, rb)
                ckpt(41)
                while pending:
                    pending.pop(0)()
                s, wv = load_w(w_in[l, 24], 8, WCOLS)
                LO["v"] = lo_u
                rb = fm_tile(wv, 0, s, hfn, hbufs, T)
                ACT(kT[:, l, 128 + LO["v"]:128 + Tp], bank(rb)[:, LO["v"]:Tp], AF.Copy, [B_bank[rb]], [B_kT[l]])
                if has_s:
                    ACT(kTs[:], bank(rb)[:, Tp:T], AF.Copy, [B_bank[rb]], [B_kTs])
                ckpt(42)
                nblk = nbp + (1 if has_s else 0)
                for j in range(nblk):
                    is_s = j >= nbp
                    if j * 128 < lo_u:
                        continue
                    need_k = is_s or (b0 + j == NPB - 1)
                    c0 = 0 if need_k else 128
                    for c in range(NCH):
                        MM(bank(3)[:, c0:256], hb[:, c, j * 128:(j + 1) * 128], wv[:, c, c0:256], c == 0, c == NCH - 1,
                           [B_hb[c], B_w[s]], [B_bank[3]])
                    if is_s:
                        CP(v_s[:], bank(3)[:, 128:256], [B_bank[3]], [B_vs])
                    else:
                        CP(vtm[:, l, 1 + j, :], bank(3)[:, 128:256], [B_bank[3]], [B_vtm[l]])
                    if is_s or (b0 + j == NPB - 1):
                        ACT(kvout[:], bank(3)[:, 0:256], AF.Copy, [B_bank[3]], [B_kvout])
                        if is_s:
                            DMA("sp", ksn[l], kvout[:, 0:128], [B_kvout], [B_dram_out], ds_kvout)
                            DMA("sp", vsn[l], kvout[:, 128:256], [B_kvout], [B_dram_out], ds_kvout)
                        else:
                            DMA("sp", kp[l], kvout[:, 0:128], [B_kvout], [B_dram_out], ds_kvout)
                            DMA("sp", vp[l], kvout[:, 128:256], [B_kvout], [B_dram_out], ds_kvout)

                ckpt(5)
                LO["v"] = lo_m

                def conv_stage():
                  if True:
                    if b0 == 0:
                        TS(u_p[:, :, 2 + 254:2 + 256], u_p[:, :, 2 + 254:2 + 256], hmask[:, 0:1], None, ALU.mult, None,
                           B_up + [B_const], B_up)
                    for m in range(NCH):
                        rb = next_ring()
                        for j in range(3):
                            MM(bank(rb)[:, LO["v"]:Tp], diag[:, j * 8 + m, :], u_p[:, m, LO["v"] + j:j + Tp], j == 0, j == 2, [B_diag, B_up[m]], [B_bank[rb]])
                        if has_s:
                            for j in range(3):
                                MM(bank(rb)[:, Tp:T].rearrange("p (s i) -> p s i", i=8), diag[:, j * 8 + m, :], u_s[:, m, :, j:j + 8],
                                   j == 0, j == 2, [B_diag, B_us[m]], [B_bank[rb]])
                        TT(s8[:, m, LO["v"]:T], bank(rb)[:, LO["v"]:T], big[:, 16 + m, LO["v"]:T], ALU.mult, [B_bank[rb], B_big[16 + m]], [B_s8[m]])
                    CP(ucarry[:, l], u_p[:, :, Tp:Tp + 2], B_up, [B_ucarry[l]])
                ckpt(6)
                pD = pall[:, 6 * 512:8 * 512].rearrange("p (c q) -> p c q", c=8)
                pO = pall[:, 4 * 512:6 * 512].rearrange("p (c q) -> p c q", c=8)

                def att_A(j):
                    is_s = j >= nbp
                    q0 = j * 128
                    pb = j % 2
                    for kb in range(2):
                        for g in range(2):
                            for quad in range(2):
                                rb = next_ring()
                                hs = 8 * g + 4 * quad
                                qv = big[g * 64:(g + 1) * 64, 24 + 4 * quad:24 + 4 * quad + 4, q0:q0 + 128]
                                qb = B_big[24 + 4 * quad:24 + 4 * quad + 4]
                                pv = bank(rb)
                                if not is_s:
                                    kcols = slice(j * 128 + kb * 128, j * 128 + kb * 128 + 128)
                                    MM(pv, kT[g * 64:(g + 1) * 64, l, kcols], qv, True, True, [B_kT[l]] + qb, [B_bank[rb]])
                                    ein = Etab[:, kb, hs:hs + 4, :]
                                elif kb == 1:
                                    MM(pv, kTs[g * 64:(g + 1) * 64, :], qv, True, True, [B_kTs] + qb, [B_bank[rb]])
                                    ein = Esn[:, hs:hs + 4, :]
                                else:
                                    for sq in range(NSEQ):
                                        MM(pv[:, sq * 32:(sq + 1) * 32], kcT[g * 64:(g + 1) * 64, sq, :],
                                           big[g * 64:(g + 1) * 64, 24 + 4 * quad:24 + 4 * quad + 4, q0 + sq * 8:q0 + sq * 8 + 8],
                                           True, True, [B_kcT] + qb, [B_bank[rb]], skip=True)
                                    ein = Etab[:, 0, hs:hs + 4, 0:8].unsqueeze(1).to_broadcast([128, 16, 4, 8])
                                xi = state["exp"] % 3
                                state["exp"] += 1
                                ACT(expS[:, xi, :], pv, AF.Exp, [B_bank[rb]], [B_exp[xi]])
                                meng = "pool" if quad == 1 else "dve"
                                if is_s and kb == 0:
                                    TT(PT[:, pb, kb, hs:hs + 4, :].rearrange("p h (s i) -> p s h i", i=8),
                                       expS[:, xi, :].rearrange("p (s h i) -> p s h i", h=4, i=8), ein, ALU.mult,
                                       [B_exp[xi], B_E], [B_PT[pb][kb]], eng=meng)
                                else:
                                    TT(PT[:, pb, kb, hs:hs + 4, :], expS[:, xi, :].rearrange("p (h q) -> p h q", h=4), ein, ALU.mult,
                                       [B_exp[xi], B_E], [B_PT[pb][kb]], eng=meng)
                    if (not is_s) and (b0 + j) == 2:
                        TS(PT[:, pb, 0], PT[:, pb, 0], hmask[:, 0:1], None, ALU.mult, None, [B_PT[pb][0], B_const], [B_PT[pb][0]])

                def att_B(j):
                    is_s = j >= nbp
                    q0 = j * 128
                    pb = j % 2
                    for g in range(2):
                        for quad in range(2):
                            hs = 8 * g + 4 * quad
                            for kb in range(2):
                                MM(pD[g * 64:(g + 1) * 64, 4 * quad:4 * quad + 4, :], ones64[:], PT[:, pb, kb, hs:hs + 4, :], kb == 0, kb == 1,
                                   [B_const, B_PT[pb][kb]], [B_bank[6 + quad]])
                    if not is_s:
                        for g in range(2):
                            for c in range(8):
                                for kb in range(2):
                                    MM(pO[g * 64:(g + 1) * 64, c, :], vtm[:, l, j + kb, g * 64:(g + 1) * 64], PT[:, pb, kb, 8 * g + c, :],
                                       kb == 0, kb == 1, [B_vtm[l], B_PT[pb][kb]], [B_bank[4 + c // 4]])
                    else:
                        for g in range(2):
                            for c in range(8):
                                MM(pO[g * 64:(g + 1) * 64, c, :], v_s[:, g * 64:(g + 1) * 64], PT[:, pb, 1, 8 * g + c, :],
                                   True, True, [B_vs, B_PT[pb][1]], [B_bank[4 + c // 4]])
                        for quad in range(2):
                            for g in range(2):
                                for sq in range(NSEQ):
                                    MM(bank(quad)[g * 64:(g + 1) * 64, sq * 32:(sq + 1) * 32],
                                       vc[:, sq, g * 64:(g + 1) * 64],
                                       PT[:, pb, 0, 8 * g + 4 * quad:8 * g + 4 * quad + 4, sq * 8:(sq + 1) * 8],
                                       True, True, [B_vc, B_PT[pb][0]], [B_bank[quad]], skip=True)
                            ACT(mixed[:, 4 * quad:4 * quad + 4, 0:128].rearrange("p c (s i) -> p s c i", i=8),
                                bank(quad).rearrange("p (s c i) -> p s c i", c=4, i=8), AF.Copy, [B_bank[quad]],
                                B_mixed[4 * quad:4 * quad + 4])
                    TT(rD[:], pD, es_t[:, l * 8:(l + 1) * 8].unsqueeze(2).to_broadcast([128, 8, 128]), ALU.add,
                       [B_bank[6], B_bank[7], B_const], [B_rD])
                    ACT(rD[:], rD[:], AF.Ln, [B_rD], [B_rD])
                    ACT(rD[:], rD[:], AF.Exp, [B_rD], [B_rD], scale=-1.0)
                    if not is_s:
                        TT(hb[:, :, q0:q0 + 128], pO, rD[:], ALU.mult, [B_bank[4], B_bank[5], B_rD], B_hb)
                    else:
                        TT(mixed[:, :, 0:128], mixed[:, :, 0:128], pO, ALU.add, B_mixed + [B_bank[4], B_bank[5]], B_mixed)
                        TT(hb[:, :, q0:q0 + 128], mixed[:, :, 0:128], rD[:], ALU.mult, B_mixed + [B_rD], B_hb)

                brc_state = {"m": 0, "wv": None, "s": None}

                def brc_tiles(n):
                    for _ in range(n):
                        m = brc_state["m"]
                        if m >= NCH:
                            return
                        if m % 2 == 0:
                            brc_state["s"], brc_state["wv"] = load_w(w_brc[l, m // 2], 8, WCOLS)
                        rb = fm_tile(brc_state["wv"], m % 2, brc_state["s"], lambda c: s8[:, c, LO["v"]:T], lambda c: [B_s8[c]], T)
                        TT(big[:, 16 + m, LO["v"]:T], bank(rb)[:, LO["v"]:T], big[:, m, LO["v"]:T], ALU.mult, [B_bank[rb], B_big[m]], [B_big[16 + m]])
                        brc_state["m"] = m + 1

                ablk = [j for j in range(nblk) if j * 128 >= lo_m]
                per = NCH // (len(ablk) + 1)
                att_A(ablk[0])
                conv_stage()
                for ii, j in enumerate(ablk):
                    if ii + 1 < len(ablk):
                        att_A(ablk[ii + 1])
                    brc_tiles(per)
                    att_B(j)
                brc_tiles(NCH)
                if not has_s:
                    CP(kT[:, l, 0:128], kT[:, l, Tp:Tp + 128], [B_kT[l]], [B_kT[l]])
                    CP(vtm[:, l, 0, :], vtm[:, l, nbp, :], [B_vtm[l]], [B_vtm[l]])

                ckpt(7)
                if has_s and l == 0:
                    sample_prep(1)
                for half in range(4):
                    s, wv = load_w(w_bra[l, half], 8, WCOLS)
                    for mt in range(2):
                        m = half * 2 + mt
                        rb = fm_tile(wv, mt, s, lambda c: hb[:, c, LO["v"]:T], lambda c: [B_hb[c]], T)
                        TT(big[:, 24 + m, LO["v"]:T], bank(rb)[:, LO["v"]:T], big[:, 8 + m, LO["v"]:T], ALU.mult, [B_bank[rb], B_big[8 + m]], [B_big[24 + m]])
                        TT(big[:, 16 + m, LO["v"]:T], big[:, 16 + m, LO["v"]:T], big[:, 24 + m, LO["v"]:T], ALU.add, [B_big[16 + m], B_big[24 + m]],
                           [B_big[16 + m]], eng="pool")
                for half in range(4):
                    s, wv = load_w(w_o[l, half], 8, WCOLS)
                    for mt in range(2):
                        m = half * 2 + mt
                        rb = fm_tile(wv, mt, s, lambda c: big[:, 16 + c, LO["v"]:T], lambda c: [B_big[16 + c]], T)
                        evac_scaled(l, m, rb, T, Tp, 2)
                post_norm_residual(l, T, Tp, 2, squares_after=True)

                ckpt(8)
                pre_norm(l, T, Tp, 3, 4, squares_done=True)
                def ff1_evac(jx, rb):
                    ACT(big[:, jx, LO["v"]:T], bank(rb)[:, LO["v"]:T], AF.Relu, [B_bank[rb]], [B_big[jx]])
                    TT(big[:, jx, LO["v"]:T], big[:, jx, LO["v"]:T], big[:, jx, LO["v"]:T], ALU.mult, [B_big[jx]], [B_big[jx]],
                       eng=("pool" if jx % 2 else "dve"))

                s0, wv0 = load_w(w_ff1[l, 0], 8, WCOLS)
                s1, wv1 = load_w(w_ff1[l, 1], 8, WCOLS)
                hbanks = [next_ring(), next_ring(), next_ring(), 4]
                fm_group([wv0, wv0, wv1, wv1], [s0, s0, s1, s1], [0, 1, 0, 1], lambda c: hb[:, c, LO["v"]:T], lambda c: [B_hb[c]], T, hbanks)
                for g in range(4):
                    ff1_evac(g, hbanks[g])
                for sl in range(2, 16):
                    s, wv = load_w(w_ff1[l, sl], 8, WCOLS)
                    for mt in range(2):
                        jx = sl * 2 + mt
                        rb = fm_tile(wv, mt, s, lambda c: hb[:, c, LO["v"]:T], lambda c: [B_hb[c]], T)
                        ff1_evac(jx, rb)
                for m in range(NCH):
                    rb = next_ring()
                    for kh in range(2):
                        s, wv = load_w(w_ff2[l, m * 2 + kh], 16, 128)
                        fm_tile(wv, 0, s, lambda c: big[:, c, LO["v"]:T], lambda c: [B_big[c]], T, nk=16, first=(kh == 0), last=(kh == 1),
                                rb=rb, kofs=kh * 16)
                    evac_scaled(l, m, rb, T, Tp, 5)
                post_norm_residual(l, T, Tp, 5, squares_after=(l + 1 < DEPTH), into_mixed=(l + 1 == DEPTH))

                ckpt(9)
                if b0 + nbp == NPB:
                    for hf in range(2):
                        for cc in range(4):
                            c = hf * 4 + cc
                            TR(bank(3)[0:2, cc * 128:(cc + 1) * 128], utail[:, c, :], ident[:], [B_utail, B_const], [B_bank[3]])
                        CP(tstage[0:2, hf * 512:(hf + 1) * 512], bank(3)[0:2, :], [B_bank[3]], [B_tstage])
                    DMA("sp", convp[l], tstage[0:2, :], [B_tstage], [B_dram_out], ds_tst)
                if has_s:
                    for hf in range(2):
                        for cc in range(4):
                            c = hf * 4 + cc
                            TR(bank(3)[0:32, cc * 128:(cc + 1) * 128],
                               ustl[:, c, :], ident[:], [B_utail, B_const], [B_bank[3]])
                        CP(tstage[:, hf * 512:(hf + 1) * 512], bank(3)[0:32, :], [B_bank[3]], [B_tstage])
                    DMA("sp", convs[l], tstage[:], [B_tstage], [B_dram_out], ds_tst)

            for j in range(nbp + (1 if has_s else 0)):
                is_s = j >= nbp
                gblk = b0 + j
                if (not is_s) and gblk < 2:
                    continue
                pending.append(make_store(j, is_s, gblk))
        fi = 0
        while pending:
            pending.pop(0)(alt=(fi % 2 == 1))
            fi += 1
    except _Stop:
        pass
    S.emit(nc)
    es.close()
    return nc


_CACHE = {}


def _prep(x_prompt, x_sample, c_prompt, c_sample, state_conv, cache_k, cache_v,
          w_ada, b_ada, g_pre1, w_in, conv_w, w_br_conv, w_br_attn, w_o, sinks,
          g_post1, g_pre2, w_ff1, w_ff2, g_post2, rel_table):
    f = lambda a: np.ascontiguousarray(np.asarray(a, dtype=np.float32))
    x_prompt, x_sample, c_prompt, c_sample = f(x_prompt), f(x_sample), f(c_prompt), f(c_sample)
    state_conv, cache_k, cache_v = f(state_conv), f(cache_k), f(cache_v)
    w_ada, b_ada, g_pre1, w_in, conv_w = f(w_ada), f(b_ada), f(g_pre1), f(w_in), f(conv_w)
    w_br_conv, w_br_attn, w_o, sinks = f(w_br_conv), f(w_br_attn), f(w_o), f(sinks)
    g_post1, g_pre2, w_ff1, w_ff2, g_post2, rel_table = f(g_post1), f(g_pre2), f(w_ff1), f(w_ff2), f(g_post2), f(rel_table)

    def wtile(w):
        L, K, N = w.shape
        return np.ascontiguousarray(w.reshape(L, K // 128, 128, N // WCOLS, WCOLS).transpose(0, 3, 2, 1, 4)).reshape(
            L, N // WCOLS, 128, (K // 128) * WCOLS)

    w_in_p = wtile(w_in[:, :, _win_perm()])
    w_bra_p = wtile(w_br_attn[:, _attn_row_perm(), :])
    w_ada_t, w_brc_t, w_o_t, w_ff1_t = wtile(w_ada), wtile(w_br_conv), wtile(w_o), wtile(w_ff1)
    w_ff2_t = np.ascontiguousarray(w_ff2.reshape(DEPTH, 2, 16, 128, 8, 128).transpose(0, 4, 1, 3, 2, 5)).reshape(DEPTH, 16, 128, 2048)
    vecs = np.zeros((208, 128), np.float32)
    for l in range(DEPTH):
        b = l * 104
        vecs[b + 0:b + 8] = g_pre1[l].reshape(8, 128)
        vecs[b + 8:b + 16] = g_post1[l].reshape(8, 128)
        vecs[b + 16:b + 24] = g_pre2[l].reshape(8, 128)
        vecs[b + 24:b + 32] = g_post2[l].reshape(8, 128)
        vecs[b + 32:b + 56] = conv_w[l].reshape(24, 128)
        vecs[b + 56:b + 104] = b_ada[l].reshape(48, 128)
    sinkrep = np.zeros((128, 16), np.float32)
    for l in range(DEPTH):
        for c in range(8):
            sinkrep[0:64, l * 8 + c] = sinks[l, c]
            sinkrep[64:128, l * 8 + c] = sinks[l, 8 + c]
    frev = _frev()
    jj = np.arange(128)
    bdmask = (jj[:, None] // 8 == jj[None, :] // 8).astype(np.float32)
    identf = np.eye(128, dtype=np.float32)

    in_maps = []
    for core in range(NCORES):
        b, half = core // 2, core % 2
        xpc = np.zeros((NPB * 128, D), np.float32)
        if half == 0:
            xpc[256:] = x_prompt[b, 0:2048]
        else:
            xpc[:] = x_prompt[b, 2048 - 256:4096]
        ss = slice(core * NSEQ, (core + 1) * NSEQ)
        cinp = np.concatenate([c_prompt[b:b + 1], c_sample[ss]], axis=0)
        in_maps.append({
            "xp": xpc,
            "xs": np.ascontiguousarray(x_sample[ss].reshape(128, D)),
            "cin": np.ascontiguousarray(cinp),
            "vecs": vecs,
            "sinkrep": sinkrep,
            "hmask": np.full((128, 1), float(half), np.float32),
            "relt": rel_table,
            "frev": frev,
            "bdmask": bdmask,
            "identf": identf,
            "w_ada": w_ada_t, "w_in": w_in_p, "w_brc": w_brc_t, "w_bra": w_bra_p, "w_o": w_o_t,
            "w_ff1": w_ff1_t, "w_ff2": w_ff2_t,
            "sconv": np.ascontiguousarray(state_conv[:, ss].reshape(DEPTH, 32, D)),
            "ck": np.ascontiguousarray(cache_k[:, ss].reshape(DEPTH, NSEQ, 128, 128)),
            "cv": np.ascontiguousarray(cache_v[:, ss].reshape(DEPTH, NSEQ, 128, 128)),
        })
    return in_maps


def _assemble(R):

    y_prompt = np.zeros((4, 4096, D), np.float32)
    y_sample = np.zeros((128, 8, D), np.float32)
    conv_prompt = np.zeros((DEPTH, 4, 2, D), np.float32)
    k_prompt = np.zeros((DEPTH, 4, 128, 2, 64), np.float32)
    v_prompt = np.zeros((DEPTH, 4, 128, 2, 64), np.float32)
    conv_sample = np.zeros((DEPTH, 128, 2, D), np.float32)
    k_sample = np.zeros((DEPTH, 128, 128, 2, 64), np.float32)
    v_sample = np.zeros((DEPTH, 128, 128, 2, 64), np.float32)
    for core in range(NCORES):
        b, half = core // 2, core % 2
        r = R[core]
        y_prompt[b, half * 2048:(half + 1) * 2048] = r["yp"]
        ss = slice(core * NSEQ, (core + 1) * NSEQ)
        y_sample[ss] = r["ys"].reshape(NSEQ, 8, D)
        conv_sample[:, ss] = r["convs"].reshape(DEPTH, NSEQ, 2, D)
        k_sample[:, ss, 0:120] = r["ks"].reshape(DEPTH, NSEQ, 120, 2, 64)
        v_sample[:, ss, 0:120] = r["vs"].reshape(DEPTH, NSEQ, 120, 2, 64)
        k_sample[:, ss, 120:128] = r["ksn"].reshape(DEPTH, NSEQ, 8, 2, 64)
        v_sample[:, ss, 120:128] = r["vsn"].reshape(DEPTH, NSEQ, 8, 2, 64)
        if half == 1:
            conv_prompt[:, b] = r["convp"]
            k_prompt[:, b] = r["kp"].reshape(DEPTH, 128, 2, 64)
            v_prompt[:, b] = r["vp"].reshape(DEPTH, 128, 2, 64)
    return (y_prompt, y_sample, conv_prompt, k_prompt, v_prompt, conv_sample, k_sample, v_sample)


def kernel(**inputs):
    in_maps = _prep(**inputs)
    if "nc" not in _CACHE:
        _CACHE["nc"] = build_program()
    res = run_bass_kernel_spmd(_CACHE["nc"], in_maps, core_ids=list(range(NCORES)))
    return _assemble(res.results)
```
